# Optimizing a Trainium2 kernel written in Bass

```python
import math
import jax
import jax.numpy as jnp
from jax import lax
import numpy as np

D_MODEL = 1024
BATCH = 32
SEQ = 256
DEPTH = 2
DEC_BATCH = 4
DEC_SEQ = 1024
PAST_LEN = 256

GRID_W = 64
N_EVEN = (DEPTH + 1) // 2
N_ODD = DEPTH // 2
HEAD_DIM = 64
H_A = 8
QK_NOPE = 64
QK_ROPE = 32
V_A = 64
Q_LORA = 256
KV_LORA = 128
H_B = 4
DH_B = HEAD_DIM
H_C = 8
KV_C = 2
DH_C = HEAD_DIM
H_D = 8
DH_D = HEAD_DIM
NA_WIN_ROWS = 8
NA_WIN_COLS = 16
MIX_EVEN = H_A * V_A + H_B * 2 * DH_B
MIX_ODD = H_C * DH_C + H_D * DH_D
EVEN_SIZES = (Q_LORA, KV_LORA, QK_ROPE, H_B * 2 * DH_B, H_B * 2 * DH_B, H_B * 2 * DH_B)
ODD_SIZES = (H_C * DH_C, KV_C * DH_C, KV_C * DH_C, H_D * DH_D, H_D * DH_D, H_D * DH_D)
D_FF = 2816
CONV_WIDTH = 3
Q_BLOCK = 128
ROPE_THETA = 10000.0
EPS = 1e-6
NEG_INF = -1e30

kernel_name = "hybrid_mla_diff_gqa_natten_dit_step"


def rmsnorm(x, w):
    xf = x.astype(jnp.float32)
    y = xf * lax.rsqrt(jnp.mean(xf * xf, axis=-1, keepdims=True) + EPS)
    return (y * w.astype(jnp.float32)).astype(x.dtype)


def split_cols(z, sizes):
    return jnp.split(z, np.cumsum(sizes)[:-1].tolist(), axis=-1)


def adaln(cvec, w, b):
    m = jax.nn.silu(cvec) @ w + b
    return jnp.split(m, 6, axis=-1)


def modulate(x, shift, scale):
    return x * (1.0 + scale[:, None, :]) + shift[:, None, :]


def axial_rope_table(n_tokens, rot_dim):
    t = np.arange(n_tokens)
    n_freq = rot_dim // 4
    inv = 1.0 / (ROPE_THETA ** (np.arange(n_freq) / n_freq))
    ang = np.concatenate([(t // GRID_W)[:, None] * inv[None, :],
                          (t % GRID_W)[:, None] * inv[None, :]], axis=-1)
    return jnp.asarray(np.cos(ang), jnp.float32), jnp.asarray(np.sin(ang), jnp.float32)


def apply_rope(x, cos, sin):
    shape = (cos.shape[0],) + (1,) * (x.ndim - 3) + (cos.shape[1],)
    cos = cos.reshape(shape).astype(x.dtype)
    sin = sin.reshape(shape).astype(x.dtype)
    half = x.shape[-1] // 2
    x1, x2 = x[..., :half], x[..., half:]
    return jnp.concatenate([x1 * cos - x2 * sin, x1 * sin + x2 * cos], axis=-1)


def map_query_blocks(fn, q):
    b, s = q.shape[0], q.shape[1]
    nb = s // Q_BLOCK
    qb = jnp.moveaxis(q.reshape((b, nb, Q_BLOCK) + q.shape[2:]), 1, 0)
    out = jnp.moveaxis(lax.map(fn, qb), 0, 1)
    return out.reshape((b, s) + out.shape[3:])


def mla_attend(q, k_nope, k_pe, v):
    scale = (QK_NOPE + QK_ROPE) ** -0.5
    def block(qb):
        s = (jnp.einsum('bqhd,bthd->bhqt', qb[..., :QK_NOPE], k_nope)
             + jnp.einsum('bqhr,btr->bhqt', qb[..., QK_NOPE:], k_pe)).astype(jnp.float32) * scale
        p = jax.nn.softmax(s, axis=-1).astype(v.dtype)
        return jnp.einsum('bhqt,bthd->bqhd', p, v)
    return map_query_blocks(block, q)


def diff_attend(q, k, v, lam):
    scale = q.shape[-1] ** -0.5
    def block(qb):
        s = jnp.einsum('bqhcd,bthcd->bhcqt', qb, k).astype(jnp.float32) * scale
        p = jax.nn.softmax(s, axis=-1)
        a = (p[:, :, 0] - lam * p[:, :, 1]).astype(v.dtype)
        return jnp.einsum('bhqt,bthd->bqhd', a, v)
    return map_query_blocks(block, q)


def gqa_attend(q, k, v):
    scale = q.shape[-1] ** -0.5
    def block(qb):
        s = jnp.einsum('bqkgd,btkd->bkgqt', qb, k).astype(jnp.float32) * scale
        p = jax.nn.softmax(s, axis=-1).astype(v.dtype)
        return jnp.einsum('bkgqt,btkd->bqkgd', p, v)
    return map_query_blocks(block, q)


def neighbourhood_attend(q, k, v, k_ctx, v_ctx, rpb):
    b, s, h, dh = q.shape
    rows = s // GRID_W
    kr = min(NA_WIN_ROWS, rows)
    kc = NA_WIN_COLS
    r = np.arange(rows)
    rs = np.clip(r - kr // 2, 0, rows - kr)
    band = rs[:, None] + np.arange(kr)[None, :]
    col = np.arange(GRID_W)
    cs = np.clip(col - kc // 2, 0, GRID_W - kc)
    col_ok = (col[None, :] >= cs[:, None]) & (col[None, :] < cs[:, None] + kc)
    dr = band - r[:, None] + NA_WIN_ROWS - 1
    dc = np.clip(col[None, :] - col[:, None] + kc - 1, 0, 2 * kc - 2)
    bias = rpb[:, dr[:, None, :, None], dc[None, :, None, :]].astype(jnp.float32)
    scale = dh ** -0.5
    qg = q.reshape(b, rows, GRID_W, h, dh)
    kg = k.reshape(b, rows, GRID_W, h, dh)[:, band]
    vg = v.reshape(b, rows, GRID_W, h, dh)[:, band]
    s_loc = jnp.einsum('brqhd,brkwhd->bhrqkw', qg, kg).astype(jnp.float32) * scale + bias[None]
    s_loc = jnp.where(col_ok[None, None, None, :, None, :], s_loc, NEG_INF)
    s_ctx = jnp.einsum('brqhd,bthd->bhrqt', qg, k_ctx).astype(jnp.float32) * scale
    n_loc = kr * GRID_W
    sc = jnp.concatenate([s_loc.reshape(b, h, rows, GRID_W, n_loc), s_ctx], axis=-1)
    p = jax.nn.softmax(sc, axis=-1).astype(v.dtype)
    p_loc = p[..., :n_loc].reshape(b, h, rows, GRID_W, kr, GRID_W)
    p_ctx = p[..., n_loc:]
    out = (jnp.einsum('bhrqkw,brkwhd->brqhd', p_loc, vg)
           + jnp.einsum('bhrqt,bthd->brqhd', p_ctx, v_ctx))
    return out.reshape(b, s, h, dh)


def even_mixer(u, w_in, w_out, w_uq, q_norm_w, kv_norm_w, w_uk, w_uv, diff_lam, diff_subln_w,
               lam_init, ctx=None, rope=None):
    b, s, _ = u.shape
    cq, ckv, kpe, qd, kd, vd = split_cols(u @ w_in, EVEN_SIZES)
    q = (rmsnorm(cq, q_norm_w) @ w_uq).reshape(b, s, H_A, QK_NOPE + QK_ROPE)
    ckv = rmsnorm(ckv, kv_norm_w)
    qd = qd.reshape(b, s, H_B, 2, DH_B)
    kd = kd.reshape(b, s, H_B, 2 * DH_B)
    vd = vd.reshape(b, s, H_B, 2 * DH_B)
    state = (ckv, kpe, kd, vd)
    if ctx is None:
        ckv_all, kpe_all, kd_all, vd_all = state
    else:
        cos_a, sin_a, cos_h, sin_h = rope
        q = jnp.concatenate([q[..., :QK_NOPE], apply_rope(q[..., QK_NOPE:], cos_a, sin_a)], axis=-1)
        qd = apply_rope(qd, cos_h, sin_h)
        kd_rot = apply_rope(kd.reshape(b, s, H_B, 2, DH_B), cos_h, sin_h).reshape(b, s, H_B, 2 * DH_B)
        ckv_all = jnp.concatenate([ckv, ctx[0]], axis=1)
        kpe_all = jnp.concatenate([apply_rope(kpe, cos_a, sin_a), ctx[1]], axis=1)
        kd_all = jnp.concatenate([kd_rot, ctx[2]], axis=1)
        vd_all = jnp.concatenate([vd, ctx[3]], axis=1)
    t = ckv_all.shape[1]
    k_nope = (ckv_all @ w_uk).reshape(b, t, H_A, QK_NOPE)
    v_a = (ckv_all @ w_uv).reshape(b, t, H_A, V_A)
    o_a = mla_attend(q, k_nope, kpe_all, v_a)
    lf = diff_lam.astype(jnp.float32)
    lam = jnp.exp(jnp.sum(lf[0] * lf[1])) - jnp.exp(jnp.sum(lf[2] * lf[3])) + lam_init
    o_b = diff_attend(qd, kd_all.reshape(b, t, H_B, 2, DH_B), vd_all, lam)
    o_b = rmsnorm(o_b, diff_subln_w) * (1.0 - lam_init)
    o = jnp.concatenate([o_a.reshape(b, s, H_A * V_A), o_b.reshape(b, s, H_B * 2 * DH_B)], axis=-1)
    return o @ w_out, state


def odd_mixer(u, w_in, w_out, qk_norm_w, rpb, ctx=None, rope=None):
    b, s, _ = u.shape
    qc, kc, vc, qn, kn, vn = split_cols(u @ w_in, ODD_SIZES)
    qc = rmsnorm(qc.reshape(b, s, H_C, DH_C), qk_norm_w[0])
    kc = rmsnorm(kc.reshape(b, s, KV_C, DH_C), qk_norm_w[1])
    vc = vc.reshape(b, s, KV_C, DH_C)
    qn = qn.reshape(b, s, H_D, DH_D)
    kn = kn.reshape(b, s, H_D, DH_D)
    vn = vn.reshape(b, s, H_D, DH_D)
    state = (kc, vc, kn, vn)
    if ctx is None:
        o_c = gqa_attend(qc.reshape(b, s, KV_C, H_C // KV_C, DH_C), kc, vc)
        o_d = gqa_attend(qn[:, :, :, None], kn, vn)[:, :, :, 0]
    else:
        cos_h, sin_h = rope
        qc = apply_rope(qc, cos_h, sin_h)
        k_all = jnp.concatenate([apply_rope(kc, cos_h, sin_h), ctx[0]], axis=1)
        v_all = jnp.concatenate([vc, ctx[1]], axis=1)
        o_c = gqa_attend(qc.reshape(b, s, KV_C, H_C // KV_C, DH_C), k_all, v_all)
        o_d = neighbourhood_attend(qn, kn, vn, ctx[2], ctx[3], rpb)
    o = jnp.concatenate([o_c.reshape(b, s, H_C * DH_C), o_d.reshape(b, s, H_D * DH_D)], axis=-1)
    return o @ w_out, state


def conv_ffn(u, w_up, conv_w, conv_b, w_down):
    z = u @ w_up
    zp = jnp.pad(z, ((0, 0), (1, 1), (0, 0)))
    z = zp[:, :-2] * conv_w[0] + zp[:, 1:-1] * conv_w[1] + zp[:, 2:] * conv_w[2] + conv_b
    gate, val = jnp.split(z, 2, axis=-1)
    return (jax.nn.silu(gate) * val) @ w_down


def setup_inputs(seed: int = 0) -> dict:
    key = jax.random.key(seed)
    ks = iter(jax.random.split(key, 40))

    def nrm(shape, scale=1.0):
        return jax.random.normal(next(ks), shape, jnp.float32) * scale

    def gain(shape):
        return 1.0 + nrm(shape, 0.05)

    return {
        "x_prompt": nrm((BATCH, SEQ, D_MODEL)),
        "x_sample": nrm((DEC_BATCH, DEC_SEQ, D_MODEL)),
        "c": nrm((DEC_BATCH, D_MODEL)),
        "cache_mla_ckv": nrm((DEC_BATCH, N_EVEN, PAST_LEN, KV_LORA)),
        "cache_mla_kpe": nrm((DEC_BATCH, N_EVEN, PAST_LEN, QK_ROPE)),
        "cache_diff_k": nrm((DEC_BATCH, N_EVEN, PAST_LEN, H_B, 2 * DH_B)),
        "cache_diff_v": nrm((DEC_BATCH, N_EVEN, PAST_LEN, H_B, 2 * DH_B)),
        "cache_gqa_k": nrm((DEC_BATCH, N_ODD, PAST_LEN, KV_C, DH_C)),
        "cache_gqa_v": nrm((DEC_BATCH, N_ODD, PAST_LEN, KV_C, DH_C)),
        "cache_na_k": nrm((DEC_BATCH, N_ODD, PAST_LEN, H_D, DH_D)),
        "cache_na_v": nrm((DEC_BATCH, N_ODD, PAST_LEN, H_D, DH_D)),
        "c_ctx": nrm((D_MODEL,)),
        "norm_w": gain((DEPTH, 4, D_MODEL)),
        "w_mod": nrm((DEPTH, D_MODEL, 6 * D_MODEL), 0.5 * D_MODEL ** -0.5),
        "b_mod": nrm((DEPTH, 6 * D_MODEL), 0.01),
        "w_in_even": nrm((N_EVEN, D_MODEL, sum(EVEN_SIZES)), D_MODEL ** -0.5),
        "w_out_even": nrm((N_EVEN, MIX_EVEN, D_MODEL), MIX_EVEN ** -0.5),
        "w_uq": nrm((N_EVEN, Q_LORA, H_A * (QK_NOPE + QK_ROPE)), Q_LORA ** -0.5),
        "q_norm_w": gain((N_EVEN, Q_LORA)),
        "kv_norm_w": gain((N_EVEN, KV_LORA)),
        "w_uk": nrm((N_EVEN, KV_LORA, H_A * QK_NOPE), KV_LORA ** -0.5),
        "w_uv": nrm((N_EVEN, KV_LORA, H_A * V_A), KV_LORA ** -0.5),
        "diff_lam": nrm((N_EVEN, 4, DH_B), 0.1),
        "diff_subln_w": gain((N_EVEN, 2 * DH_B)),
        "w_in_odd": nrm((N_ODD, D_MODEL, sum(ODD_SIZES)), D_MODEL ** -0.5),
        "w_out_odd": nrm((N_ODD, MIX_ODD, D_MODEL), MIX_ODD ** -0.5),
        "qk_norm_w": gain((N_ODD, 2, DH_C)),
        "na_rpb": nrm((N_ODD, H_D, 2 * NA_WIN_ROWS - 1, 2 * NA_WIN_COLS - 1), 0.1),
        "w_up": nrm((DEPTH, D_MODEL, 2 * D_FF), D_MODEL ** -0.5),
        "conv_w": nrm((DEPTH, CONV_WIDTH, 2 * D_FF), CONV_WIDTH ** -0.5),
        "conv_b": nrm((DEPTH, 2 * D_FF), 0.01),
        "w_down": nrm((DEPTH, D_FF, D_MODEL), D_FF ** -0.5),
    }


def reference(x_prompt, x_sample, c, cache_mla_ckv, cache_mla_kpe, cache_diff_k, cache_diff_v,
              cache_gqa_k, cache_gqa_v, cache_na_k, cache_na_v, c_ctx, norm_w, w_mod, b_mod,
              w_in_even, w_out_even, w_uq, q_norm_w, kv_norm_w, w_uk, w_uv, diff_lam, diff_subln_w,
              w_in_odd, w_out_odd, qk_norm_w, na_rpb, w_up, conv_w, conv_b, w_down):
    n_lat = x_sample.shape[1]
    cos_a, sin_a = axial_rope_table(n_lat, QK_ROPE)
    cos_h, sin_h = axial_rope_table(n_lat, HEAD_DIM)
    hp, hs = x_prompt, x_sample
    even_states, odd_states = [], []
    for l in range(DEPTH):
        mp = adaln(c_ctx[None, :], w_mod[l], b_mod[l])
        ms = adaln(c, w_mod[l], b_mod[l])
        up = modulate(rmsnorm(hp, norm_w[l, 0]), mp[0], mp[1])
        us = modulate(rmsnorm(hs, norm_w[l, 0]), ms[0], ms[1])
        i = l // 2
        if l % 2 == 0:
            lam_init = 0.8 - 0.6 * math.exp(-0.3 * l)
            args = (w_in_even[i], w_out_even[i], w_uq[i], q_norm_w[i], kv_norm_w[i], w_uk[i],
                    w_uv[i], diff_lam[i], diff_subln_w[i], lam_init)
            yp, st = even_mixer(up, *args)
            ys, _ = even_mixer(us, *args,
                               ctx=(cache_mla_ckv[:, i], cache_mla_kpe[:, i], cache_diff_k[:, i], cache_diff_v[:, i]),
                               rope=(cos_a, sin_a, cos_h, sin_h))
            even_states.append(st)
        else:
            args = (w_in_odd[i], w_out_odd[i], qk_norm_w[i], na_rpb[i])
            yp, st = odd_mixer(up, *args)
            ys, _ = odd_mixer(us, *args,
                              ctx=(cache_gqa_k[:, i], cache_gqa_v[:, i], cache_na_k[:, i], cache_na_v[:, i]),
                              rope=(cos_h, sin_h))
            odd_states.append(st)
        hp = hp + mp[2][:, None, :] * rmsnorm(yp, norm_w[l, 1])
        hs = hs + ms[2][:, None, :] * rmsnorm(ys, norm_w[l, 1])
        up = modulate(rmsnorm(hp, norm_w[l, 2]), mp[3], mp[4])
        us = modulate(rmsnorm(hs, norm_w[l, 2]), ms[3], ms[4])
        hp = hp + mp[5][:, None, :] * rmsnorm(conv_ffn(up, w_up[l], conv_w[l], conv_b[l], w_down[l]), norm_w[l, 3])
        hs = hs + ms[5][:, None, :] * rmsnorm(conv_ffn(us, w_up[l], conv_w[l], conv_b[l], w_down[l]), norm_w[l, 3])
    new_mla_ckv = jnp.stack([st[0] for st in even_states], axis=1)
    new_mla_kpe = jnp.stack([st[1] for st in even_states], axis=1)
    new_diff_k = jnp.stack([st[2] for st in even_states], axis=1)
    new_diff_v = jnp.stack([st[3] for st in even_states], axis=1)
    new_gqa_k = jnp.stack([st[0] for st in odd_states], axis=1)
    new_gqa_v = jnp.stack([st[1] for st in odd_states], axis=1)
    new_na_k = jnp.stack([st[2] for st in odd_states], axis=1)
    new_na_v = jnp.stack([st[3] for st in odd_states], axis=1)
    return (hp, hs, new_mla_ckv, new_mla_kpe, new_diff_k, new_diff_v, new_gqa_k, new_gqa_v, new_na_k, new_na_v)
```

```python
import math
import numpy as np
from contextlib import ExitStack
import concourse.bass as bass
import concourse.mybir as mybir
from concourse.bass_utils import run_bass_kernel_spmd

F32 = mybir.dt.float32
BF16 = mybir.dt.bfloat16
AF = mybir.ActivationFunctionType
ALU = mybir.AluOpType

EPOCH = 6000
ENGS = ("pe", "act", "dve", "pool", "sp")
EPS = 1e-6
BIG = 30000.0
NCORES = 8


class Prog:
    def __init__(self, nc):
        self.nc = nc
        self.ops = {e: [] for e in ENGS}
        self.cnt = {e: 0 for e in ENGS}
        self.esems = {e: [] for e in ENGS}
        self.waited = {}
        self.lastw = {}
        self.readers = {}
        self.dsems = {}
        self.allkeys = set()
        self.phase = ''
        self.labels = {e: [] for e in ENGS}

    def _esem(self, e, ep):
        while len(self.esems[e]) <= ep:
            self.esems[e].append(self.nc.alloc_semaphore(name=f"s_{e}_{len(self.esems[e])}"))
        return self.esems[e][ep]

    def _event_of(self, e, idx):
        return (self._esem(e, idx // EPOCH), idx % EPOCH + 1, e, idx)

    def _need(self, eng, ev, waits):
        if ev is None:
            return
        sem, val, src, idx = ev
        if src == "pe" and eng == "pe":
            return
        key = (eng, id(sem))
        if self.waited.get(key, 0) >= val:
            return
        self.waited[key] = val
        waits.append((sem, val))

    def _expand(self, k):
        if k.endswith("*"):
            pre = k[:-1] + "."
            return [k] + [x for x in self.allkeys if x.startswith(pre)]
        if "." in k:
            self.allkeys.add(k)
            return [k, k.split(".")[0] + "*"]
        return [k]

    def op(self, eng, fn, reads=(), writes=(), dsem=None):
        waits = []
        rk = [x for k in reads for x in self._expand(k)]
        wk = [x for k in writes for x in self._expand(k)]
        for k in rk:
            self._need(eng, self.lastw.get(k), waits)
            if k.startswith("ps"):
                for ev in self.readers.get(k, ()):
                    if ev[2] != eng:
                        self._need(eng, ev, waits)
        for k in wk:
            self._need(eng, self.lastw.get(k), waits)
            for ev in self.readers.get(k, ()):
                self._need(eng, ev, waits)
        if dsem is not None:
            if dsem not in self.dsems:
                self.dsems[dsem] = [self.nc.alloc_semaphore(name=f"d_{dsem}"), 0]
            d = self.dsems[dsem]
            d[1] += 16
            ev = (d[0], d[1], "dma", None)
            inc = (d[0], 16)
        else:
            idx = self.cnt[eng]
            self.cnt[eng] += 1
            ev = self._event_of(eng, idx)
            inc = (ev[0], 1)
        for k in writes:
            if k.endswith("*"):
                for x in self._expand(k):
                    self.lastw[x] = ev
                    self.readers[x] = []
            else:
                self.lastw[k] = ev
                self.readers[k] = []
        for k in reads:
            if k.endswith("*"):
                for x in self._expand(k):
                    self.readers.setdefault(x, []).append(ev)
            else:
                self.readers.setdefault(k, []).append(ev)
        self.ops[eng].append((fn, waits, inc))
        self.labels[eng].append(self.phase)
        return ev

    def final_waits(self, eng="sp"):
        waits = []
        for name, (sem, val) in self.dsems.items():
            if val > 0:
                waits.append((sem, val))
        for e in ENGS:
            if e != eng and self.cnt[e] > 0:
                ev = self._event_of(e, self.cnt[e] - 1)
                waits.append((ev[0], ev[1]))
        self.ops[eng].append((None, waits, None))

    def emit(self, block):
        hmap = {"pe": "tensor", "act": "scalar", "dve": "vector", "pool": "gpsimd", "sp": "sync"}

        def mk(e):
            def body(h):
                for fn, waits, inc in self.ops[e]:
                    for sem, val in waits:
                        h.wait_ge(sem, val)
                    if fn is not None:
                        fn(h).then_inc(inc[0], inc[1])
            return body

        for e in ENGS:
            if self.ops[e]:
                getattr(block, hmap[e])(mk(e))


def sb_ap(t, p0, npart, off, dims):
    fsz = 1
    for s in t.shape[1:]:
        fsz *= s
    return bass.AP(t, p0 * fsz + off, [[fsz, npart]] + [list(d) for d in dims])


def build_program():
    nc = bass.Bass("TRN2", target_bir_lowering=False)
    di = lambda n, s: nc.dram_tensor(n, list(s), F32, kind="ExternalInput").ap()
    do = lambda n, s: nc.dram_tensor(n, list(s), F32, kind="ExternalOutput").ap()
    xin = {"P": di("xp", [1024, 1024]), "S": di("xs", [1024, 1024])}
    yout = {"P": do("yp", [1024, 1024]), "S": do("ys", [1024, 1024])}
    st_out = [do("st0", [1024, 1184]), do("st1", [1024, 1280])]
    d_ident = di("ident", [128, 128])
    d_cvT = di("cvT", [128, 16])
    d_vecs = di("vecs", [128, 528])
    d_ropeh = di("ropeh", [128, 2, 1024])
    d_ropea = di("ropea", [128, 2, 1024])
    d_aug = di("aug", [32, 1024])
    d_colok = di("colok", [128, 64])
    d_lam = di("lamb", [128, 256])
    d_kvwb = di("kvwb", [128, 128])
    c_ckv = di("c_ckv", [256, 128]); c_kpe = di("c_kpe", [256, 32])
    c_dk = di("c_dk", [256, 512]); c_dv = di("c_dv", [256, 512])
    c_gk = di("c_gk", [256, 128]); c_gv = di("c_gv", [256, 128])
    c_nk = di("c_nk", [256, 512]); c_nv = di("c_nv", [256, 512])
    w_mod = di("w_mod", [2, 1024, 6144])
    w_in = [di("w_in_even", [1024, 1952]), di("w_in_odd", [1024, 2304])]
    w_outw = [di("w_out_even", [1024, 1024]), di("w_out_odd", [1024, 1024])]
    w_uq = di("w_uq", [256, 768]); w_uk = di("w_uk", [128, 512]); w_uv = di("w_uv", [128, 512])
    w_up = di("w_up", [2, 1024, 5632]); w_down = di("w_down", [2, 2816, 1024])
    tpad = di("rpbp", [120, 160])
    d_qk1b = di("qk1b", [128, 64])

    VO = {}
    o = 0
    for name, n in [("nw", 64), ("bmod", 96), ("cw", 264), ("cb", 88), ("qnw", 2), ("kvw", 1), ("sub", 1),
                    ("qkw", 2), ("qkws", 2)]:
        VO[name] = o
        o += n
    assert o <= 528

    with ExitStack() as es:
        sb = lambda n, s, d: es.enter_context(nc.sbuf_tensor("sb_" + n, list(s), d))
        P = Prog(nc)
        ps = [es.enter_context(nc.psum_tensor(f"ps{i}", [128, 512], F32)) for i in range(8)]
        rr = {"s": 0, "a": 0}

        rr["ns"] = 4

        def sbank():
            i = rr["s"] % rr["ns"]
            rr["s"] += 1
            return ps[i], f"ps{i}"

        def abank():
            na = 8 - rr["ns"]
            i = rr["ns"] + rr["a"] % na
            rr["a"] += 1
            return ps[i], f"ps{i}"

        X = sb("X", [128, 8, 1024], F32)
        U = sb("U", [128, 8, 1024], BF16)
        Y = sb("Y*", [128, 8, 512], F32)
        SQ = sb("SQ*", [128, 8, 512], BF16)
        OT = sb("OT*", [128, 8, 1024], BF16)
        NW = 4
        WB = [sb(f"WB{i}", [128, 4096], BF16) for i in range(NW)]
        WSW = sb("WSW", [128, 4096], BF16)
        ATT = sb("ATT", [128, 14336], BF16)
        ident = sb("ident", [128, 128], F32)
        ones = sb("ones", [128, 128], BF16)
        bd = sb("bd", [128, 128], BF16)
        vecs = sb("vecs", [128, 528], F32)
        cvT = sb("cvT", [128, 16], F32)
        csT = sb("csT", [128, 16], BF16)
        modT = sb("modT", [128, 2, 96], F32)
        MT = sb("MT", [128, 2, 6, 16], F32)
        RS = sb("RS", [128, 512], F32)
        RD = sb("RD", [128, 2, 512], F32)
        R1 = sb("R1", [128, 512], F32)
        R2 = sb("R2", [128, 512], F32)
        R3 = sb("R3", [128, 512], F32)
        R4 = sb("R4", [128, 512], F32)
        PTALL = sb("PTALL", [128, 7 * 512], BF16)
        PT = [PTALL[:, i * 512:(i + 1) * 512] for i in range(6)]
        ZB = {(nm, par): PTALL[:, (ni * 2 + par) * 516:(ni * 2 + par + 1) * 516] for ni, nm in enumerate("gv") for par in range(2)}
        DG = {(nm, tap, par): PTALL[:, 2064 + ((ni * 2 + ti) * 2 + par) * 128:2064 + ((ni * 2 + ti) * 2 + par + 1) * 128]
              for ni, nm in enumerate("gv") for ti, tap in enumerate((0, 2)) for par in range(2)}
        identb = sb("identb", [128, 128], BF16)
        PTS = sb("PTS", [128, 512], BF16)
        ropeh = sb("ropeh", [128, 2, 1024], BF16)
        ropea = sb("ropea", [128, 2, 1024], BF16)
        colok = sb("colok", [128, 64], F32)
        CM2 = Y[:].bitcast(BF16).rearrange("p a b -> p (a b)")[:, 0:6656].rearrange("p (h x c) -> p h x c", x=26, c=64)
        qk1b = sb("qk1b", [128, 64], F32)
        lamt = sb("lamt", [128, 256], F32)
        lam = sb("lam", [128, 4], F32)
        kvwb = sb("kvwb", [128, 128], F32)
        epsD = sb("epsD", [128, 1], F32)
        CKVN = sb("CKVN*", [128, 1280], BF16)
        KPE = sb("KPE*", [128, 1280], BF16)
        WK96 = sb("WK96", [128, 2, 8, 96], BF16)
        STG = sb("STG", [128, 2, 512], F32)
        SMALL = sb("SMALL", [128, 8], F32)

        def A(eng, fn, r=(), w=(), dsem=None):
            return P.op(eng, fn, reads=r, writes=w, dsem=dsem)

        def MM(out, lhsT, rhs, st, sp_, r, w):
            A("pe", lambda h: h.matmul(out, lhsT=lhsT, rhs=rhs, start=st, stop=sp_), r, w)

        def ACT(out, in_, func, r, w, scale=None, bias=None):
            kw = {}
            if scale is not None:
                kw["scale"] = scale
            if bias is not None:
                kw["bias"] = bias
            A("act", lambda h: h.activation(out=out, in_=in_, func=func, **kw), r, w)

        def TT(out, in0, in1, op, r, w, eng="dve"):
            A(eng, lambda h: h.tensor_tensor(out=out, in0=in0, in1=in1, op=op), r, w)

        def STT(out, in0, scalar, in1, op0, op1, r, w):
            A("dve", lambda h: h.scalar_tensor_tensor(out=out, in0=in0, scalar=scalar, in1=in1, op0=op0, op1=op1), r, w)

        def TS(out, in0, s1, s2, op0, op1, r, w):
            A("dve", lambda h: h.tensor_scalar(out=out, in0=in0, scalar1=s1, scalar2=s2, op0=op0, op1=op1), r, w)

        def CP(eng, out, in_, r, w):
            if eng == "act":
                ACT(out, in_, AF.Copy, r, w)
            else:
                A(eng, lambda h: h.tensor_copy(out=out, in_=in_), r, w)

        def MS(eng, ap, val, w):
            A(eng, lambda h: h.memset(ap, val), (), w)

        uniq = [0]

        def DMA(eng, out, in_, r, w, dsem):
            if dsem in ("c0", "c1"):
                uniq[0] += 1
                dsem = f"c{uniq[0] + 10}"
            A(eng, lambda h: h.dma_start(out=out, in_=in_), r, w, dsem=dsem)

        wctr = [0]

        def wload(src, kc, ncols):
            i = wctr[0] % NW
            wctr[0] += 1
            key = f"WB{i}"
            dst = sb_ap(WB[i], 0, 128, 0, [[ncols, kc], [1, ncols]])
            DMA("pool", dst, src, (), [key], dsem=key)
            t = WB[i]
            return (lambda k, c0, c1: t[:, k * ncols + c0: k * ncols + c1]), key

        DMA("sp", ident[:], d_ident, (), ["ident"], "c0")
        DMA("sp", cvT[:], d_cvT, (), ["cvT"], "c0")
        DMA("sp", vecs[:], d_vecs, (), ["vecs"], "c0")
        DMA("sp", colok[:], d_colok, (), ["colok"], "c0")
        DMA("sp", lamt[:], d_lam, (), ["lamt"], "c0")
        DMA("sp", kvwb[:], d_kvwb, (), ["kvwb"], "c0")
        DMA("pool", ropeh[:], d_ropeh, (), ["ropeh"], "c1")
        DMA("pool", KPE[0:16, 0:1024], d_aug[16:32, :], (), ["AUG"], "c1")
        DMA("pool", KPE[32:48, 0:1024], d_aug[0:16, :], (), ["AUG"], "c1")
        DMA("pool", ropea[:], d_ropea, (), ["ropea"], "c1")
        MS("dve", ones[:], 1.0, ["ones"])
        MS("dve", bd[:], 0.0, ["bd"])
        MS("dve", bd[0:64, 0:64], 1.0, ["bd"])
        MS("dve", bd[64:128, 64:128], 1.0, ["bd"])
        MS("dve", epsD[:], EPS, ["epsD"])
        CP("dve", identb[:], ident[:], ["ident"], ["identb"])
        MS("dve", WK96[:], 0.0, ["WK96"])
        DMA("sp", qk1b[:], d_qk1b, (), ["qk1b"], "c0")
        lam_init = 0.8 - 0.6 * math.exp(-0.3 * 0)
        TT(lamt[:, 0:64], lamt[:, 0:64], lamt[:, 64:128], ALU.mult, ["lamt"], ["lamt"])
        TT(lamt[:, 128:192], lamt[:, 128:192], lamt[:, 192:256], ALU.mult, ["lamt"], ["lamt"])
        A("dve", lambda h: h.reduce_sum(out=lam[:, 0:1], in_=lamt[:, 0:64], axis=mybir.AxisListType.X), ["lamt"], ["lam"])
        A("dve", lambda h: h.reduce_sum(out=lam[:, 1:2], in_=lamt[:, 128:192], axis=mybir.AxisListType.X), ["lamt"], ["lam"])
        ACT(lam[:, 0:2], lam[:, 0:2], AF.Exp, ["lam"], ["lam"])
        TT(lam[:, 2:3], lam[:, 1:2], lam[:, 0:1], ALU.subtract, ["lam"], ["lam"])
        TS(lam[:, 3:4], lam[:, 2:3], -lam_init, None, ALU.add, ALU.bypass, ["lam"], ["lam"])
        TS(SMALL[:, 0:1], vecs[:, VO["sub"]:VO["sub"] + 1], 1.0 - lam_init, None, ALU.mult, ALU.bypass, ["vecs"], ["SMALL"])

        import os
        ACT(csT[:], cvT[:], AF.Silu, ["cvT"], ["csT"])
        MT_SRC = {0: (8, 0, "g"), 1: (0, None, "c"), 2: (16, 1, "m"), 3: (32, 2, "g"), 4: (24, None, "c"), 5: (40, 3, "m")}

        def mod_part(l, i):
            c0, nwi, kind = MT_SRC[i]
            pm, pmk = sbank()
            for pc in range(2):
                src = w_mod[l].rearrange("(kc p) n -> p kc n", p=128)[:, :, c0 * 128 + pc * 512:c0 * 128 + (pc + 1) * 512]
                W, wk = wload(src, 8, 512)
                for jj in range(4):
                    j = pc * 4 + jj
                    for k in range(8):
                        MM(pm[:, 2 * j:2 * j + 2], W(k, jj * 128, jj * 128 + 128), csT[:, 2 * k:2 * k + 2],
                           k == 0, k == 7, [wk, "csT"], [pmk])
            bmb = sb_ap(vecs, 0, 128, VO["bmod"] + 48 * l + c0, [[1, 8], [0, 2]])
            mk_ = f"modT{l}_{i}"
            mod_v = modT[:, l, c0 * 2:(c0 + 8) * 2].rearrange("p (a b) -> p a b", b=2)
            TT(mod_v, pm[:, 0:16].rearrange("p (a b) -> p a b", b=2), bmb, ALU.add, [pmk, "vecs"], [mk_])
            mt_v = MT[:, l, i, :].rearrange("p (a b) -> p a b", b=2)
            tk_ = f"MT{l}_{i}"
            if kind == "c":
                CP("dve", mt_v, mod_v, [mk_], [tk_])
            else:
                nwb = sb_ap(vecs, 0, 128, VO["nw"] + (l * 4 + nwi) * 8, [[1, 8], [0, 2]])
                if kind == "g":
                    STT(mt_v, mod_v, 1.0, nwb, ALU.add, ALU.mult, [mk_, "vecs"], [tk_])
                else:
                    TT(mt_v, mod_v, nwb, ALU.mult, [mk_, "vecs"], [tk_])

        def mtc(l, i, k, col):
            return MT[:, l, i, 2 * k + col:2 * k + col + 1]

        def rstd_from(ssb, ssk, inv_n, out, outk, nrow=128, n=512):
            ACT(out[0:nrow, 0:n], ssb[0:nrow, 0:n], AF.Ln, [ssk, "epsD"], [outk], scale=inv_n, bias=epsD[0:nrow, :])
            ACT(out[0:nrow, 0:n], out[0:nrow, 0:n], AF.Exp, [outk], [outk], scale=-0.5)

        def norm_mod(l, col, gi, si, t):
            xs = X[:, :, t * 512:(t + 1) * 512]
            ACT(SQ[:, 0:4, :], X[:, 0:4, t * 512:(t + 1) * 512], AF.Square, [f"X{t}"], ["SQ.n0"])
            TT(SQ[:, 4:8, :], X[:, 4:8, t * 512:(t + 1) * 512], X[:, 4:8, t * 512:(t + 1) * 512], ALU.mult, [f"X{t}"], ["SQ.n1"])
            sb_, sk = sbank()
            for k in range(8):
                MM(sb_[:, :], ones[:], SQ[:, k, :], k == 0, k == 7, ["ones", "SQ*"], [sk])
            rstd_from(sb_, sk, 1.0 / 1024, RS, "RS")
            rb = [(R1, "R1"), (R2, "R2"), (R3, "R3"), (R4, "R4")]
            for k in range(8):
                tb_, tk_ = rb[k % 4]
                STT(tb_[:, :], X[:, k, t * 512:(t + 1) * 512], mtc(l, gi, k, col), RS[:, :], ALU.mult, ALU.mult,
                    [f"X{t}", "RS", f"MT{l}_{gi}"], [tk_])
                ACT(U[:, k, t * 512:(t + 1) * 512], tb_[:, :], AF.Identity, [tk_, f"MT{l}_{si}"], [f"U{t}"], bias=mtc(l, si, k, col))

        def post_res(l, col, gwi, t, yk="Y*"):
            sb_, sk = sbank()
            for k in range(8):
                MM(sb_[:, :], ones[:], SQ[:, k, :], k == 0, k == 7, ["ones", "SQ*"], [sk])
            rstd_from(sb_, sk, 1.0 / 1024, RS, "RS")
            TT(Y[:], Y[:], sb_ap(RS, 0, 128, 0, [[0, 8], [1, 512]]), ALU.mult, ["Y*", "RS"], ["Y*"])
            for k in range(8):
                xs = X[:, k, t * 512:(t + 1) * 512]
                STT(xs, Y[:, k, :], mtc(l, gwi, k, col), xs, ALU.mult, ALU.add, ["Y*", f"MT{l}_{gwi}", f"X{t}"], [f"X{t}"])

        def proj_fm(W, wk, cols, t, evac, ukey=None, src=None, nk=8):
            for i, (c0, c1) in enumerate(cols):
                b_, bk = sbank()
                for k in range(nk):
                    rhs = U[:, k, t * 512:(t + 1) * 512] if src is None else src(k)
                    MM(b_[0:c1 - c0, :], W(k, c0, c1), rhs, k == 0, k == nk - 1, [wk, ukey or f"U{t}"], [bk])
                evac(i, b_, bk)

        def head_rms(b_, bk, lhs, inv_n, n=512):
            ACT(PTS[:, 0:n], b_[:, 0:n], AF.Square, [bk], ["PTS"])
            s2, s2k = sbank()
            MM(s2[:, 0:n], lhs, PTS[:, 0:n], True, True, ["ones", "bd", "PTS"], [s2k])
            rstd_from(s2, s2k, inv_n, RS, "RS", n=n)

        def rope_comb(out, okey, b_, bk, bs_, bsk, tab, tcols, p0, p1, w=None, ws=None, n=512):
            c = tab[p0:p1, 0, tcols[0]:tcols[1]]
            s = tab[p0:p1, 1, tcols[0]:tcols[1]]
            if w is None:
                TT(R1[p0:p1, 0:n], b_[p0:p1, 0:n], c, ALU.mult, [bk, "ropeh", "ropea"], ["R1"])
                TT(R2[p0:p1, 0:n], bs_[p0:p1, 0:n], s, ALU.mult, [bsk, "ropeh", "ropea"], ["R2"])
            else:
                STT(R1[p0:p1, 0:n], b_[p0:p1, 0:n], w, c, ALU.mult, ALU.mult, [bk, "ropeh", "vecs"], ["R1"])
                STT(R2[p0:p1, 0:n], bs_[p0:p1, 0:n], ws, s, ALU.mult, ALU.mult, [bsk, "ropeh", "vecs"], ["R2"])
            return R1, R2

        deferred = []
        gcount = [0]
        pending_side = []

        def attend(nh, qf, kf, vf, dv, scale, qsegs, ep, cmf=None, tag="", fused=True, side=None, side_every=8):
            base_phase = P.phase.split("/")[0]
            P.phase = base_phase + "/att" + tag
            items = []
            for h in range(nh):
                for (q0, q1, kbs) in qsegs:
                    for i, kb in enumerate(kbs):
                        items.append((h, q0, q1, kb, i == 0, i == len(kbs) - 1))
            LA = 4
            acc = {}
            pts = {}

            def emit_s(j):
                h, q0, q1, kb, first, last = items[j]
                n = q1 - q0
                s_, sk = sbank()
                qa, qk_ = qf(h, q0, q1)
                ka, kk_ = kf(h, kb)
                MM(s_[:, 0:n], ka, qa, True, True, [qk_, kk_], [sk])
                pt = PT[j % 6]
                ptk = f"PT{j % 6}"
                ACT(pt[:, 0:n], s_[:, 0:n], AF.Exp, [sk], [ptk], scale=scale)
                if cmf is not None:
                    cm = cmf(h, q0, kb)
                    if cm is not None:
                        TT(pt[:, 0:n], pt[:, 0:n], cm, ALU.mult, [ptk, "Y*", "SQ*", "WSW"], [ptk])
                pts[j] = (pt, ptk)

            def emit_pv(j):
                h, q0, q1, kb, first, last = items[j]
                n = q1 - q0
                if first:
                    acc[(h, q0)] = (abank(), (None, None) if fused else abank())
                (num, numk), (den, denk) = acc[(h, q0)]
                pt, ptk = pts.pop(j)
                va, vk_ = vf(h, kb)
                MM(num[:, 0:n], va, pt[:, 0:n], first, last, [vk_, ptk], [numk])
                if not fused:
                    MM(den[:, 0:n], ones[:], pt[:, 0:n], first, last, ["ones", ptk], [denk])
                if last:
                    ep(h, q0, q1, num, numk, den, denk)
                    del acc[(h, q0)]

            GRP = 2
            LA = 4
            rr["ns"] = 4
            gi_ = 0
            deferred.clear()
            for j0 in range(0, len(items) + LA, GRP):
                gi_ += 1
                gcount[0] = gi_
                while deferred and deferred[0][0] <= gi_:
                    deferred.pop(0)[1]()
                if side and gi_ % side_every == 0:
                    ph_ = P.phase
                    P.phase = base_phase + "/side"
                    side.pop(0)()
                    P.phase = ph_
                for j in range(j0, j0 + GRP):
                    if 0 <= j - LA < len(items):
                        emit_pv(j - LA)
                for j in range(j0, j0 + GRP):
                    if j < len(items):
                        emit_s(j)
            while deferred:
                deferred.pop(0)[1]()
            rr["ns"] = 4
            P.phase = base_phase

        def ep_std(h, q0, q1, num, numk, den, denk, chunk0=0):
            n = q1 - q0
            pb = 64 * (h % 2)
            sl = h % 2
            ACT(RD[0:64, sl, 0:n], num[64:128, 0:n], AF.Ln, [numk], [f"RD{sl}"])
            ACT(RD[0:64, sl, 0:n], RD[0:64, sl, 0:n], AF.Exp, [f"RD{sl}"], [f"RD{sl}"], scale=-1.0)
            TT(OT[pb:pb + 64, chunk0 + h // 2, q0:q1], num[0:64, 0:n], RD[0:64, sl, 0:n], ALU.mult,
               [numk, f"RD{sl}"], [f"OT.{chunk0 + h // 2}.{pb}.{q0}"])

        def load_x(g):
            xd = xin[g]
            for t in range(2):
                stg = Y[:].rearrange("p a b -> p (a b)")
                DMA("sp", Y[:].rearrange("p a b -> p (a b)").rearrange("p (k f) -> p k f", f=1024),
                    xd[t * 512:(t + 1) * 512, :].rearrange("(k p) f -> p k f", p=128), (), ["Y*"], "xl")
                for k in range(8):
                    b_, bk = sbank()
                    for blk in range(4):
                        A("pe", lambda h, b_=b_, blk=blk, k=k: h.transpose(
                            b_[:, blk * 128:(blk + 1) * 128], stg[:, blk * 1024 + k * 128: blk * 1024 + (k + 1) * 128],
                            ident[:]), ["Y*", "ident"], [bk])
                    CP("act" if k % 2 else "dve", X[:, k, t * 512:(t + 1) * 512], b_[:, :], [bk], [f"X{t}"])

        def store_y(g):
            yd = yout[g]
            stg = Y[:].rearrange("p a b -> p (a b)")
            for t in range(2):
                for blk in range(4):
                    for half in range(2):
                        b_, bk = sbank()
                        for kk in range(4):
                            k = half * 4 + kk
                            A("pe", lambda h, b_=b_, kk=kk, k=k, blk=blk, t=t: h.transpose(
                                b_[:, kk * 128:(kk + 1) * 128], X[:, k, t * 512 + blk * 128: t * 512 + (blk + 1) * 128],
                                ident[:]), [f"X{t}", "ident"], [bk])
                        CP("act" if half else "dve", stg[:, blk * 1024 + half * 512: blk * 1024 + (half + 1) * 512],
                           b_[:, :], [bk], ["Y*"])
                DMA("sp", yd[t * 512:(t + 1) * 512, :].rearrange("(k p) f -> p k f", p=128),
                    Y[:].rearrange("p a b -> p (a b)").rearrange("p (k f) -> p k f", f=1024), ["Y*"], (), "yo")

        def wout_phase(l, col, t):
            wo = w_outw[l].rearrange("(kc p) n -> p kc n", p=128)
            for half in range(2):
                W, wk = wload(wo[:, :, half * 512:(half + 1) * 512], 8, 512)
                for mm in range(4):
                    m = half * 4 + mm
                    b_, bk = sbank()
                    for k in range(8):
                        MM(b_[:, :], W(k, mm * 128, mm * 128 + 128), OT[:, k, t * 512:(t + 1) * 512], k == 0, k == 7,
                           [wk, "OT*"], [bk])
                    ACT(SQ[:, m, :], b_[:, :], AF.Square, [bk], [f"SQ.{m}"])
                    CP("dve", Y[:, m, :], b_[:, :], [bk], [f"Y.{m}"])
            post_res(l, col, 2, t)

        def ffn_phase_dve(l, g, col):
            wu = w_up[l].rearrange("(kc p) n -> p kc n", p=128)
            wd = w_down[l].rearrange("(j p) n -> p j n", p=128)
            cw = lambda tap, j: vecs[:, VO["cw"] + (l * 3 + tap) * 44 + j: VO["cw"] + (l * 3 + tap) * 44 + j + 1]
            cb = lambda j: vecs[:, VO["cb"] + l * 44 + j: VO["cb"] + l * 44 + j + 1]
            segs = [(0, 256), (256, 512)] if g == "P" else [(0, 512)]
            H = ATT[:, 0:11264].rearrange("p (j n) -> p j n", n=512)
            for t in range(2):
                for pc in range(11):
                    Wg, wgk = wload(wu[:, :, pc * 256:(pc + 1) * 256], 8, 256)
                    Wv, wvk = wload(wu[:, :, 2816 + pc * 256:2816 + (pc + 1) * 256], 8, 256)
                    for jj in range(2):
                        j = pc * 2 + jj
                        zb = {}
                        for nm, Wx, wxk in (("g", Wg, wgk), ("v", Wv, wvk)):
                            b_, bk = abank()
                            for k in range(8):
                                MM(b_[:, :], Wx(k, jj * 128, jj * 128 + 128), U[:, k, t * 512:(t + 1) * 512], k == 0, k == 7,
                                   [wxk, f"U{t}"], [bk])
                            zb[nm] = (b_, bk)
                        if g == "S":
                            ot = 1 - t
                            hcol = ot * 512 + (0 if ot == 1 else 511)
                            for nm, Wx, wxk in (("g", Wg, wgk), ("v", Wv, wvk)):
                                b_, bk = zb[nm]
                                hb, hbk = sbank()
                                for k in range(8):
                                    MM(hb[:, 0:1], Wx(k, jj * 128, jj * 128 + 128), U[:, k, hcol:hcol + 1], k == 0, k == 7,
                                       [wxk, f"U{ot}"], [hbk])
                                zb[nm + "h"] = (hb, hbk)
                        for ci, nm in enumerate(("g", "v")):
                            b_, bk = zb[nm]
                            jc = j if nm == "g" else 22 + j
                            a_ = (R1 if nm == "g" else R2) if j % 2 == 0 else (R3 if nm == "g" else R4)
                            ak = ("R1" if nm == "g" else "R2") if j % 2 == 0 else ("R3" if nm == "g" else "R4")
                            ACT(a_[:, :], b_[:, :], AF.Identity, [bk, "vecs"], [ak], scale=cw(1, jc), bias=cb(jc))
                            for (a, b) in segs:
                                STT(a_[:, a + 1:b], b_[:, a:b - 1], cw(0, jc), a_[:, a + 1:b], ALU.mult, ALU.add,
                                    [bk, ak, "vecs"], [ak])
                                STT(a_[:, a:b - 1], b_[:, a + 1:b], cw(2, jc), a_[:, a:b - 1], ALU.mult, ALU.add,
                                    [bk, ak, "vecs"], [ak])
                            if g == "S":
                                hb, hbk = zb[nm + "h"]
                                if t == 0:
                                    STT(a_[:, 511:512], hb[:, 0:1], cw(2, jc), a_[:, 511:512], ALU.mult, ALU.add,
                                        [hbk, ak, "vecs"], [ak])
                                else:
                                    STT(a_[:, 0:1], hb[:, 0:1], cw(0, jc), a_[:, 0:1], ALU.mult, ALU.add,
                                        [hbk, ak, "vecs"], [ak])
                        ag, agk, av, avk = (R1, "R1", R2, "R2") if j % 2 == 0 else (R3, "R3", R4, "R4")
                        ACT(ag[:, :], ag[:, :], AF.Silu, [agk], [agk])
                        TT(H[:, j, :], ag[:, :], av[:, :], ALU.mult, [agk, avk], [f"ATT.{j}"])
                for mp in range(4):
                    accs = [abank() for _ in range(2)]
                    for jh in range(2):
                        W, wk = wload(wd[:, jh * 11:(jh + 1) * 11, mp * 256:(mp + 1) * 256], 11, 256)
                        for mm in range(2):
                            b_, bk = accs[mm]
                            for jj in range(11):
                                j = jh * 11 + jj
                                MM(b_[:, :], W(jj, mm * 128, mm * 128 + 128), H[:, j, :], j == 0, j == 21, [wk, "ATT*"], [bk])
                    for mm in range(2):
                        m = mp * 2 + mm
                        b_, bk = accs[mm]
                        ACT(Y[:, m, :], b_[:, :], AF.Copy, [bk], [f"Y.{m}"])
                        TT(SQ[:, m, :], Y[:, m, :], Y[:, m, :], ALU.mult, [f"Y.{m}"], [f"SQ.{m}"])
                post_res(l, col, 5, t)

        def ffn_phase(l, g, col):
            wu = w_up[l].rearrange("(kc p) n -> p kc n", p=128)
            wd = w_down[l].rearrange("(j p) n -> p j n", p=128)
            cw = lambda tap, j: vecs[:, VO["cw"] + (l * 3 + tap) * 44 + j: VO["cw"] + (l * 3 + tap) * 44 + j + 1]
            cb = lambda j: vecs[:, VO["cb"] + l * 44 + j: VO["cb"] + l * 44 + j + 1]
            nseg = 2 if g == "P" else 1
            sw_ = 512 // nseg
            pw_ = sw_ + 2
            H = ATT[:, 0:11264].rearrange("p (j n) -> p j n", n=512)
            zkeys = lambda nm, par: f"ZB{nm}{par}"

            def stage_a(t, j, Wg, wgk, Wv, wvk, jj):
                par = j % 2
                zb = {}
                for nm, Wx, wxk in (("g", Wg, wgk), ("v", Wv, wvk)):
                    b_, bk = abank()
                    for k in range(8):
                        MM(b_[:, :], Wx(k, jj * 128, jj * 128 + 128), U[:, k, t * 512:(t + 1) * 512], k == 0, k == 7,
                           [wxk, f"U{t}"], [bk])
                    zb[nm] = (b_, bk)
                hb, hbk = None, None
                if g == "S":
                    ot = 1 - t
                    hcol = ot * 512 + (0 if ot == 1 else 511)
                    hb, hbk = sbank()
                    for ci, (nm, Wx, wxk) in enumerate((("g", Wg, wgk), ("v", Wv, wvk))):
                        for k in range(8):
                            MM(hb[:, ci:ci + 1], Wx(k, jj * 128, jj * 128 + 128), U[:, k, hcol:hcol + 1], k == 0, k == 7,
                               [wxk, f"U{ot}"], [hbk])
                for ci, nm in enumerate(("g", "v")):
                    b_, bk = zb[nm]
                    jc = j if nm == "g" else 22 + j
                    a_, ak = ((R1, "R1") if nm == "g" else (R2, "R2")) if par == 0 else ((R3, "R3") if nm == "g" else (R4, "R4"))
                    ACT(a_[:, :], b_[:, :], AF.Identity, [bk, "vecs"], [ak], scale=cw(1, jc), bias=cb(jc))
                    z_ = ZB[(nm, par)]
                    zk = zkeys(nm, par)
                    zin = z_[:, 0:nseg * pw_].rearrange("p (s w) -> p s w", w=pw_)[:, :, 1:1 + sw_]
                    ACT(zin, b_[:, :].rearrange("p (s w) -> p s w", w=sw_), AF.Copy, [bk], [zk])
                    if g == "S":
                        pad = 513 if t == 0 else 0
                        CP("dve", z_[:, pad:pad + 1], hb[:, ci:ci + 1], [hbk], [zk])
                    for tap in (0, 2):
                        TS(DG[(nm, tap, par)], identb[:], cw(tap, jc), None, ALU.mult, ALU.bypass, ["identb", "vecs"], [f"DG{nm}{tap}{par}"])
                return (t, j, par)

            def stage_b(info):
                t, j, par = info
                for nm in ("g", "v"):
                    a_, ak = ((R1, "R1") if nm == "g" else (R2, "R2")) if par == 0 else ((R3, "R3") if nm == "g" else (R4, "R4"))
                    z_ = ZB[(nm, par)]
                    zk = zkeys(nm, par)
                    B_, Bk = sbank()
                    for sg in range(nseg):
                        o0 = sg * sw_
                        zo = sg * pw_
                        MM(B_[:, o0:o0 + sw_], DG[(nm, 0, par)], z_[:, zo:zo + sw_], True, False, [f"DG{nm}0{par}", zk], [Bk])
                        MM(B_[:, o0:o0 + sw_], DG[(nm, 2, par)], z_[:, zo + 2:zo + 2 + sw_], False, True, [f"DG{nm}2{par}", zk], [Bk])
                    TT(a_[:, :], B_[:, :], a_[:, :], ALU.add, [Bk, ak], [ak])
                ag, agk, av, avk = (R1, "R1", R2, "R2") if par == 0 else (R3, "R3", R4, "R4")
                ACT(ag[:, :], ag[:, :], AF.Silu, [agk], [agk])
                TT(H[:, j, :], ag[:, :], av[:, :], ALU.mult, [agk, avk], [f"ATT.{j}"])

            for t in range(2):
                for nm in "gv":
                    for par in range(2):
                        z_ = ZB[(nm, par)]
                        if g == "P":
                            MS("dve", z_[:, 0:516].rearrange("p (s w) -> p s w", w=258)[:, :, 0:258:257], 0.0, [zkeys(nm, par)])
                        else:
                            zp = 0 if t == 0 else 513
                            MS("dve", z_[:, zp:zp + 1], 0.0, [zkeys(nm, par)])
                pending = None
                for pc in range(11):
                    Wg, wgk = wload(wu[:, :, pc * 256:(pc + 1) * 256], 8, 256)
                    Wv, wvk = wload(wu[:, :, 2816 + pc * 256:2816 + (pc + 1) * 256], 8, 256)
                    for jj in range(2):
                        j = pc * 2 + jj
                        cur = stage_a(t, j, Wg, wgk, Wv, wvk, jj)
                        if pending is not None:
                            stage_b(pending)
                        pending = cur
                stage_b(pending)
                for mp in range(4):
                    accs = [abank() for _ in range(2)]
                    for jh in range(2):
                        W, wk = wload(wd[:, jh * 11:(jh + 1) * 11, mp * 256:(mp + 1) * 256], 11, 256)
                        for mm in range(2):
                            b_, bk = accs[mm]
                            for jj in range(11):
                                j = jh * 11 + jj
                                MM(b_[:, :], W(jj, mm * 128, mm * 128 + 128), H[:, j, :], j == 0, j == 21, [wk, "ATT*"], [bk])
                    for mm in range(2):
                        m = mp * 2 + mm
                        b_, bk = accs[mm]
                        ACT(Y[:, m, :], b_[:, :], AF.Copy, [bk], [f"Y.{m}"])
                        TT(SQ[:, m, :], Y[:, m, :], Y[:, m, :], ALU.mult, [f"Y.{m}"], [f"SQ.{m}"])
                post_res(l, col, 5, t)

        def load_ctx_T(dst_fn, src, ncols, dkey):
            stg = STG[:, 0, :]
            for blk in range(2):
                DMA("sp", STG[:, blk, 0:ncols], src[blk * 128:(blk + 1) * 128, :], (), ["STG0", "STG1"], "cx")
            for blk in range(2):
                for c0 in range(0, ncols, 128):
                    cn = min(128, ncols - c0)
                    b_, bk = sbank()
                    A("pe", lambda h, b_=b_, blk=blk, c0=c0, cn=cn: h.transpose(b_[0:cn, 0:128], STG[:, blk, c0:c0 + cn], ident[:]),
                      ["STG0", "STG1", "ident"], [bk])
                    CP("dve", dst_fn(c0 // 128, blk, cn), b_[0:cn, 0:128], [bk], [dkey])

        def swap64(Wx, wxk):
            for k in range(8):
                src = Wx(k, 0, 512).rearrange("p (g s d) -> p g s d", s=2, d=32)
                dst = WSW[:, k * 512:(k + 1) * 512].rearrange("p (g s d) -> p g s d", s=2, d=32)
                CP("dve", dst[:, :, 0, :], src[:, :, 1, :], [wxk], ["WSW"])
                CP("dve", dst[:, :, 1, :], src[:, :, 0, :], [wxk], ["WSW"])

        def mixer_even(g, col):
            l = 0
            rope = g == "S"
            nkb = 10 if g == "S" else 8
            wi = w_in[0].rearrange("(kc p) n -> p kc n", p=128)
            WA, wak = wload(wi[:, :, 0:416], 8, 416)
            for v in range(2 if rope else 1):
                for k in range(8):
                    if v == 0:
                        CP("dve", WK96[:, 0, k, 64:96], WA(k, 384, 416), [wak], ["WK96"])
                    else:
                        CP("dve", WK96[:, 1, k, 64:80], WA(k, 400, 416), [wak], ["WK96"])
                        CP("dve", WK96[:, 1, k, 80:96], WA(k, 384, 400), [wak], ["WK96"])
            CQ = ATT[:, 0:2048].rearrange("p (a n) -> p a n", n=1024)
            for t in range(2):
                def ev_ckv(i, b_, bk, t=t):
                    head_rms(b_, bk, ones[:], 1.0 / 128)
                    STT(CKVN[:, t * 512:(t + 1) * 512], b_[:, :], vecs[:, VO["kvw"]:VO["kvw"] + 1], RS[:, :], ALU.mult, ALU.mult,
                        [bk, "RS", "vecs"], [f"CKVN.{t}"])
                proj_fm(WA, wak, [(256, 384)], t, ev_ckv)
                b_, bk = sbank()
                for k in range(8):
                    MM(b_[0:96, :], WK96[:, 0, k, :], U[:, k, t * 512:(t + 1) * 512], k == 0, k == 7, ["WK96", f"U{t}"], [bk])
                if rope:
                    bs_, bsk = sbank()
                    for k in range(8):
                        MM(bs_[0:96, :], WK96[:, 1, k, :], U[:, k, t * 512:(t + 1) * 512], k == 0, k == 7, ["WK96", f"U{t}"], [bsk])
                    rope_comb(None, None, b_, bk, bs_, bsk, ropea, (t * 512, (t + 1) * 512), 64, 96)
                    TT(KPE[64:96, t * 512:(t + 1) * 512], R1[64:96, :], R2[64:96, :], ALU.add, ["R1", "R2"], ["KPE*"])
                else:
                    CP("dve", KPE[64:96, t * 512:(t + 1) * 512], b_[64:96, :], [bk], ["KPE*"])
                cqb = []
                for i in range(2):
                    b2, b2k = abank()
                    for k in range(8):
                        MM(b2[:, :], WA(k, i * 128, i * 128 + 128), U[:, k, t * 512:(t + 1) * 512], k == 0, k == 7, [wak, f"U{t}"], [b2k])
                    ACT(SQ[:, i, :], b2[:, :], AF.Square, [b2k], ["SQ*"])
                    cqb.append((b2, b2k))
                s2, s2k = sbank()
                for i in range(2):
                    MM(s2[:, :], ones[:], SQ[:, i, :], i == 0, i == 1, ["ones", "SQ*"], [s2k])
                rstd_from(s2, s2k, 1.0 / 256, RS, "RS")
                for i in range(2):
                    b2, b2k = cqb[i]
                    STT(CQ[:, i, t * 512:(t + 1) * 512], b2[:, :], vecs[:, VO["qnw"] + i:VO["qnw"] + i + 1], RS[:, :], ALU.mult, ALU.mult,
                        [b2k, "RS", "vecs"], [f"CQ.{i}{t}"])
            if g == "S":
                load_ctx_T(lambda c, blk, cn: CKVN[0:cn, 1024 + blk * 128:1024 + (blk + 1) * 128], c_ckv, 128, "CKVN*")
                for blk in range(2):
                    DMA("sp", STG[:, blk, 0:32], c_kpe[blk * 128:(blk + 1) * 128, :], (), ["STG0", "STG1"], "cx")
                for blk in range(2):
                    b_, bk = sbank()
                    A("pe", lambda h, b_=b_, blk=blk: h.transpose(b_[0:32, 0:128], STG[:, blk, 0:32], ident[:]), ["STG0", "STG1", "ident"], [bk])
                    CP("dve", KPE[64:96, 1024 + blk * 128:1024 + (blk + 1) * 128], b_[0:32, 0:128], [bk], ["KPE*"])
            kmix = int(os.environ.get("KMIX", "99"))
            if kmix <= 1:
                return
            if g == "P":
                state_out(0, [(256, 416), (928, 1440), (1440, 1952)], wi)
            if kmix <= 2:
                return
            Wq, wqk = wload(w_uq.rearrange("(kc p) n -> p kc n", p=128), 2, 768)
            if rope:
                for k in range(2):
                    for hh in range(8):
                        c = hh * 96
                        CP("dve", WSW[:, k * 768 + c + 64:k * 768 + c + 80], Wq(k, c + 80, c + 96), [wqk], ["WSW"])
                        CP("dve", WSW[:, k * 768 + c + 80:k * 768 + c + 96], Wq(k, c + 64, c + 80), [wqk], ["WSW"])
                        CP("dve", WSW[:, k * 768 + c:k * 768 + c + 64], Wq(k, c, c + 64), [wqk], ["WSW"])
            Wk_, wkk = wload(w_uk.rearrange("(kc p) n -> p kc n", p=128), 1, 512)
            Wv_, wvk = wload(w_uv.rearrange("(kc p) n -> p kc n", p=128), 1, 512)
            NK = 1280
            KA = ATT[:, 2048:2048 + 4 * NK].rearrange("p (h n) -> p h n", n=NK)
            VA = ATT[:, 7168:7168 + 10 * 512].rearrange("p (b c) -> p b c", c=512)
            QA = ATT[:, 12288:12288 + 2048].rearrange("p (h n) -> p h n", n=512)
            for hf in range(2):
                for kt in range(0, NK if g == "S" else 1024, 512):
                    n = min(512, (NK if g == "S" else 1024) - kt)
                    for hh in range(4):
                        hg = hf * 4 + hh
                        b_, bk = sbank()
                        MM(b_[0:64, 0:n], Wk_(0, hg * 64, hg * 64 + 64), CKVN[:, kt:kt + n], True, True, [wkk, "CKVN*"], [bk])
                        CP("act" if hh % 2 else "dve", KA[0:64, hh, kt:kt + n], b_[0:64, 0:n], [bk], [f"KA.{hh}n{kt}"])
                        CP("dve" if hh % 2 else "act", KA[64:96, hh, kt:kt + n], KPE[64:96, kt:kt + n], ["KPE*"], [f"KA.{hh}r{kt}"])
                MS("dve", VA[:, :, :].rearrange("p b (h c) -> p b h c", c=128)[:, :, :, 64:128], 1.0, ["VA*"])
                for kb in range(nkb):
                    b_, bk = sbank()
                    MM(b_[:, 0:256], CKVN[:, kb * 128:(kb + 1) * 128], Wv_(0, hf * 256, hf * 256 + 256), True, True, [wvk, "CKVN*"], [bk])
                    CP("act" if kb % 2 else "dve", VA[:, kb, :].rearrange("p (h c) -> p h c", c=128)[:, :, 0:64],
                       b_[:, 0:256].rearrange("p (h c) -> p h c", c=64), [bk], [f"VA.{kb}"])
                for t in range(2):
                    for hh in range(4):
                        hg = hf * 4 + hh
                        b_, bk = sbank()
                        for k in range(2):
                            MM(b_[0:96, :], Wq(k, hg * 96, hg * 96 + 96), CQ[:, k, t * 512:(t + 1) * 512], k == 0, k == 1, [wqk, "CQ*"], [bk])
                        CP("act", QA[0:64, hh, :], b_[0:64, :], [bk], [f"QA.{hh}n"])
                        if rope:
                            bs_, bsk = sbank()
                            for k in range(2):
                                MM(bs_[0:96, :], WSW[:, k * 768 + hg * 96:k * 768 + hg * 96 + 96], CQ[:, k, t * 512:(t + 1) * 512],
                                   k == 0, k == 1, ["WSW", "CQ*"], [bsk])
                            rope_comb(None, None, b_, bk, bs_, bsk, ropea, (t * 512, (t + 1) * 512), 64, 96)
                            TT(QA[64:96, hh, :], R1[64:96, :], R2[64:96, :], ALU.add, ["R1", "R2"], [f"QA.{hh}r"])
                        else:
                            CP("dve", QA[64:96, hh, :], b_[64:96, :], [bk], [f"QA.{hh}r"])
                    if g == "S":
                        qsegs = [(0, 512, list(range(10)))]
                    else:
                        qsegs = [(0, 256, [4 * t, 4 * t + 1]), (256, 512, [4 * t + 2, 4 * t + 3])]
                    attend(4, lambda h, q0, q1: (QA[0:96, h, q0:q1], "QA*"),
                           lambda h, kb: (KA[0:96, h, kb * 128:(kb + 1) * 128], "KA*"),
                           lambda h, kb: (VA[:, kb, h * 128:(h + 1) * 128], "VA*"),
                           64, 96 ** -0.5, qsegs,
                           lambda h, q0, q1, num, numk, den, denk, t=t, hf=hf: ep_std(h + 0, t * 512 + q0, t * 512 + q1, num, numk, den, denk,
                                                                                       chunk0=hf * 2))
            if kmix <= 3:
                return
            NKd = NK if g == "S" else 1024
            QD = ATT[:, 0:2048].rearrange("p (h n) -> p h n", n=512)
            KD = ATT[:, 2048:2048 + 4 * NK].rearrange("p (h n) -> p h n", n=NK)
            VD = ATT[:, 7168:7168 + 10 * 512].rearrange("p (b c) -> p b c", c=512)
            WQd, wqdk = wload(wi[:, :, 416:928], 8, 512)
            WKd, wkdk = wload(wi[:, :, 928:1440], 8, 512)
            WVd, wvdk = wload(wi[:, :, 1440:1952], 8, 512)

            def proj_rope(Wx, wxk, dst_fn, dkey):
                if rope:
                    swap64(Wx, wxk)
                for t in range(2):
                    for hh in range(4):
                        b_, bk = sbank()
                        for k in range(8):
                            MM(b_[:, :], Wx(k, hh * 128, hh * 128 + 128), U[:, k, t * 512:(t + 1) * 512], k == 0, k == 7, [wxk, f"U{t}"], [bk])
                        if rope:
                            bs_, bsk = sbank()
                            for k in range(8):
                                MM(bs_[:, :], WSW[:, k * 512 + hh * 128:k * 512 + hh * 128 + 128], U[:, k, t * 512:(t + 1) * 512],
                                   k == 0, k == 7, ["WSW", f"U{t}"], [bsk])
                            rope_comb(None, None, b_, bk, bs_, bsk, ropeh, (t * 512, (t + 1) * 512), 0, 128)
                            TT(dst_fn(hh, t), R1[:, :], R2[:, :], ALU.add, ["R1", "R2"], [dkey])
                        else:
                            CP("act", dst_fn(hh, t), b_[:, :], [bk], [dkey])

            proj_rope(WKd, wkdk, lambda hh, t: KD[:, hh, t * 512:(t + 1) * 512], "KA*")
            for tb in range(8):
                b_, bk = sbank()
                for k in range(8):
                    MM(b_[:, :], U[:, k, tb * 128:(tb + 1) * 128], WVd(k, 0, 512), k == 0, k == 7, [wvdk, f"U{tb // 4}"], [bk])
                CP("act", VD[:, tb, :], b_[:, :], [bk], ["VA*"])
            if g == "S":
                load_ctx_T(lambda c, blk, cn: KD[:, c, 1024 + blk * 128:1024 + (blk + 1) * 128], c_dk, 512, "KA*")
                for blk in range(2):
                    DMA("pool", VD[:, 8 + blk, :], c_dv[blk * 128:(blk + 1) * 128, :], (), [f"VA.{8 + blk}"], f"cv{blk}")
            for t in range(2):
                def qdst(hh, tt):
                    return QD[:, hh, :]
                if rope and t == 0:
                    swap64(WQd, wqdk)
                for hh in range(4):
                    b_, bk = sbank()
                    for k in range(8):
                        MM(b_[:, :], WQd(k, hh * 128, hh * 128 + 128), U[:, k, t * 512:(t + 1) * 512], k == 0, k == 7, [wqdk, f"U{t}"], [bk])
                    if rope:
                        bs_, bsk = sbank()
                        for k in range(8):
                            MM(bs_[:, :], WSW[:, k * 512 + hh * 128:k * 512 + hh * 128 + 128], U[:, k, t * 512:(t + 1) * 512],
                               k == 0, k == 7, ["WSW", f"U{t}"], [bsk])
                        rope_comb(None, None, b_, bk, bs_, bsk, ropeh, (t * 512, (t + 1) * 512), 0, 128)
                        TT(QD[:, hh, :], R1[:, :], R2[:, :], ALU.add, ["R1", "R2"], ["QA*"])
                    else:
                        CP("act", QD[:, hh, :], b_[:, :], [bk], ["QA*"])
                if g == "S":
                    qsegs = [(0, 512, list(range(10)))]
                else:
                    qsegs = [(0, 256, [4 * t, 4 * t + 1]), (256, 512, [4 * t + 2, 4 * t + 3])]
                hold = {}

                def ep_diff(h8, q0, q1, num, numk, den, denk, t=t):
                    h, c = h8 // 2, h8 % 2
                    n = q1 - q0
                    ACT(RD[:, c, 0:n], den[:, 0:n], AF.Ln, [denk], [f"RD{c}"])
                    ACT(RD[:, c, 0:n], RD[:, c, 0:n], AF.Exp, [f"RD{c}"], [f"RD{c}"], scale=-1.0)
                    if c == 0:
                        TT(R1[:, q0:q1], num[:, 0:n], RD[:, 0, 0:n], ALU.mult, [numk, "RD0"], ["R1"])
                    else:
                        TT(R2[:, q0:q1], num[:, 0:n], RD[:, 1, 0:n], ALU.mult, [numk, "RD1"], ["R2"])
                        STT(R1[:, q0:q1], R2[:, q0:q1], lam[:, 3:4], R1[:, q0:q1], ALU.mult, ALU.add, ["R1", "R2", "lam"], ["R1"])
                        ob, obk = (R3, "R3") if (h + q0 // 256) % 2 == 0 else (R4, "R4")
                        CP("dve", ob[:, 0:n], R1[:, q0:q1], ["R1"], [obk])
                        ACT(SQ[:, (h + q0 // 256) % 2, 0:n], R1[:, q0:q1], AF.Square, ["R1"], [f"SQ.d{(h + q0 // 256) % 2}"])

                        def tail(h=h, q0=q0, q1=q1, n=n, ob=ob, obk=obk, t=t):
                            sl_ = (h + q0 // 256) % 2
                            s2, s2k = sbank()
                            MM(s2[:, 0:n], ones[:], SQ[:, sl_, 0:n], True, True, ["ones", f"SQ.d{sl_}"], [s2k])
                            rstd_from(s2, s2k, 1.0 / 128, RS, "RS", n=n)
                            STT(OT[:, 4 + h, t * 512 + q0:t * 512 + q1], ob[:, 0:n], SMALL[:, 0:1], RS[:, 0:n], ALU.mult, ALU.mult,
                                [obk, "RS", "SMALL"], [f"OT.{4 + h}.{t}.{q0}"])
                        deferred.append((gcount[0] + 2, tail))

                attend(8, lambda h8, q0, q1: (QD[64 * (h8 % 2):64 * (h8 % 2) + 64, h8 // 2, q0:q1], "QA*"),
                       lambda h8, kb: (KD[64 * (h8 % 2):64 * (h8 % 2) + 64, h8 // 2, kb * 128:(kb + 1) * 128], "KA*"),
                       lambda h8, kb: (VD[:, kb, (h8 // 2) * 128:(h8 // 2) * 128 + 128], "VA*"),
                       128, 64 ** -0.5, qsegs, ep_diff, fused=False)

        def state_out(l, colsets, wi):
            for ci, (c0, c1) in enumerate(colsets):
                pieces = [(a, min(a + 512, c1)) for a in range(c0, c1, 512)]
                for (a, b) in pieces:
                    w_ = b - a
                    W, wk = wload(wi[:, :, a:b], 8, w_)
                    off = sum(x1 - x0 for x0, x1 in colsets[:ci]) + (a - c0)
                    for tb in range(8):
                        b_, bk = sbank()
                        for k in range(8):
                            MM(b_[:, 0:w_], U[:, k, tb * 128:(tb + 1) * 128], W(k, 0, w_), k == 0, k == 7, [wk, f"U{tb // 4}"], [bk])
                        sl = tb % 2
                        sk_ = f"STG{sl}"
                        if (l == 0 and a == 256) or (l == 1 and a == 512):
                            nh_, hw_, wt = (1, 128, kvwb) if l == 0 else (2, 64, qk1b)
                            for hh in range(nh_):
                                A("act", lambda h, b_=b_, hh=hh, hw_=hw_: h.activation(
                                    out=PTS[:, 0:hw_], in_=b_[:, hh * hw_:(hh + 1) * hw_], func=AF.Square, accum_out=SMALL[:, 1:2]),
                                  [bk], ["PTS", "SMALL"])
                                ACT(SMALL[:, 2:3], SMALL[:, 1:2], AF.Ln, ["SMALL", "epsD"], ["SMALL"], scale=1.0 / hw_, bias=epsD[:, :])
                                ACT(SMALL[:, 2:3], SMALL[:, 2:3], AF.Exp, ["SMALL"], ["SMALL"], scale=-0.5)
                                STT(STG[:, sl, hh * hw_:(hh + 1) * hw_], b_[:, hh * hw_:(hh + 1) * hw_], SMALL[:, 2:3], wt[:, 0:hw_],
                                    ALU.mult, ALU.mult, [bk, "SMALL", "kvwb", "qk1b"], [sk_])
                            CP("dve", STG[:, sl, 128:w_], b_[:, 128:w_], [bk], [sk_])
                        else:
                            CP("act" if tb % 2 else "dve", STG[:, sl, 0:w_], b_[:, 0:w_], [bk], [sk_])
                        DMA("sp", st_out[l][tb * 128:(tb + 1) * 128, off:off + w_], STG[:, sl, 0:w_], [sk_], (), f"so{sl}")
                    if pending_side:
                        ph_ = P.phase
                        P.phase = ph_.split("/")[0] + "/mod"
                        pending_side.pop(0)()
                        P.phase = ph_

        def mixer_odd(g, col):
            l = 1
            rope = g == "S"
            NK = 1280
            nkeys = NK if g == "S" else 1024
            nkb = 10 if g == "S" else 8
            wi = w_in[1].rearrange("(kc p) n -> p kc n", p=128)
            qw = lambda i: vecs[:, VO["qkw"] + i:VO["qkw"] + i + 1]
            qws = lambda i: vecs[:, VO["qkws"] + i:VO["qkws"] + i + 1]
            if g == "P":
                state_out(1, [(512, 768), (1280, 1792), (1792, 2304)], wi)
            QC = ATT[:, 0:2048].rearrange("p (h n) -> p h n", n=512)
            KC = ATT[:, 2048:2048 + 2 * NK].rearrange("p (h n) -> p h n", n=NK)
            VC = ATT[:, 4608:4608 + 10 * 256].rearrange("p (b c) -> p b c", c=256)
            WQ, wqk = wload(wi[:, :, 0:512], 8, 512)
            WKV, wkvk = wload(wi[:, :, 512:768], 8, 256)
            WD = WSW[:, 0:2048].rearrange("p (k c) -> p k c", c=256)
            WDS = WSW[:, 2048:4096].rearrange("p (k c) -> p k c", c=256)
            for k in range(8):
                for kv in range(2):
                    for dup in range(2):
                        c = kv * 128 + dup * 64
                        CP("dve", WD[:, k, c:c + 64], WKV(k, kv * 64, kv * 64 + 64), [wkvk], ["WSW"])
                        if rope:
                            CP("dve", WDS[:, k, c:c + 32], WKV(k, kv * 64 + 32, kv * 64 + 64), [wkvk], ["WSW"])
                            CP("dve", WDS[:, k, c + 32:c + 64], WKV(k, kv * 64, kv * 64 + 32), [wkvk], ["WSW"])

            def qk_norm_rope(dst, dkey, b_, bk, bs_, bsk, wi_, t):
                head_rms(b_, bk, bd[:], 1.0 / 64)
                if rope:
                    rope_comb(None, None, b_, bk, bs_, bsk, ropeh, (t * 512, (t + 1) * 512), 0, 128, w=qw(wi_), ws=qws(wi_))
                    TT(R1[:, :], R1[:, :], R2[:, :], ALU.add, ["R1", "R2"], ["R1"])
                    TT(dst, R1[:, :], RS[:, :], ALU.mult, ["R1", "RS"], [dkey])
                else:
                    STT(dst, b_[:, :], qw(wi_), RS[:, :], ALU.mult, ALU.mult, [bk, "RS", "vecs"], [dkey])

            for t in range(2):
                for kv in range(2):
                    b_, bk = sbank()
                    for k in range(8):
                        MM(b_[:, :], WD[:, k, kv * 128:(kv + 1) * 128], U[:, k, t * 512:(t + 1) * 512], k == 0, k == 7, ["WSW", f"U{t}"], [bk])
                    bs_, bsk = None, None
                    if rope:
                        bs_, bsk = sbank()
                        for k in range(8):
                            MM(bs_[:, :], WDS[:, k, kv * 128:(kv + 1) * 128], U[:, k, t * 512:(t + 1) * 512], k == 0, k == 7, ["WSW", f"U{t}"], [bsk])
                    qk_norm_rope(KC[:, kv, t * 512:(t + 1) * 512], "KA*", b_, bk, bs_, bsk, 1, t)
            MS("dve", VC[:, :, :].rearrange("p b (h c) -> p b h c", c=128)[:, :, :, 64:128], 1.0, ["VA*"])
            for tb in range(8):
                b_, bk = sbank()
                for k in range(8):
                    MM(b_[:, 0:128], U[:, k, tb * 128:(tb + 1) * 128], WKV(k, 128, 256), k == 0, k == 7, [wkvk, f"U{tb // 4}"], [bk])
                CP("act" if tb % 2 else "dve", VC[:, tb, :].rearrange("p (h c) -> p h c", c=128)[:, :, 0:64],
                   b_[:, 0:128].rearrange("p (h c) -> p h c", c=64), [bk], [f"VA.{tb}"])
            if g == "S":
                for blk in range(2):
                    DMA("sp", STG[:, blk, 0:128], c_gk[blk * 128:(blk + 1) * 128, :], (), ["STG0", "STG1"], "cx")
                for blk in range(2):
                    for kv in range(2):
                        b_, bk = sbank()
                        A("pe", lambda h, b_=b_, blk=blk, kv=kv: h.transpose(b_[0:64, 0:128], STG[:, blk, kv * 64:(kv + 1) * 64], ident[:]),
                          ["STG0", "STG1", "ident"], [bk])
                        CP("dve", KC[0:64, kv, 1024 + blk * 128:1024 + (blk + 1) * 128], b_[0:64, 0:128], [bk], ["KA*"])
                        CP("act", KC[64:128, kv, 1024 + blk * 128:1024 + (blk + 1) * 128], b_[0:64, 0:128], [bk], ["KA*"])
                for blk in range(2):
                    DMA("sp", STG[:, blk, 0:128], c_gv[blk * 128:(blk + 1) * 128, :], ["KA*"], ["STG0", "STG1"], "cx")
                for blk in range(2):
                    CP("dve", VC[:, 8 + blk, :].rearrange("p (h c) -> p h c", c=128)[:, :, 0:64],
                       STG[:, blk, 0:128].rearrange("p (h c) -> p h c", c=64), ["STG0", "STG1"], [f"VA.{8 + blk}"])
            if rope:
                swap64(WQ, wqk)
            CMB = {}
            for hg in range(8):
                if hg < 4:
                    CMB[hg] = (CM2[:, hg], "Y*")
                elif hg < 6:
                    CMB[hg] = (SQ[:].rearrange("p a b -> p (a b)")[:, 0:3328].rearrange("p (h x c) -> p h x c", x=26, c=64)[:, hg - 4], "SQ*")
                else:
                    CMB[hg] = (WSW[:, 0:3328].rearrange("p (h x c) -> p h x c", x=26, c=64)[:, hg - 6], "WSW")

            def cm_zero(which):
                def f():
                    if which == 0:
                        MS("dve", Y[:], 0.0, ["Y*"])
                        MS("dve", SQ[:].rearrange("p a b -> p (a b)")[:, 0:3328], 0.0, ["SQ*"])
                    else:
                        MS("dve", WSW[:, 0:3328], 0.0, ["WSW"])
                return f

            def cm_gen(hg):
                def f():
                    dstv, dk = CMB[hg]
                    TAh = STG[:].rearrange("p a b -> p (a b)")[:, 0:960]
                    for i in range(2):
                        src = bass.AP(tpad.tensor, hg * 15 * 160 + 16, [[1, 64], [160, 15], [1, 64]])
                        DMA("sp", TAh[64 * i:64 * i + 64, :].rearrange("p (a b) -> p a b", b=64), src, (), ["STG0", "STG1"], "tp")
                    for i in range(2):
                        srcs = sb_ap(STG, 64 * i, 64, 14 * 64 + 63, [[-64, 15], [-1, 64]])
                        dst = dstv[64 * i:64 * i + 64, i + 4:i + 19, :]
                        ACT(dst, srcs, AF.Exp, ["STG0", "STG1"], [dk])
                        ck = sb_ap(colok, 64 * i, 64, 0, [[0, 15], [1, 64]])
                        TT(dst, dst, ck, ALU.mult, ["colok", dk], [dk])
                return f
            side_t = {0: [cm_zero(0)] + [cm_gen(hg) for hg in range(0, 5)], 1: [cm_zero(1)] + [cm_gen(hg) for hg in range(5, 8)]} if g == "S" else {0: None, 1: None}
            for t in range(2):
                for hh in range(4):
                    b_, bk = sbank()
                    for k in range(8):
                        MM(b_[:, :], WQ(k, hh * 128, hh * 128 + 128), U[:, k, t * 512:(t + 1) * 512], k == 0, k == 7, [wqk, f"U{t}"], [bk])
                    bs_, bsk = None, None
                    if rope:
                        bs_, bsk = sbank()
                        for k in range(8):
                            MM(bs_[:, :], WSW[:, k * 512 + hh * 128:k * 512 + hh * 128 + 128], U[:, k, t * 512:(t + 1) * 512], k == 0, k == 7,
                               ["WSW", f"U{t}"], [bsk])
                    qk_norm_rope(QC[:, hh, :], "QA*", b_, bk, bs_, bsk, 0, t)
                if g == "S":
                    qsegs = [(0, 512, list(range(10)))]
                else:
                    qsegs = [(0, 256, [4 * t, 4 * t + 1]), (256, 512, [4 * t + 2, 4 * t + 3])]
                attend(8, lambda h, q0, q1: (QC[64 * (h % 2):64 * (h % 2) + 64, h // 2, q0:q1], "QA*"),
                       lambda h, kb: (KC[64 * (h % 2):64 * (h % 2) + 64, h // 4, kb * 128:(kb + 1) * 128], "KA*"),
                       lambda h, kb: (VC[:, kb, (h // 4) * 128:(h // 4) * 128 + 128], "VA*"),
                       64, 64 ** -0.5, qsegs,
                       lambda h, q0, q1, num, numk, den, denk, t=t: ep_std(h, t * 512 + q0, t * 512 + q1, num, numk, den, denk, chunk0=0),
                       side=side_t[t], side_every=8 if t == 0 else 12)
                while side_t[t]:
                    side_t[t].pop(0)()

            KN = ATT[:, 0:4 * NK].rearrange("p (h n) -> p h n", n=NK)
            VN = ATT[:, 5120:5120 + 10 * 512].rearrange("p (b c) -> p b c", c=512)
            QNt = [ATT[:, 10240 + tt * 2048:10240 + (tt + 1) * 2048].rearrange("p (h n) -> p h n", n=512) for tt in range(2)]
            WQn, wqnk = None, None
            for hf in range(2):
                WKn, wknk = wload(wi[:, :, 1280 + hf * 256:1280 + (hf + 1) * 256], 8, 256)
                WVn, wvnk = wload(wi[:, :, 1792 + hf * 256:1792 + (hf + 1) * 256], 8, 256)
                WQn, wqnk = wload(wi[:, :, 768 + hf * 256:768 + (hf + 1) * 256], 8, 256)
                for t in range(2):
                    for hh in range(4):
                        b_, bk = sbank()
                        for k in range(8):
                            MM(b_[0:64, :], WKn(k, hh * 64, hh * 64 + 64), U[:, k, t * 512:(t + 1) * 512], k == 0, k == 7, [wknk, f"U{t}"], [bk])
                        CP("act", KN[0:64, hh, t * 512:(t + 1) * 512], b_[0:64, :], [bk], ["KA*"])
                MS("dve", VN[:, :, :].rearrange("p b (h c) -> p b h c", c=128)[:, :, :, 64:128], 1.0, ["VA*", "KA*", "QA*"])
                for tb in range(8):
                    b_, bk = sbank()
                    for k in range(8):
                        MM(b_[:, 0:256], U[:, k, tb * 128:(tb + 1) * 128], WVn(k, 0, 256), k == 0, k == 7, [wvnk, f"U{tb // 4}"], [bk])
                    CP("act" if tb % 2 else "dve", VN[:, tb, :].rearrange("p (h c) -> p h c", c=128)[:, :, 0:64],
                       b_[:, 0:256].rearrange("p (h c) -> p h c", c=64), [bk], [f"VA.{tb}"])
                if g == "S":
                    for blk in range(2):
                        DMA("sp", STG[:, blk, 0:256], c_nk[blk * 128:(blk + 1) * 128, hf * 256:(hf + 1) * 256], (), ["STG0", "STG1"], "cx")
                    for blk in range(2):
                        for hh in range(4):
                            b_, bk = sbank()
                            A("pe", lambda h, b_=b_, blk=blk, hh=hh: h.transpose(b_[0:64, 0:128], STG[:, blk, hh * 64:(hh + 1) * 64], ident[:]),
                              ["STG0", "STG1", "ident"], [bk])
                            CP("dve", KN[0:64, hh, 1024 + blk * 128:1024 + (blk + 1) * 128], b_[0:64, 0:128], [bk], ["KA*"])
                    for blk in range(2):
                        DMA("pool", VN[:, 8 + blk, :].rearrange("p (h c) -> p h c", c=128)[:, :, 0:64],
                            c_nv[blk * 128:(blk + 1) * 128, hf * 256:(hf + 1) * 256].rearrange("p (h c) -> p h c", c=64), (), [f"VA.{8 + blk}"], f"cn{blk}")
                    CP("dve", KN[64:80, :, 0:1024], sb_ap(KPE, 0, 16, 0, [[0, 4], [1, 1024]]), ["AUG"], ["KA*", "VA*", "QA*"])
                    MS("dve", KN[64:80, :, 1024:1280], 0.0, ["KA*", "VA*", "QA*"])
                if g == "S":
                    for tt in range(2):
                        CP("act", QNt[tt][64:80, :, :], sb_ap(KPE, 32, 16, tt * 512, [[0, 4], [1, 512]]), ["AUG"], [f"QA.a{tt}", "KA*", "VA*"])
                for t in range(2):
                    QN = QNt[t]
                    for hh in range(4):
                        b_, bk = sbank()
                        for k in range(8):
                            MM(b_[0:64, :], WQn(k, hh * 64, hh * 64 + 64), U[:, k, t * 512:(t + 1) * 512], k == 0, k == 7, [wqnk, f"U{t}"], [bk])
                        ACT(QN[0:64, hh, :], b_[0:64, :], AF.Copy, [bk], [f"QA.q{t}{hh}"], scale=0.125)
                    if g == "S":
                        own = list(range(0, 6)) if t == 0 else list(range(2, 8))
                        qsegs = [(0, 512, own + [8, 9])]
                        kdim = 80

                        def cmf(h, q0, kb, t=t, hf=hf):
                            if kb >= 8:
                                return None
                            x0 = 7 + 8 * t - 2 * kb + 4
                            return CMB[hf * 4 + h][0][:, x0:x0 + 8, :].rearrange("p a b -> p (a b)")
                    else:
                        qsegs = [(0, 256, [4 * t, 4 * t + 1]), (256, 512, [4 * t + 2, 4 * t + 3])]
                        kdim = 64
                        cmf = None
                    attend(4, lambda h, q0, q1, QN=QN: (QN[0:kdim, h, q0:q1], "QA*"),
                           lambda h, kb: (KN[0:kdim, h, kb * 128:(kb + 1) * 128], "KA*"),
                           lambda h, kb: (VN[:, kb, h * 128:(h + 1) * 128], "VA*"),
                           64, 1.0, qsegs,
                           lambda h, q0, q1, num, numk, den, denk, t=t, hf=hf: ep_std(h, t * 512 + q0, t * 512 + q1, num, numk, den, denk,
                                                                                       chunk0=4 + hf * 2),
                           cmf=cmf)

        import os
        dbg = os.environ.get("KDBG", "")
        stop = False
        for g, col in (("P", 0), ("S", 1)):
            if stop:
                break
            P.phase = f"{g}:load"
            load_x(g)
            if dbg == f"{g},0,load":
                store_y(g)
                break
            for l in range(2):
                P.phase = f"{g}{l}:norm1"
                if g == "P":
                    mod_part(l, 0)
                    mod_part(l, 1)
                for t in range(2):
                    norm_mod(l, col, 0, 1, t)
                if g == "P":
                    pending_side.extend([(lambda l=l, i=i: mod_part(l, i)) for i in (2, 3, 4, 5)])
                if dbg == f"{g},{l},norm":
                    stop = True
                    break
                P.phase = f"{g}{l}:mixer"
                if l == 0:
                    mixer_even(g, col)
                else:
                    mixer_odd(g, col)
                P.phase = f"{g}{l}:modrest"
                while pending_side:
                    pending_side.pop(0)()
                P.phase = f"{g}{l}:wout"
                if dbg == f"{g},{l},mixonly":
                    stop = True
                    break
                for t in range(2):
                    wout_phase(l, col, t)
                if dbg == f"{g},{l},mix":
                    stop = True
                    break
                P.phase = f"{g}{l}:norm2"
                for t in range(2):
                    norm_mod(l, col, 3, 4, t)
                P.phase = f"{g}{l}:ffn"
                if g == "P":
                    ffn_phase(l, g, col)
                else:
                    ffn_phase_dve(l, g, col)
                if dbg == f"{g},{l},ffn":
                    stop = True
                    break
            P.phase = f"{g}:store"
            store_y(g)

        P.final_waits("sp")
        _CACHE["labels"] = P.labels
        with nc.Block() as block:
            P.emit(block)
    return nc


_CACHE = {}


def _rope_tables():
    def table(n, rot):
        t = np.arange(n)
        nf = rot // 4
        inv = 1.0 / (10000.0 ** (np.arange(nf) / nf))
        ang = np.concatenate([(t // 64)[:, None] * inv[None, :], (t % 64)[:, None] * inv[None, :]], axis=-1)
        return np.cos(ang).astype(np.float32), np.sin(ang).astype(np.float32)
    cos_a, sin_a = table(1024, 32)
    cos_h, sin_h = table(1024, 64)
    rh = np.zeros((128, 2, 1024), np.float32)
    for p in range(128):
        d = p % 64
        j = d % 32
        rh[p, 0] = cos_h[:, j]
        rh[p, 1] = (-1.0 if d < 32 else 1.0) * sin_h[:, j]
    ra = np.zeros((128, 2, 1024), np.float32)
    for p in range(64, 96):
        d = p - 64
        j = d % 16
        ra[p, 0] = cos_a[:, j]
        ra[p, 1] = (-1.0 if d < 16 else 1.0) * sin_a[:, j]
    return rh, ra


def _na_consts():
    rows = 16
    r = np.arange(rows)
    rs = np.clip(r - 4, 0, rows - 8)
    rowok = (r[None, :] >= rs[:, None]) & (r[None, :] < rs[:, None] + 8)
    col = np.arange(64)
    cs = np.clip(col - 8, 0, 48)
    col_ok = (col[None, :] >= cs[:, None]) & (col[None, :] < cs[:, None] + 16)
    aug = np.zeros((32, 1024), np.float32)
    tq = np.arange(1024) // 64
    for m in range(16):
        aug[m] = np.where(rowok[tq, m], 0.0, -BIG)
        aug[16 + m] = (tq == m).astype(np.float32)
    ck = np.zeros((128, 64), np.float32)
    for i in range(2):
        ck[64 * i:64 * i + 64] = col_ok.T.astype(np.float32)
    return aug, ck


def make_in_maps(inp):
    f = lambda k: np.ascontiguousarray(np.asarray(inp[k], dtype=np.float32))
    x_prompt, x_sample, c = f("x_prompt"), f("x_sample"), f("c")
    fm = lambda v: np.ascontiguousarray(v.reshape(-1, 128).T)
    vec_list = []
    nwv = f("norm_w")
    vec_list.append(np.concatenate([fm(nwv[l, i]) for l in range(2) for i in range(4)], axis=1))
    bm = f("b_mod")
    vec_list.append(np.concatenate([fm(bm[l]) for l in range(2)], axis=1))
    cwv = f("conv_w")
    vec_list.append(np.concatenate([fm(cwv[l, j]) for l in range(2) for j in range(3)], axis=1))
    cbv = f("conv_b")
    vec_list.append(np.concatenate([fm(cbv[l]) for l in range(2)], axis=1))
    vec_list.append(fm(f("q_norm_w")[0]))
    vec_list.append(fm(f("kv_norm_w")[0]))
    vec_list.append(fm(f("diff_subln_w")[0]))
    qk = f("qk_norm_w")[0]
    vec_list.append(np.stack([np.tile(qk[0], 2), np.tile(qk[1], 2)], axis=1))
    sw = lambda v: np.concatenate([v[32:], v[:32]])
    vec_list.append(np.stack([np.tile(sw(qk[0]), 2), np.tile(sw(qk[1]), 2)], axis=1))
    vecs = np.concatenate(vec_list, axis=1).astype(np.float32)
    vecs = np.ascontiguousarray(np.pad(vecs, ((0, 0), (0, 528 - vecs.shape[1]))))
    rh, ra = _rope_tables()
    aug, colok = _na_consts()
    shared = {
        "ident": np.eye(128, dtype=np.float32), "vecs": vecs, "ropeh": rh, "ropea": ra, "aug": aug, "colok": colok,
        "lamb": np.ascontiguousarray(np.broadcast_to(f("diff_lam")[0].reshape(1, 256), (128, 256))),
        "kvwb": np.ascontiguousarray(np.broadcast_to(f("kv_norm_w")[0].reshape(1, 128), (128, 128))),
        "qk1b": np.ascontiguousarray(np.broadcast_to(qk[1].reshape(1, 64), (128, 64))),
        "w_mod": f("w_mod"), "w_in_even": f("w_in_even")[0], "w_in_odd": f("w_in_odd")[0],
        "w_out_even": f("w_out_even")[0], "w_out_odd": f("w_out_odd")[0], "w_uq": f("w_uq")[0], "w_uk": f("w_uk")[0],
        "w_uv": f("w_uv")[0], "w_up": f("w_up"), "w_down": f("w_down"),
        "rpbp": np.ascontiguousarray(np.pad(f("na_rpb")[0].reshape(120, 31), ((0, 0), (64, 65)))),
    }
    c_ctx = f("c_ctx")
    caches = {"c_ckv": ("cache_mla_ckv", 128), "c_kpe": ("cache_mla_kpe", 32), "c_dk": ("cache_diff_k", 512),
              "c_dv": ("cache_diff_v", 512), "c_gk": ("cache_gqa_k", 128), "c_gv": ("cache_gqa_v", 128),
              "c_nk": ("cache_na_k", 512), "c_nv": ("cache_na_v", 512)}
    in_maps = []
    for core in range(NCORES):
        b = core % 4
        m = dict(shared)
        m["xp"] = np.ascontiguousarray(x_prompt[core * 4:(core + 1) * 4].reshape(1024, 1024))
        m["xs"] = np.ascontiguousarray(x_sample[b])
        cv = np.stack([c_ctx, c[b]], axis=0)
        m["cvT"] = np.ascontiguousarray(cv.reshape(2, 8, 128).transpose(2, 1, 0).reshape(128, 16))
        for k, (src, n) in caches.items():
            m[k] = np.ascontiguousarray(f(src)[b, 0].reshape(256, n))
        in_maps.append(m)
    return in_maps


def kernel(**inp):
    if "nc" not in _CACHE:
        _CACHE["nc"] = build_program()
    nc = _CACHE["nc"]
    in_maps = make_in_maps(inp)
    res = run_bass_kernel_spmd(nc, in_maps, core_ids=list(range(NCORES)))
    R = res.results
    y_prompt = np.concatenate([R[i]["yp"].reshape(4, 256, 1024) for i in range(NCORES)], axis=0)
    y_sample = np.stack([R[i]["ys"] for i in range(4)], axis=0)
    st0 = np.concatenate([R[i]["st0"].reshape(4, 256, 1184) for i in range(NCORES)], axis=0)
    st1 = np.concatenate([R[i]["st1"].reshape(4, 256, 1280) for i in range(NCORES)], axis=0)
    outs = (
        y_prompt, y_sample,
        st0[:, :, 0:128].reshape(32, 1, 256, 128), st0[:, :, 128:160].reshape(32, 1, 256, 32),
        st0[:, :, 160:672].reshape(32, 1, 256, 4, 128), st0[:, :, 672:1184].reshape(32, 1, 256, 4, 128),
        st1[:, :, 0:128].reshape(32, 1, 256, 2, 64), st1[:, :, 128:256].reshape(32, 1, 256, 2, 64),
        st1[:, :, 256:768].reshape(32, 1, 256, 8, 64), st1[:, :, 768:1280].reshape(32, 1, 256, 8, 64),
    )
    return tuple(np.ascontiguousarray(o, dtype=np.float32) for o in outs)
```

```python
import math
import numpy as np
from contextlib import ExitStack
import concourse.bass as bass
import concourse.mybir as mybir
from concourse.bass_utils import run_bass_kernel_spmd

F32 = mybir.dt.float32
BF16 = mybir.dt.bfloat16
AF = mybir.ActivationFunctionType
ALU = mybir.AluOpType

EPOCH = 6000
ENGS = ("pe", "act", "dve", "pool", "sp")
EPS = 1e-6
BIG = 30000.0
NCORES = 8


class Prog:
    def __init__(self, nc):
        self.nc = nc
        self.ops = {e: [] for e in ENGS}
        self.cnt = {e: 0 for e in ENGS}
        self.esems = {e: [] for e in ENGS}
        self.waited = {}
        self.lastw = {}
        self.readers = {}
        self.dsems = {}
        self.allkeys = set()
        self.phase = ''
        self.labels = {e: [] for e in ENGS}

    def _esem(self, e, ep):
        while len(self.esems[e]) <= ep:
            self.esems[e].append(self.nc.alloc_semaphore(name=f"s_{e}_{len(self.esems[e])}"))
        return self.esems[e][ep]

    def _event_of(self, e, idx):
        return (self._esem(e, idx // EPOCH), idx % EPOCH + 1, e, idx)

    def _need(self, eng, ev, waits):
        if ev is None:
            return
        sem, val, src, idx = ev
        if src == "pe" and eng == "pe":
            return
        key = (eng, id(sem))
        if self.waited.get(key, 0) >= val:
            return
        self.waited[key] = val
        waits.append((sem, val))

    def _expand(self, k):
        if k.endswith("*"):
            pre = k[:-1] + "."
            return [k] + [x for x in self.allkeys if x.startswith(pre)]
        if "." in k:
            self.allkeys.add(k)
            return [k, k.split(".")[0] + "*"]
        return [k]

    def op(self, eng, fn, reads=(), writes=(), dsem=None):
        waits = []
        rk = [x for k in reads for x in self._expand(k)]
        wk = [x for k in writes for x in self._expand(k)]
        for k in rk:
            self._need(eng, self.lastw.get(k), waits)
            if k.startswith("ps"):
                for ev in self.readers.get(k, ()):
                    if ev[2] != eng:
                        self._need(eng, ev, waits)
        for k in wk:
            self._need(eng, self.lastw.get(k), waits)
            for ev in self.readers.get(k, ()):
                self._need(eng, ev, waits)
        if dsem is not None:
            if dsem not in self.dsems:
                self.dsems[dsem] = [self.nc.alloc_semaphore(name=f"d_{dsem}"), 0]
            d = self.dsems[dsem]
            d[1] += 16
            ev = (d[0], d[1], "dma", None)
            inc = (d[0], 16)
        else:
            idx = self.cnt[eng]
            self.cnt[eng] += 1
            ev = self._event_of(eng, idx)
            inc = (ev[0], 1)
        for k in writes:
            if k.endswith("*"):
                for x in self._expand(k):
                    self.lastw[x] = ev
                    self.readers[x] = []
            else:
                self.lastw[k] = ev
                self.readers[k] = []
        for k in reads:
            if k.endswith("*"):
                for x in self._expand(k):
                    self.readers.setdefault(x, []).append(ev)
            else:
                self.readers.setdefault(k, []).append(ev)
        self.ops[eng].append((fn, waits, inc))
        self.labels[eng].append(self.phase)
        return ev

    def final_waits(self, eng="sp"):
        waits = []
        for name, (sem, val) in self.dsems.items():
            if val > 0:
                waits.append((sem, val))
        for e in ENGS:
            if e != eng and self.cnt[e] > 0:
                ev = self._event_of(e, self.cnt[e] - 1)
                waits.append((ev[0], ev[1]))
        self.ops[eng].append((None, waits, None))

    def emit(self, block):
        hmap = {"pe": "tensor", "act": "scalar", "dve": "vector", "pool": "gpsimd", "sp": "sync"}

        def mk(e):
            def body(h):
                for fn, waits, inc in self.ops[e]:
                    for sem, val in waits:
                        h.wait_ge(sem, val)
                    if fn is not None:
                        fn(h).then_inc(inc[0], inc[1])
            return body

        for e in ENGS:
            if self.ops[e]:
                getattr(block, hmap[e])(mk(e))


def sb_ap(t, p0, npart, off, dims):
    fsz = 1
    for s in t.shape[1:]:
        fsz *= s
    return bass.AP(t, p0 * fsz + off, [[fsz, npart]] + [list(d) for d in dims])


def build_program():
    nc = bass.Bass("TRN2", target_bir_lowering=False)
    di = lambda n, s: nc.dram_tensor(n, list(s), F32, kind="ExternalInput").ap()
    do = lambda n, s: nc.dram_tensor(n, list(s), F32, kind="ExternalOutput").ap()
    xin = {"P": di("xp", [1024, 1024]), "S": di("xs", [1024, 1024])}
    yout = {"P": do("yp", [1024, 1024]), "S": do("ys", [1024, 1024])}
    st_out = [do("st0", [1024, 1184]), do("st1", [1024, 1280])]
    d_ident = di("ident", [128, 128])
    d_cvT = di("cvT", [128, 16])
    d_vecs = di("vecs", [128, 528])
    d_ropeh = di("ropeh", [128, 2, 1024])
    d_ropea = di("ropea", [128, 2, 1024])
    d_aug = di("aug", [32, 1024])
    d_colok = di("colok", [128, 64])
    d_lam = di("lamb", [128, 256])
    d_kvwb = di("kvwb", [128, 128])
    c_ckv = di("c_ckv", [256, 128]); c_kpe = di("c_kpe", [256, 32])
    c_dk = di("c_dk", [256, 512]); c_dv = di("c_dv", [256, 512])
    c_gk = di("c_gk", [256, 128]); c_gv = di("c_gv", [256, 128])
    c_nk = di("c_nk", [256, 512]); c_nv = di("c_nv", [256, 512])
    w_mod = di("w_mod", [2, 1024, 6144])
    w_in = [di("w_in_even", [1024, 1952]), di("w_in_odd", [1024, 2304])]
    w_outw = [di("w_out_even", [1024, 1024]), di("w_out_odd", [1024, 1024])]
    w_uq = di("w_uq", [256, 768]); w_uk = di("w_uk", [128, 512]); w_uv = di("w_uv", [128, 512])
    w_up = di("w_up", [2, 1024, 5632]); w_down = di("w_down", [2, 2816, 1024])
    tpad = di("rpbp", [120, 160])
    d_qk1b = di("qk1b", [128, 64])

    VO = {}
    o = 0
    for name, n in [("nw", 64), ("bmod", 96), ("cw", 264), ("cb", 88), ("qnw", 2), ("kvw", 1), ("sub", 1),
                    ("qkw", 2), ("qkws", 2)]:
        VO[name] = o
        o += n
    assert o <= 528

    with ExitStack() as es:
        sb = lambda n, s, d: es.enter_context(nc.sbuf_tensor("sb_" + n, list(s), d))
        P = Prog(nc)
        ps = [es.enter_context(nc.psum_tensor(f"ps{i}", [128, 512], F32)) for i in range(8)]
        rr = {"s": 0, "a": 0}

        rr["ns"] = 4

        def sbank():
            i = rr["s"] % rr["ns"]
            rr["s"] += 1
            return ps[i], f"ps{i}"

        def abank():
            na = 8 - rr["ns"]
            i = rr["ns"] + rr["a"] % na
            rr["a"] += 1
            return ps[i], f"ps{i}"

        X = sb("X", [128, 8, 1024], F32)
        U = sb("U", [128, 8, 1024], BF16)
        Y = sb("Y*", [128, 8, 512], F32)
        SQ = sb("SQ*", [128, 8, 512], BF16)
        OT = sb("OT*", [128, 8, 1024], BF16)
        NW = 4
        WB = [sb(f"WB{i}", [128, 4096], BF16) for i in range(NW)]
        WSW = sb("WSW", [128, 4096], BF16)
        ATT = sb("ATT", [128, 14336], BF16)
        ident = sb("ident", [128, 128], F32)
        ones = sb("ones", [128, 128], BF16)
        bd = sb("bd", [128, 128], BF16)
        vecs = sb("vecs", [128, 528], F32)
        cvT = sb("cvT", [128, 16], F32)
        csT = sb("csT", [128, 16], BF16)
        modT = sb("modT", [128, 2, 96], F32)
        MT = sb("MT", [128, 2, 6, 16], F32)
        RS = sb("RS", [128, 512], F32)
        RD = sb("RD", [128, 2, 512], F32)
        R1 = sb("R1", [128, 512], F32)
        R2 = sb("R2", [128, 512], F32)
        R3 = sb("R3", [128, 512], F32)
        R4 = sb("R4", [128, 512], F32)
        PTALL = sb("PTALL", [128, 7 * 512], BF16)
        PT = [PTALL[:, i * 512:(i + 1) * 512] for i in range(6)]
        ZB = {(nm, par): PTALL[:, (ni * 2 + par) * 516:(ni * 2 + par + 1) * 516] for ni, nm in enumerate("gv") for par in range(2)}
        DG = {(nm, tap, par): PTALL[:, 2064 + ((ni * 2 + ti) * 2 + par) * 128:2064 + ((ni * 2 + ti) * 2 + par + 1) * 128]
              for ni, nm in enumerate("gv") for ti, tap in enumerate((0, 2)) for par in range(2)}
        identb = sb("identb", [128, 128], BF16)
        PTS = sb("PTS", [128, 512], BF16)
        ropeh = sb("ropeh", [128, 2, 1024], BF16)
        ropea = sb("ropea", [128, 2, 1024], BF16)
        colok = sb("colok", [128, 64], F32)
        CM2 = Y[:].bitcast(BF16).rearrange("p a b -> p (a b)")[:, 0:6656].rearrange("p (h x c) -> p h x c", x=26, c=64)
        qk1b = sb("qk1b", [128, 64], F32)
        lamt = sb("lamt", [128, 256], F32)
        lam = sb("lam", [128, 4], F32)
        kvwb = sb("kvwb", [128, 128], F32)
        epsD = sb("epsD", [128, 1], F32)
        CKVN = sb("CKVN*", [128, 1280], BF16)
        KPE = sb("KPE*", [128, 1280], BF16)
        WK96 = sb("WK96", [128, 2, 8, 96], BF16)
        STG = sb("STG", [128, 2, 512], F32)
        SMALL = sb("SMALL", [128, 8], F32)

        def A(eng, fn, r=(), w=(), dsem=None):
            return P.op(eng, fn, reads=r, writes=w, dsem=dsem)

        def MM(out, lhsT, rhs, st, sp_, r, w):
            A("pe", lambda h: h.matmul(out, lhsT=lhsT, rhs=rhs, start=st, stop=sp_), r, w)

        def ACT(out, in_, func, r, w, scale=None, bias=None):
            kw = {}
            if scale is not None:
                kw["scale"] = scale
            if bias is not None:
                kw["bias"] = bias
            A("act", lambda h: h.activation(out=out, in_=in_, func=func, **kw), r, w)

        def TT(out, in0, in1, op, r, w, eng="dve"):
            A(eng, lambda h: h.tensor_tensor(out=out, in0=in0, in1=in1, op=op), r, w)

        def STT(out, in0, scalar, in1, op0, op1, r, w):
            A("dve", lambda h: h.scalar_tensor_tensor(out=out, in0=in0, scalar=scalar, in1=in1, op0=op0, op1=op1), r, w)

        def TS(out, in0, s1, s2, op0, op1, r, w):
            A("dve", lambda h: h.tensor_scalar(out=out, in0=in0, scalar1=s1, scalar2=s2, op0=op0, op1=op1), r, w)

        def CP(eng, out, in_, r, w):
            if eng == "act":
                ACT(out, in_, AF.Copy, r, w)
            else:
                A(eng, lambda h: h.tensor_copy(out=out, in_=in_), r, w)

        def MS(eng, ap, val, w):
            A(eng, lambda h: h.memset(ap, val), (), w)

        uniq = [0]

        def DMA(eng, out, in_, r, w, dsem):
            if dsem in ("c0", "c1"):
                uniq[0] += 1
                dsem = f"c{uniq[0] + 10}"
            A(eng, lambda h: h.dma_start(out=out, in_=in_), r, w, dsem=dsem)

        wctr = [0]

        def wload(src, kc, ncols):
            i = wctr[0] % NW
            wctr[0] += 1
            key = f"WB{i}"
            dst = sb_ap(WB[i], 0, 128, 0, [[ncols, kc], [1, ncols]])
            DMA("pool", dst, src, (), [key], dsem=key)
            t = WB[i]
            return (lambda k, c0, c1: t[:, k * ncols + c0: k * ncols + c1]), key

        DMA("sp", ident[:], d_ident, (), ["ident"], "c0")
        DMA("sp", cvT[:], d_cvT, (), ["cvT"], "c0")
        DMA("sp", vecs[:], d_vecs, (), ["vecs"], "c0")
        DMA("sp", colok[:], d_colok, (), ["colok"], "c0")
        DMA("sp", lamt[:], d_lam, (), ["lamt"], "c0")
        DMA("sp", kvwb[:], d_kvwb, (), ["kvwb"], "c0")
        DMA("pool", ropeh[:], d_ropeh, (), ["ropeh"], "c1")
        DMA("pool", KPE[0:16, 0:1024], d_aug[16:32, :], (), ["AUG"], "c1")
        DMA("pool", KPE[32:48, 0:1024], d_aug[0:16, :], (), ["AUG"], "c1")
        DMA("pool", ropea[:], d_ropea, (), ["ropea"], "c1")
        MS("dve", ones[:], 1.0, ["ones"])
        MS("dve", bd[:], 0.0, ["bd"])
        MS("dve", bd[0:64, 0:64], 1.0, ["bd"])
        MS("dve", bd[64:128, 64:128], 1.0, ["bd"])
        MS("dve", epsD[:], EPS, ["epsD"])
        CP("dve", identb[:], ident[:], ["ident"], ["identb"])
        MS("dve", WK96[:], 0.0, ["WK96"])
        DMA("sp", qk1b[:], d_qk1b, (), ["qk1b"], "c0")
        lam_init = 0.8 - 0.6 * math.exp(-0.3 * 0)
        TT(lamt[:, 0:64], lamt[:, 0:64], lamt[:, 64:128], ALU.mult, ["lamt"], ["lamt"])
        TT(lamt[:, 128:192], lamt[:, 128:192], lamt[:, 192:256], ALU.mult, ["lamt"], ["lamt"])
        A("dve", lambda h: h.reduce_sum(out=lam[:, 0:1], in_=lamt[:, 0:64], axis=mybir.AxisListType.X), ["lamt"], ["lam"])
        A("dve", lambda h: h.reduce_sum(out=lam[:, 1:2], in_=lamt[:, 128:192], axis=mybir.AxisListType.X), ["lamt"], ["lam"])
        ACT(lam[:, 0:2], lam[:, 0:2], AF.Exp, ["lam"], ["lam"])
        TT(lam[:, 2:3], lam[:, 1:2], lam[:, 0:1], ALU.subtract, ["lam"], ["lam"])
        TS(lam[:, 3:4], lam[:, 2:3], -lam_init, None, ALU.add, ALU.bypass, ["lam"], ["lam"])
        TS(SMALL[:, 0:1], vecs[:, VO["sub"]:VO["sub"] + 1], 1.0 - lam_init, None, ALU.mult, ALU.bypass, ["vecs"], ["SMALL"])

        import os
        ACT(csT[:], cvT[:], AF.Silu, ["cvT"], ["csT"])
        MT_SRC = {0: (8, 0, "g"), 1: (0, None, "c"), 2: (16, 1, "m"), 3: (32, 2, "g"), 4: (24, None, "c"), 5: (40, 3, "m")}

        def mod_part(l, i):
            c0, nwi, kind = MT_SRC[i]
            pm, pmk = sbank()
            for pc in range(2):
                src = w_mod[l].rearrange("(kc p) n -> p kc n", p=128)[:, :, c0 * 128 + pc * 512:c0 * 128 + (pc + 1) * 512]
                W, wk = wload(src, 8, 512)
                for jj in range(4):
                    j = pc * 4 + jj
                    for k in range(8):
                        MM(pm[:, 2 * j:2 * j + 2], W(k, jj * 128, jj * 128 + 128), csT[:, 2 * k:2 * k + 2],
                           k == 0, k == 7, [wk, "csT"], [pmk])
            bmb = sb_ap(vecs, 0, 128, VO["bmod"] + 48 * l + c0, [[1, 8], [0, 2]])
            mk_ = f"modT{l}_{i}"
            mod_v = modT[:, l, c0 * 2:(c0 + 8) * 2].rearrange("p (a b) -> p a b", b=2)
            TT(mod_v, pm[:, 0:16].rearrange("p (a b) -> p a b", b=2), bmb, ALU.add, [pmk, "vecs"], [mk_])
            mt_v = MT[:, l, i, :].rearrange("p (a b) -> p a b", b=2)
            tk_ = f"MT{l}_{i}"
            if kind == "c":
                CP("dve", mt_v, mod_v, [mk_], [tk_])
            else:
                nwb = sb_ap(vecs, 0, 128, VO["nw"] + (l * 4 + nwi) * 8, [[1, 8], [0, 2]])
                if kind == "g":
                    STT(mt_v, mod_v, 1.0, nwb, ALU.add, ALU.mult, [mk_, "vecs"], [tk_])
                else:
                    TT(mt_v, mod_v, nwb, ALU.mult, [mk_, "vecs"], [tk_])

        def mtc(l, i, k, col):
            return MT[:, l, i, 2 * k + col:2 * k + col + 1]

        def rstd_from(ssb, ssk, inv_n, out, outk, nrow=128, n=512):
            ACT(out[0:nrow, 0:n], ssb[0:nrow, 0:n], AF.Ln, [ssk, "epsD"], [outk], scale=inv_n, bias=epsD[0:nrow, :])
            ACT(out[0:nrow, 0:n], out[0:nrow, 0:n], AF.Exp, [outk], [outk], scale=-0.5)

        def norm_mod(l, col, gi, si, t):
            xs = X[:, :, t * 512:(t + 1) * 512]
            ACT(SQ[:], xs, AF.Square, [f"X{t}"], ["SQ*"])
            sb_, sk = sbank()
            for k in range(8):
                MM(sb_[:, :], ones[:], SQ[:, k, :], k == 0, k == 7, ["ones", "SQ*"], [sk])
            rstd_from(sb_, sk, 1.0 / 1024, RS, "RS")
            rb = [(R1, "R1"), (R2, "R2"), (R3, "R3"), (R4, "R4")]
            for k in range(8):
                tb_, tk_ = rb[k % 4]
                STT(tb_[:, :], X[:, k, t * 512:(t + 1) * 512], mtc(l, gi, k, col), RS[:, :], ALU.mult, ALU.mult,
                    [f"X{t}", "RS", f"MT{l}_{gi}"], [tk_])
                ACT(U[:, k, t * 512:(t + 1) * 512], tb_[:, :], AF.Identity, [tk_, f"MT{l}_{si}"], [f"U{t}"], bias=mtc(l, si, k, col))

        def post_res(l, col, gwi, t, yk="Y*"):
            sb_, sk = sbank()
            for k in range(8):
                MM(sb_[:, :], ones[:], SQ[:, k, :], k == 0, k == 7, ["ones", "SQ*"], [sk])
            rstd_from(sb_, sk, 1.0 / 1024, RS, "RS")
            TT(Y[:], Y[:], sb_ap(RS, 0, 128, 0, [[0, 8], [1, 512]]), ALU.mult, ["Y*", "RS"], ["Y*"])
            for k in range(8):
                xs = X[:, k, t * 512:(t + 1) * 512]
                STT(xs, Y[:, k, :], mtc(l, gwi, k, col), xs, ALU.mult, ALU.add, ["Y*", f"MT{l}_{gwi}", f"X{t}"], [f"X{t}"])

        def proj_fm(W, wk, cols, t, evac, ukey=None, src=None, nk=8):
            for i, (c0, c1) in enumerate(cols):
                b_, bk = sbank()
                for k in range(nk):
                    rhs = U[:, k, t * 512:(t + 1) * 512] if src is None else src(k)
                    MM(b_[0:c1 - c0, :], W(k, c0, c1), rhs, k == 0, k == nk - 1, [wk, ukey or f"U{t}"], [bk])
                evac(i, b_, bk)

        def head_rms(b_, bk, lhs, inv_n, n=512):
            ACT(PTS[:, 0:n], b_[:, 0:n], AF.Square, [bk], ["PTS"])
            s2, s2k = sbank()
            MM(s2[:, 0:n], lhs, PTS[:, 0:n], True, True, ["ones", "bd", "PTS"], [s2k])
            rstd_from(s2, s2k, inv_n, RS, "RS", n=n)

        def rope_comb(out, okey, b_, bk, bs_, bsk, tab, tcols, p0, p1, w=None, ws=None, n=512):
            c = tab[p0:p1, 0, tcols[0]:tcols[1]]
            s = tab[p0:p1, 1, tcols[0]:tcols[1]]
            if w is None:
                TT(R1[p0:p1, 0:n], b_[p0:p1, 0:n], c, ALU.mult, [bk, "ropeh", "ropea"], ["R1"])
                TT(R2[p0:p1, 0:n], bs_[p0:p1, 0:n], s, ALU.mult, [bsk, "ropeh", "ropea"], ["R2"])
            else:
                STT(R1[p0:p1, 0:n], b_[p0:p1, 0:n], w, c, ALU.mult, ALU.mult, [bk, "ropeh", "vecs"], ["R1"])
                STT(R2[p0:p1, 0:n], bs_[p0:p1, 0:n], ws, s, ALU.mult, ALU.mult, [bsk, "ropeh", "vecs"], ["R2"])
            return R1, R2

        deferred = []
        gcount = [0]
        pending_side = []
        early_side = []
        early_done = set()

        def attend(nh, qf, kf, vf, dv, scale, qsegs, ep, cmf=None, tag="", fused=True, side=None, side_every=8):
            base_phase = P.phase.split("/")[0]
            P.phase = base_phase + "/att" + tag
            items = []
            for h in range(nh):
                for (q0, q1, kbs) in qsegs:
                    for i, kb in enumerate(kbs):
                        items.append((h, q0, q1, kb, i == 0, i == len(kbs) - 1))
            LA = 4
            acc = {}
            pts = {}

            def emit_s(j):
                h, q0, q1, kb, first, last = items[j]
                n = q1 - q0
                s_, sk = sbank()
                qa, qk_ = qf(h, q0, q1)
                ka, kk_ = kf(h, kb)
                MM(s_[:, 0:n], ka, qa, True, True, [qk_, kk_], [sk])
                pt = PT[j % 6]
                ptk = f"PT{j % 6}"
                ACT(pt[:, 0:n], s_[:, 0:n], AF.Exp, [sk], [ptk], scale=scale)
                if cmf is not None:
                    cm = cmf(h, q0, kb)
                    if cm is not None:
                        TT(pt[:, 0:n], pt[:, 0:n], cm, ALU.mult, [ptk, "Y*", "SQ*", "WSW"], [ptk])
                pts[j] = (pt, ptk)

            def emit_pv(j):
                h, q0, q1, kb, first, last = items[j]
                n = q1 - q0
                if first:
                    acc[(h, q0)] = (abank(), (None, None) if fused else abank())
                (num, numk), (den, denk) = acc[(h, q0)]
                pt, ptk = pts.pop(j)
                va, vk_ = vf(h, kb)
                MM(num[:, 0:n], va, pt[:, 0:n], first, last, [vk_, ptk], [numk])
                if not fused:
                    MM(den[:, 0:n], ones[:], pt[:, 0:n], first, last, ["ones", ptk], [denk])
                if last:
                    ep(h, q0, q1, num, numk, den, denk)
                    del acc[(h, q0)]

            GRP = 2
            LA = 4
            rr["ns"] = 4
            gi_ = 0
            deferred.clear()
            for j0 in range(0, len(items) + LA, GRP):
                gi_ += 1
                gcount[0] = gi_
                while deferred and deferred[0][0] <= gi_:
                    deferred.pop(0)[1]()
                if side and gi_ % side_every == 0:
                    ph_ = P.phase
                    P.phase = base_phase + "/side"
                    side.pop(0)()
                    P.phase = ph_
                for j in range(j0, j0 + GRP):
                    if 0 <= j - LA < len(items):
                        emit_pv(j - LA)
                for j in range(j0, j0 + GRP):
                    if j < len(items):
                        emit_s(j)
            while deferred:
                deferred.pop(0)[1]()
            rr["ns"] = 4
            P.phase = base_phase

        def ep_std(h, q0, q1, num, numk, den, denk, chunk0=0):
            n = q1 - q0
            pb = 64 * (h % 2)
            sl = h % 2
            ACT(RD[0:64, sl, 0:n], num[64:128, 0:n], AF.Ln, [numk], [f"RD{sl}"])
            ACT(RD[0:64, sl, 0:n], RD[0:64, sl, 0:n], AF.Exp, [f"RD{sl}"], [f"RD{sl}"], scale=-1.0)
            TT(OT[pb:pb + 64, chunk0 + h // 2, q0:q1], num[0:64, 0:n], RD[0:64, sl, 0:n], ALU.mult,
               [numk, f"RD{sl}"], [f"OT.{chunk0 + h // 2}.{pb}.{q0}"])

        def load_x(g):
            xd = xin[g]
            for t in range(2):
                stg = Y[:].rearrange("p a b -> p (a b)")
                DMA("sp", Y[:].rearrange("p a b -> p (a b)").rearrange("p (k f) -> p k f", f=1024),
                    xd[t * 512:(t + 1) * 512, :].rearrange("(k p) f -> p k f", p=128), (), ["Y*"], "xl")
                for k in range(8):
                    b_, bk = sbank()
                    for blk in range(4):
                        A("pe", lambda h, b_=b_, blk=blk, k=k: h.transpose(
                            b_[:, blk * 128:(blk + 1) * 128], stg[:, blk * 1024 + k * 128: blk * 1024 + (k + 1) * 128],
                            ident[:]), ["Y*", "ident"], [bk])
                    CP("act" if k % 2 else "dve", X[:, k, t * 512:(t + 1) * 512], b_[:, :], [bk], [f"X{t}"])

        def store_y(g):
            yd = yout[g]
            stg = Y[:].rearrange("p a b -> p (a b)")
            for t in range(2):
                for blk in range(4):
                    for half in range(2):
                        b_, bk = sbank()
                        for kk in range(4):
                            k = half * 4 + kk
                            A("pe", lambda h, b_=b_, kk=kk, k=k, blk=blk, t=t: h.transpose(
                                b_[:, kk * 128:(kk + 1) * 128], X[:, k, t * 512 + blk * 128: t * 512 + (blk + 1) * 128],
                                ident[:]), [f"X{t}", "ident"], [bk])
                        CP("act" if half else "dve", stg[:, blk * 1024 + half * 512: blk * 1024 + (half + 1) * 512],
                           b_[:, :], [bk], ["Y*"])
                DMA("sp", yd[t * 512:(t + 1) * 512, :].rearrange("(k p) f -> p k f", p=128),
                    Y[:].rearrange("p a b -> p (a b)").rearrange("p (k f) -> p k f", f=1024), ["Y*"], (), "yo")

        def wout_phase(l, col, t):
            wo = w_outw[l].rearrange("(kc p) n -> p kc n", p=128)
            for half in range(2):
                W, wk = wload(wo[:, :, half * 512:(half + 1) * 512], 8, 512)
                for mm in range(4):
                    m = half * 4 + mm
                    b_, bk = sbank()
                    for k in range(8):
                        MM(b_[:, :], W(k, mm * 128, mm * 128 + 128), OT[:, k, t * 512:(t + 1) * 512], k == 0, k == 7,
                           [wk, "OT*"], [bk])
                    ACT(SQ[:, m, :], b_[:, :], AF.Square, [bk], [f"SQ.{m}"])
                    CP("dve", Y[:, m, :], b_[:, :], [bk], [f"Y.{m}"])
            post_res(l, col, 2, t)

        def ffn_phase_dve(l, g, col):
            wu = w_up[l].rearrange("(kc p) n -> p kc n", p=128)
            wd = w_down[l].rearrange("(j p) n -> p j n", p=128)
            cw = lambda tap, j: vecs[:, VO["cw"] + (l * 3 + tap) * 44 + j: VO["cw"] + (l * 3 + tap) * 44 + j + 1]
            cb = lambda j: vecs[:, VO["cb"] + l * 44 + j: VO["cb"] + l * 44 + j + 1]
            segs = [(0, 256), (256, 512)] if g == "P" else [(0, 512)]
            H = ATT[:, 0:11264].rearrange("p (j n) -> p j n", n=512)
            for t in range(2):
                for pc in range(11):
                    Wg, wgk = wload(wu[:, :, pc * 256:(pc + 1) * 256], 8, 256)
                    Wv, wvk = wload(wu[:, :, 2816 + pc * 256:2816 + (pc + 1) * 256], 8, 256)
                    for jj in range(2):
                        j = pc * 2 + jj
                        zb = {}
                        for nm, Wx, wxk in (("g", Wg, wgk), ("v", Wv, wvk)):
                            b_, bk = abank()
                            for k in range(8):
                                MM(b_[:, :], Wx(k, jj * 128, jj * 128 + 128), U[:, k, t * 512:(t + 1) * 512], k == 0, k == 7,
                                   [wxk, f"U{t}"], [bk])
                            zb[nm] = (b_, bk)
                        if g == "S":
                            ot = 1 - t
                            hcol = ot * 512 + (0 if ot == 1 else 511)
                            for nm, Wx, wxk in (("g", Wg, wgk), ("v", Wv, wvk)):
                                b_, bk = zb[nm]
                                hb, hbk = sbank()
                                for k in range(8):
                                    MM(hb[:, 0:1], Wx(k, jj * 128, jj * 128 + 128), U[:, k, hcol:hcol + 1], k == 0, k == 7,
                                       [wxk, f"U{ot}"], [hbk])
                                zb[nm + "h"] = (hb, hbk)
                        for ci, nm in enumerate(("g", "v")):
                            b_, bk = zb[nm]
                            jc = j if nm == "g" else 22 + j
                            a_ = (R1 if nm == "g" else R2) if j % 2 == 0 else (R3 if nm == "g" else R4)
                            ak = ("R1" if nm == "g" else "R2") if j % 2 == 0 else ("R3" if nm == "g" else "R4")
                            ACT(a_[:, :], b_[:, :], AF.Identity, [bk, "vecs"], [ak], scale=cw(1, jc), bias=cb(jc))
                            for (a, b) in segs:
                                STT(a_[:, a + 1:b], b_[:, a:b - 1], cw(0, jc), a_[:, a + 1:b], ALU.mult, ALU.add,
                                    [bk, ak, "vecs"], [ak])
                                STT(a_[:, a:b - 1], b_[:, a + 1:b], cw(2, jc), a_[:, a:b - 1], ALU.mult, ALU.add,
                                    [bk, ak, "vecs"], [ak])
                            if g == "S":
                                hb, hbk = zb[nm + "h"]
                                if t == 0:
                                    STT(a_[:, 511:512], hb[:, 0:1], cw(2, jc), a_[:, 511:512], ALU.mult, ALU.add,
                                        [hbk, ak, "vecs"], [ak])
                                else:
                                    STT(a_[:, 0:1], hb[:, 0:1], cw(0, jc), a_[:, 0:1], ALU.mult, ALU.add,
                                        [hbk, ak, "vecs"], [ak])
                        ag, agk, av, avk = (R1, "R1", R2, "R2") if j % 2 == 0 else (R3, "R3", R4, "R4")
                        ACT(ag[:, :], ag[:, :], AF.Silu, [agk], [agk])
                        TT(H[:, j, :], ag[:, :], av[:, :], ALU.mult, [agk, avk], [f"ATT.{j}"])
                for mp in range(4):
                    accs = [abank() for _ in range(2)]
                    for jh in range(2):
                        W, wk = wload(wd[:, jh * 11:(jh + 1) * 11, mp * 256:(mp + 1) * 256], 11, 256)
                        for mm in range(2):
                            b_, bk = accs[mm]
                            for jj in range(11):
                                j = jh * 11 + jj
                                MM(b_[:, :], W(jj, mm * 128, mm * 128 + 128), H[:, j, :], j == 0, j == 21, [wk, "ATT*"], [bk])
                    for mm in range(2):
                        m = mp * 2 + mm
                        b_, bk = accs[mm]
                        ACT(Y[:, m, :], b_[:, :], AF.Copy, [bk], [f"Y.{m}"])
                        TT(SQ[:, m, :], Y[:, m, :], Y[:, m, :], ALU.mult, [f"Y.{m}"], [f"SQ.{m}"])
                post_res(l, col, 5, t)

        def ffn_phase(l, g, col):
            wu = w_up[l].rearrange("(kc p) n -> p kc n", p=128)
            wd = w_down[l].rearrange("(j p) n -> p j n", p=128)
            cw = lambda tap, j: vecs[:, VO["cw"] + (l * 3 + tap) * 44 + j: VO["cw"] + (l * 3 + tap) * 44 + j + 1]
            cb = lambda j: vecs[:, VO["cb"] + l * 44 + j: VO["cb"] + l * 44 + j + 1]
            nseg = 2 if g == "P" else 1
            sw_ = 512 // nseg
            pw_ = sw_ + 2
            H = ATT[:, 0:11264].rearrange("p (j n) -> p j n", n=512)
            zkeys = lambda nm, par: f"ZB{nm}{par}"

            def stage_a(t, j, Wg, wgk, Wv, wvk, jj):
                par = j % 2
                zb = {}
                for nm, Wx, wxk in (("g", Wg, wgk), ("v", Wv, wvk)):
                    b_, bk = abank()
                    for k in range(8):
                        MM(b_[:, :], Wx(k, jj * 128, jj * 128 + 128), U[:, k, t * 512:(t + 1) * 512], k == 0, k == 7,
                           [wxk, f"U{t}"], [bk])
                    zb[nm] = (b_, bk)
                hb, hbk = None, None
                if g == "S":
                    ot = 1 - t
                    hcol = ot * 512 + (0 if ot == 1 else 511)
                    hb, hbk = sbank()
                    for ci, (nm, Wx, wxk) in enumerate((("g", Wg, wgk), ("v", Wv, wvk))):
                        for k in range(8):
                            MM(hb[:, ci:ci + 1], Wx(k, jj * 128, jj * 128 + 128), U[:, k, hcol:hcol + 1], k == 0, k == 7,
                               [wxk, f"U{ot}"], [hbk])
                for ci, nm in enumerate(("g", "v")):
                    b_, bk = zb[nm]
                    jc = j if nm == "g" else 22 + j
                    a_, ak = ((R1, "R1") if nm == "g" else (R2, "R2")) if par == 0 else ((R3, "R3") if nm == "g" else (R4, "R4"))
                    ACT(a_[:, :], b_[:, :], AF.Identity, [bk, "vecs"], [ak], scale=cw(1, jc), bias=cb(jc))
                    z_ = ZB[(nm, par)]
                    zk = zkeys(nm, par)
                    zin = z_[:, 0:nseg * pw_].rearrange("p (s w) -> p s w", w=pw_)[:, :, 1:1 + sw_]
                    ACT(zin, b_[:, :].rearrange("p (s w) -> p s w", w=sw_), AF.Copy, [bk], [zk])
                    if g == "S":
                        pad = 513 if t == 0 else 0
                        CP("dve", z_[:, pad:pad + 1], hb[:, ci:ci + 1], [hbk], [zk])
                    for tap in (0, 2):
                        TS(DG[(nm, tap, par)], identb[:], cw(tap, jc), None, ALU.mult, ALU.bypass, ["identb", "vecs"], [f"DG{nm}{tap}{par}"])
                return (t, j, par)

            def stage_b(info):
                t, j, par = info
                for nm in ("g", "v"):
                    a_, ak = ((R1, "R1") if nm == "g" else (R2, "R2")) if par == 0 else ((R3, "R3") if nm == "g" else (R4, "R4"))
                    z_ = ZB[(nm, par)]
                    zk = zkeys(nm, par)
                    B_, Bk = sbank()
                    for sg in range(nseg):
                        o0 = sg * sw_
                        zo = sg * pw_
                        MM(B_[:, o0:o0 + sw_], DG[(nm, 0, par)], z_[:, zo:zo + sw_], True, False, [f"DG{nm}0{par}", zk], [Bk])
                        MM(B_[:, o0:o0 + sw_], DG[(nm, 2, par)], z_[:, zo + 2:zo + 2 + sw_], False, True, [f"DG{nm}2{par}", zk], [Bk])
                    TT(a_[:, :], B_[:, :], a_[:, :], ALU.add, [Bk, ak], [ak])
                ag, agk, av, avk = (R1, "R1", R2, "R2") if par == 0 else (R3, "R3", R4, "R4")
                ACT(ag[:, :], ag[:, :], AF.Silu, [agk], [agk])
                TT(H[:, j, :], ag[:, :], av[:, :], ALU.mult, [agk, avk], [f"ATT.{j}"])

            for t in range(2):
                for nm in "gv":
                    for par in range(2):
                        z_ = ZB[(nm, par)]
                        if g == "P":
                            MS("dve", z_[:, 0:516].rearrange("p (s w) -> p s w", w=258)[:, :, 0:258:257], 0.0, [zkeys(nm, par)])
                        else:
                            zp = 0 if t == 0 else 513
                            MS("dve", z_[:, zp:zp + 1], 0.0, [zkeys(nm, par)])
                pending = None
                for pc in range(11):
                    Wg, wgk = wload(wu[:, :, pc * 256:(pc + 1) * 256], 8, 256)
                    Wv, wvk = wload(wu[:, :, 2816 + pc * 256:2816 + (pc + 1) * 256], 8, 256)
                    for jj in range(2):
                        j = pc * 2 + jj
                        cur = stage_a(t, j, Wg, wgk, Wv, wvk, jj)
                        if pending is not None:
                            stage_b(pending)
                        pending = cur
                stage_b(pending)
                for mp in range(4):
                    accs = [abank() for _ in range(2)]
                    for jh in range(2):
                        W, wk = wload(wd[:, jh * 11:(jh + 1) * 11, mp * 256:(mp + 1) * 256], 11, 256)
                        for mm in range(2):
                            b_, bk = accs[mm]
                            for jj in range(11):
                                j = jh * 11 + jj
                                MM(b_[:, :], W(jj, mm * 128, mm * 128 + 128), H[:, j, :], j == 0, j == 21, [wk, "ATT*"], [bk])
                    for mm in range(2):
                        m = mp * 2 + mm
                        b_, bk = accs[mm]
                        ACT(Y[:, m, :], b_[:, :], AF.Copy, [bk], [f"Y.{m}"])
                        TT(SQ[:, m, :], Y[:, m, :], Y[:, m, :], ALU.mult, [f"Y.{m}"], [f"SQ.{m}"])
                    if early_side and t == 1:
                        early_side.pop(0)()
                post_res(l, col, 5, t)

        def load_ctx_T(dst_fn, src, ncols, dkey):
            stg = STG[:, 0, :]
            for blk in range(2):
                DMA("sp", STG[:, blk, 0:ncols], src[blk * 128:(blk + 1) * 128, :], (), ["STG0", "STG1"], "cx")
            for blk in range(2):
                for c0 in range(0, ncols, 128):
                    cn = min(128, ncols - c0)
                    b_, bk = sbank()
                    A("pe", lambda h, b_=b_, blk=blk, c0=c0, cn=cn: h.transpose(b_[0:cn, 0:128], STG[:, blk, c0:c0 + cn], ident[:]),
                      ["STG0", "STG1", "ident"], [bk])
                    CP("dve", dst_fn(c0 // 128, blk, cn), b_[0:cn, 0:128], [bk], [dkey])

        def swap64(Wx, wxk):
            for k in range(8):
                src = Wx(k, 0, 512).rearrange("p (g s d) -> p g s d", s=2, d=32)
                dst = WSW[:, k * 512:(k + 1) * 512].rearrange("p (g s d) -> p g s d", s=2, d=32)
                CP("dve", dst[:, :, 0, :], src[:, :, 1, :], [wxk], ["WSW"])
                CP("dve", dst[:, :, 1, :], src[:, :, 0, :], [wxk], ["WSW"])

        def mixer_even(g, col):
            l = 0
            rope = g == "S"
            nkb = 10 if g == "S" else 8
            wi = w_in[0].rearrange("(kc p) n -> p kc n", p=128)
            WA, wak = wload(wi[:, :, 0:416], 8, 416)
            for v in range(2 if rope else 1):
                for k in range(8):
                    if v == 0:
                        CP("dve", WK96[:, 0, k, 64:96], WA(k, 384, 416), [wak], ["WK96"])
                    else:
                        CP("dve", WK96[:, 1, k, 64:80], WA(k, 400, 416), [wak], ["WK96"])
                        CP("dve", WK96[:, 1, k, 80:96], WA(k, 384, 400), [wak], ["WK96"])
            CQ = ATT[:, 0:2048].rearrange("p (a n) -> p a n", n=1024)
            for t in range(2):
                def ev_ckv(i, b_, bk, t=t):
                    head_rms(b_, bk, ones[:], 1.0 / 128)
                    STT(CKVN[:, t * 512:(t + 1) * 512], b_[:, :], vecs[:, VO["kvw"]:VO["kvw"] + 1], RS[:, :], ALU.mult, ALU.mult,
                        [bk, "RS", "vecs"], [f"CKVN.{t}"])
                proj_fm(WA, wak, [(256, 384)], t, ev_ckv)
                b_, bk = sbank()
                for k in range(8):
                    MM(b_[0:96, :], WK96[:, 0, k, :], U[:, k, t * 512:(t + 1) * 512], k == 0, k == 7, ["WK96", f"U{t}"], [bk])
                if rope:
                    bs_, bsk = sbank()
                    for k in range(8):
                        MM(bs_[0:96, :], WK96[:, 1, k, :], U[:, k, t * 512:(t + 1) * 512], k == 0, k == 7, ["WK96", f"U{t}"], [bsk])
                    rope_comb(None, None, b_, bk, bs_, bsk, ropea, (t * 512, (t + 1) * 512), 64, 96)
                    TT(KPE[64:96, t * 512:(t + 1) * 512], R1[64:96, :], R2[64:96, :], ALU.add, ["R1", "R2"], ["KPE*"])
                else:
                    CP("dve", KPE[64:96, t * 512:(t + 1) * 512], b_[64:96, :], [bk], ["KPE*"])
                cqb = []
                for i in range(2):
                    b2, b2k = abank()
                    for k in range(8):
                        MM(b2[:, :], WA(k, i * 128, i * 128 + 128), U[:, k, t * 512:(t + 1) * 512], k == 0, k == 7, [wak, f"U{t}"], [b2k])
                    ACT(SQ[:, i, :], b2[:, :], AF.Square, [b2k], ["SQ*"])
                    cqb.append((b2, b2k))
                s2, s2k = sbank()
                for i in range(2):
                    MM(s2[:, :], ones[:], SQ[:, i, :], i == 0, i == 1, ["ones", "SQ*"], [s2k])
                rstd_from(s2, s2k, 1.0 / 256, RS, "RS")
                for i in range(2):
                    b2, b2k = cqb[i]
                    STT(CQ[:, i, t * 512:(t + 1) * 512], b2[:, :], vecs[:, VO["qnw"] + i:VO["qnw"] + i + 1], RS[:, :], ALU.mult, ALU.mult,
                        [b2k, "RS", "vecs"], [f"CQ.{i}{t}"])
            if g == "S":
                load_ctx_T(lambda c, blk, cn: CKVN[0:cn, 1024 + blk * 128:1024 + (blk + 1) * 128], c_ckv, 128, "CKVN*")
                for blk in range(2):
                    DMA("sp", STG[:, blk, 0:32], c_kpe[blk * 128:(blk + 1) * 128, :], (), ["STG0", "STG1"], "cx")
                for blk in range(2):
                    b_, bk = sbank()
                    A("pe", lambda h, b_=b_, blk=blk: h.transpose(b_[0:32, 0:128], STG[:, blk, 0:32], ident[:]), ["STG0", "STG1", "ident"], [bk])
                    CP("dve", KPE[64:96, 1024 + blk * 128:1024 + (blk + 1) * 128], b_[0:32, 0:128], [bk], ["KPE*"])
            kmix = int(os.environ.get("KMIX", "99"))
            if kmix <= 1:
                return
            if g == "P":
                state_out(0, [(256, 416), (928, 1440), (1440, 1952)], wi)
            if kmix <= 2:
                return
            Wq, wqk = wload(w_uq.rearrange("(kc p) n -> p kc n", p=128), 2, 768)
            if rope:
                for k in range(2):
                    for hh in range(8):
                        c = hh * 96
                        CP("dve", WSW[:, k * 768 + c + 64:k * 768 + c + 80], Wq(k, c + 80, c + 96), [wqk], ["WSW"])
                        CP("dve", WSW[:, k * 768 + c + 80:k * 768 + c + 96], Wq(k, c + 64, c + 80), [wqk], ["WSW"])
                        CP("dve", WSW[:, k * 768 + c:k * 768 + c + 64], Wq(k, c, c + 64), [wqk], ["WSW"])
            Wk_, wkk = wload(w_uk.rearrange("(kc p) n -> p kc n", p=128), 1, 512)
            Wv_, wvk = wload(w_uv.rearrange("(kc p) n -> p kc n", p=128), 1, 512)
            NK = 1280
            KA = ATT[:, 2048:2048 + 4 * NK].rearrange("p (h n) -> p h n", n=NK)
            VA = ATT[:, 7168:7168 + 10 * 512].rearrange("p (b c) -> p b c", c=512)
            QA = ATT[:, 12288:12288 + 2048].rearrange("p (h n) -> p h n", n=512)
            for hf in range(2):
                for kt in range(0, NK if g == "S" else 1024, 512):
                    n = min(512, (NK if g == "S" else 1024) - kt)
                    for hh in range(4):
                        hg = hf * 4 + hh
                        b_, bk = sbank()
                        MM(b_[0:64, 0:n], Wk_(0, hg * 64, hg * 64 + 64), CKVN[:, kt:kt + n], True, True, [wkk, "CKVN*"], [bk])
                        CP("act" if hh % 2 else "dve", KA[0:64, hh, kt:kt + n], b_[0:64, 0:n], [bk], [f"KA.{hh}n{kt}"])
                        CP("dve" if hh % 2 else "act", KA[64:96, hh, kt:kt + n], KPE[64:96, kt:kt + n], ["KPE*"], [f"KA.{hh}r{kt}"])
                MS("dve", VA[:, :, :].rearrange("p b (h c) -> p b h c", c=128)[:, :, :, 64:128], 1.0, ["VA*"])
                for kb in range(nkb):
                    b_, bk = sbank()
                    MM(b_[:, 0:256], CKVN[:, kb * 128:(kb + 1) * 128], Wv_(0, hf * 256, hf * 256 + 256), True, True, [wvk, "CKVN*"], [bk])
                    CP("act" if kb % 2 else "dve", VA[:, kb, :].rearrange("p (h c) -> p h c", c=128)[:, :, 0:64],
                       b_[:, 0:256].rearrange("p (h c) -> p h c", c=64), [bk], [f"VA.{kb}"])
                for t in range(2):
                    for hh in range(4):
                        hg = hf * 4 + hh
                        b_, bk = sbank()
                        for k in range(2):
                            MM(b_[0:96, :], Wq(k, hg * 96, hg * 96 + 96), CQ[:, k, t * 512:(t + 1) * 512], k == 0, k == 1, [wqk, "CQ*"], [bk])
                        CP("act", QA[0:64, hh, :], b_[0:64, :], [bk], [f"QA.{hh}n"])
                        if rope:
                            bs_, bsk = sbank()
                            for k in range(2):
                                MM(bs_[0:96, :], WSW[:, k * 768 + hg * 96:k * 768 + hg * 96 + 96], CQ[:, k, t * 512:(t + 1) * 512],
                                   k == 0, k == 1, ["WSW", "CQ*"], [bsk])
                            rope_comb(None, None, b_, bk, bs_, bsk, ropea, (t * 512, (t + 1) * 512), 64, 96)
                            TT(QA[64:96, hh, :], R1[64:96, :], R2[64:96, :], ALU.add, ["R1", "R2"], [f"QA.{hh}r"])
                        else:
                            CP("dve", QA[64:96, hh, :], b_[64:96, :], [bk], [f"QA.{hh}r"])
                    if g == "S":
                        qsegs = [(0, 512, list(range(10)))]
                    else:
                        qsegs = [(0, 256, [4 * t, 4 * t + 1]), (256, 512, [4 * t + 2, 4 * t + 3])]
                    attend(4, lambda h, q0, q1: (QA[0:96, h, q0:q1], "QA*"),
                           lambda h, kb: (KA[0:96, h, kb * 128:(kb + 1) * 128], "KA*"),
                           lambda h, kb: (VA[:, kb, h * 128:(h + 1) * 128], "VA*"),
                           64, 96 ** -0.5, qsegs,
                           lambda h, q0, q1, num, numk, den, denk, t=t, hf=hf: ep_std(h + 0, t * 512 + q0, t * 512 + q1, num, numk, den, denk,
                                                                                       chunk0=hf * 2))
            if kmix <= 3:
                return
            NKd = NK if g == "S" else 1024
            QD = ATT[:, 0:2048].rearrange("p (h n) -> p h n", n=512)
            KD = ATT[:, 2048:2048 + 4 * NK].rearrange("p (h n) -> p h n", n=NK)
            VD = ATT[:, 7168:7168 + 10 * 512].rearrange("p (b c) -> p b c", c=512)
            WQd, wqdk = wload(wi[:, :, 416:928], 8, 512)
            WKd, wkdk = wload(wi[:, :, 928:1440], 8, 512)
            WVd, wvdk = wload(wi[:, :, 1440:1952], 8, 512)

            def proj_rope(Wx, wxk, dst_fn, dkey):
                if rope:
                    swap64(Wx, wxk)
                for t in range(2):
                    for hh in range(4):
                        b_, bk = sbank()
                        for k in range(8):
                            MM(b_[:, :], Wx(k, hh * 128, hh * 128 + 128), U[:, k, t * 512:(t + 1) * 512], k == 0, k == 7, [wxk, f"U{t}"], [bk])
                        if rope:
                            bs_, bsk = sbank()
                            for k in range(8):
                                MM(bs_[:, :], WSW[:, k * 512 + hh * 128:k * 512 + hh * 128 + 128], U[:, k, t * 512:(t + 1) * 512],
                                   k == 0, k == 7, ["WSW", f"U{t}"], [bsk])
                            rope_comb(None, None, b_, bk, bs_, bsk, ropeh, (t * 512, (t + 1) * 512), 0, 128)
                            TT(dst_fn(hh, t), R1[:, :], R2[:, :], ALU.add, ["R1", "R2"], [dkey])
                        else:
                            CP("act", dst_fn(hh, t), b_[:, :], [bk], [dkey])

            proj_rope(WKd, wkdk, lambda hh, t: KD[:, hh, t * 512:(t + 1) * 512], "KA*")
            for tb in range(8):
                b_, bk = sbank()
                for k in range(8):
                    MM(b_[:, :], U[:, k, tb * 128:(tb + 1) * 128], WVd(k, 0, 512), k == 0, k == 7, [wvdk, f"U{tb // 4}"], [bk])
                CP("act", VD[:, tb, :], b_[:, :], [bk], ["VA*"])
            if g == "S":
                load_ctx_T(lambda c, blk, cn: KD[:, c, 1024 + blk * 128:1024 + (blk + 1) * 128], c_dk, 512, "KA*")
                for blk in range(2):
                    DMA("pool", VD[:, 8 + blk, :], c_dv[blk * 128:(blk + 1) * 128, :], (), [f"VA.{8 + blk}"], f"cv{blk}")
            for t in range(2):
                def qdst(hh, tt):
                    return QD[:, hh, :]
                if rope and t == 0:
                    swap64(WQd, wqdk)
                for hh in range(4):
                    b_, bk = sbank()
                    for k in range(8):
                        MM(b_[:, :], WQd(k, hh * 128, hh * 128 + 128), U[:, k, t * 512:(t + 1) * 512], k == 0, k == 7, [wqdk, f"U{t}"], [bk])
                    if rope:
                        bs_, bsk = sbank()
                        for k in range(8):
                            MM(bs_[:, :], WSW[:, k * 512 + hh * 128:k * 512 + hh * 128 + 128], U[:, k, t * 512:(t + 1) * 512],
                               k == 0, k == 7, ["WSW", f"U{t}"], [bsk])
                        rope_comb(None, None, b_, bk, bs_, bsk, ropeh, (t * 512, (t + 1) * 512), 0, 128)
                        TT(QD[:, hh, :], R1[:, :], R2[:, :], ALU.add, ["R1", "R2"], ["QA*"])
                    else:
                        CP("act", QD[:, hh, :], b_[:, :], [bk], ["QA*"])
                if g == "S":
                    qsegs = [(0, 512, list(range(10)))]
                else:
                    qsegs = [(0, 256, [4 * t, 4 * t + 1]), (256, 512, [4 * t + 2, 4 * t + 3])]
                hold = {}

                def ep_diff(h8, q0, q1, num, numk, den, denk, t=t):
                    h, c = h8 // 2, h8 % 2
                    n = q1 - q0
                    ACT(RD[:, c, 0:n], den[:, 0:n], AF.Ln, [denk], [f"RD{c}"])
                    ACT(RD[:, c, 0:n], RD[:, c, 0:n], AF.Exp, [f"RD{c}"], [f"RD{c}"], scale=-1.0)
                    if c == 0:
                        TT(R1[:, q0:q1], num[:, 0:n], RD[:, 0, 0:n], ALU.mult, [numk, "RD0"], ["R1"])
                    else:
                        TT(R2[:, q0:q1], num[:, 0:n], RD[:, 1, 0:n], ALU.mult, [numk, "RD1"], ["R2"])
                        STT(R1[:, q0:q1], R2[:, q0:q1], lam[:, 3:4], R1[:, q0:q1], ALU.mult, ALU.add, ["R1", "R2", "lam"], ["R1"])
                        ob, obk = (R3, "R3") if (h + q0 // 256) % 2 == 0 else (R4, "R4")
                        CP("dve", ob[:, 0:n], R1[:, q0:q1], ["R1"], [obk])
                        ACT(SQ[:, (h + q0 // 256) % 2, 0:n], R1[:, q0:q1], AF.Square, ["R1"], [f"SQ.d{(h + q0 // 256) % 2}"])

                        def tail(h=h, q0=q0, q1=q1, n=n, ob=ob, obk=obk, t=t):
                            sl_ = (h + q0 // 256) % 2
                            s2, s2k = sbank()
                            MM(s2[:, 0:n], ones[:], SQ[:, sl_, 0:n], True, True, ["ones", f"SQ.d{sl_}"], [s2k])
                            rstd_from(s2, s2k, 1.0 / 128, RS, "RS", n=n)
                            STT(OT[:, 4 + h, t * 512 + q0:t * 512 + q1], ob[:, 0:n], SMALL[:, 0:1], RS[:, 0:n], ALU.mult, ALU.mult,
                                [obk, "RS", "SMALL"], [f"OT.{4 + h}.{t}.{q0}"])
                        deferred.append((gcount[0] + 2, tail))

                attend(8, lambda h8, q0, q1: (QD[64 * (h8 % 2):64 * (h8 % 2) + 64, h8 // 2, q0:q1], "QA*"),
                       lambda h8, kb: (KD[64 * (h8 % 2):64 * (h8 % 2) + 64, h8 // 2, kb * 128:(kb + 1) * 128], "KA*"),
                       lambda h8, kb: (VD[:, kb, (h8 // 2) * 128:(h8 // 2) * 128 + 128], "VA*"),
                       128, 64 ** -0.5, qsegs, ep_diff, fused=False)

        def state_out(l, colsets, wi):
            for ci, (c0, c1) in enumerate(colsets):
                pieces = [(a, min(a + 512, c1)) for a in range(c0, c1, 512)]
                for (a, b) in pieces:
                    w_ = b - a
                    W, wk = wload(wi[:, :, a:b], 8, w_)
                    off = sum(x1 - x0 for x0, x1 in colsets[:ci]) + (a - c0)
                    for tb in range(8):
                        b_, bk = sbank()
                        for k in range(8):
                            MM(b_[:, 0:w_], U[:, k, tb * 128:(tb + 1) * 128], W(k, 0, w_), k == 0, k == 7, [wk, f"U{tb // 4}"], [bk])
                        sl = tb % 2
                        sk_ = f"STG{sl}"
                        if (l == 0 and a == 256) or (l == 1 and a == 512):
                            nh_, hw_, wt = (1, 128, kvwb) if l == 0 else (2, 64, qk1b)
                            for hh in range(nh_):
                                A("act", lambda h, b_=b_, hh=hh, hw_=hw_: h.activation(
                                    out=PTS[:, 0:hw_], in_=b_[:, hh * hw_:(hh + 1) * hw_], func=AF.Square, accum_out=SMALL[:, 1:2]),
                                  [bk], ["PTS", "SMALL"])
                                ACT(SMALL[:, 2:3], SMALL[:, 1:2], AF.Ln, ["SMALL", "epsD"], ["SMALL"], scale=1.0 / hw_, bias=epsD[:, :])
                                ACT(SMALL[:, 2:3], SMALL[:, 2:3], AF.Exp, ["SMALL"], ["SMALL"], scale=-0.5)
                                STT(STG[:, sl, hh * hw_:(hh + 1) * hw_], b_[:, hh * hw_:(hh + 1) * hw_], SMALL[:, 2:3], wt[:, 0:hw_],
                                    ALU.mult, ALU.mult, [bk, "SMALL", "kvwb", "qk1b"], [sk_])
                            CP("dve", STG[:, sl, 128:w_], b_[:, 128:w_], [bk], [sk_])
                        else:
                            CP("act" if tb % 2 else "dve", STG[:, sl, 0:w_], b_[:, 0:w_], [bk], [sk_])
                        DMA("sp", st_out[l][tb * 128:(tb + 1) * 128, off:off + w_], STG[:, sl, 0:w_], [sk_], (), f"so{sl}")
                    if pending_side:
                        ph_ = P.phase
                        P.phase = ph_.split("/")[0] + "/mod"
                        pending_side.pop(0)()
                        P.phase = ph_

        def mixer_odd(g, col):
            l = 1
            rope = g == "S"
            NK = 1280
            nkeys = NK if g == "S" else 1024
            nkb = 10 if g == "S" else 8
            wi = w_in[1].rearrange("(kc p) n -> p kc n", p=128)
            qw = lambda i: vecs[:, VO["qkw"] + i:VO["qkw"] + i + 1]
            qws = lambda i: vecs[:, VO["qkws"] + i:VO["qkws"] + i + 1]
            if g == "P":
                state_out(1, [(512, 768), (1280, 1792), (1792, 2304)], wi)
            QC = ATT[:, 0:2048].rearrange("p (h n) -> p h n", n=512)
            KC = ATT[:, 2048:2048 + 2 * NK].rearrange("p (h n) -> p h n", n=NK)
            VC = ATT[:, 4608:4608 + 10 * 256].rearrange("p (b c) -> p b c", c=256)
            WQ, wqk = wload(wi[:, :, 0:512], 8, 512)
            WKV, wkvk = wload(wi[:, :, 512:768], 8, 256)
            WD = WSW[:, 0:2048].rearrange("p (k c) -> p k c", c=256)
            WDS = WSW[:, 2048:4096].rearrange("p (k c) -> p k c", c=256)
            for k in range(8):
                for kv in range(2):
                    for dup in range(2):
                        c = kv * 128 + dup * 64
                        CP("dve", WD[:, k, c:c + 64], WKV(k, kv * 64, kv * 64 + 64), [wkvk], ["WSW"])
                        if rope:
                            CP("dve", WDS[:, k, c:c + 32], WKV(k, kv * 64 + 32, kv * 64 + 64), [wkvk], ["WSW"])
                            CP("dve", WDS[:, k, c + 32:c + 64], WKV(k, kv * 64, kv * 64 + 32), [wkvk], ["WSW"])

            def qk_norm_rope(dst, dkey, b_, bk, bs_, bsk, wi_, t):
                head_rms(b_, bk, bd[:], 1.0 / 64)
                if rope:
                    rope_comb(None, None, b_, bk, bs_, bsk, ropeh, (t * 512, (t + 1) * 512), 0, 128, w=qw(wi_), ws=qws(wi_))
                    TT(R1[:, :], R1[:, :], R2[:, :], ALU.add, ["R1", "R2"], ["R1"])
                    TT(dst, R1[:, :], RS[:, :], ALU.mult, ["R1", "RS"], [dkey])
                else:
                    STT(dst, b_[:, :], qw(wi_), RS[:, :], ALU.mult, ALU.mult, [bk, "RS", "vecs"], [dkey])

            for t in range(2):
                for kv in range(2):
                    b_, bk = sbank()
                    for k in range(8):
                        MM(b_[:, :], WD[:, k, kv * 128:(kv + 1) * 128], U[:, k, t * 512:(t + 1) * 512], k == 0, k == 7, ["WSW", f"U{t}"], [bk])
                    bs_, bsk = None, None
                    if rope:
                        bs_, bsk = sbank()
                        for k in range(8):
                            MM(bs_[:, :], WDS[:, k, kv * 128:(kv + 1) * 128], U[:, k, t * 512:(t + 1) * 512], k == 0, k == 7, ["WSW", f"U{t}"], [bsk])
                    qk_norm_rope(KC[:, kv, t * 512:(t + 1) * 512], "KA*", b_, bk, bs_, bsk, 1, t)
            MS("dve", VC[:, :, :].rearrange("p b (h c) -> p b h c", c=128)[:, :, :, 64:128], 1.0, ["VA*"])
            for tb in range(8):
                b_, bk = sbank()
                for k in range(8):
                    MM(b_[:, 0:128], U[:, k, tb * 128:(tb + 1) * 128], WKV(k, 128, 256), k == 0, k == 7, [wkvk, f"U{tb // 4}"], [bk])
                CP("act" if tb % 2 else "dve", VC[:, tb, :].rearrange("p (h c) -> p h c", c=128)[:, :, 0:64],
                   b_[:, 0:128].rearrange("p (h c) -> p h c", c=64), [bk], [f"VA.{tb}"])
            if g == "S":
                for blk in range(2):
                    DMA("sp", STG[:, blk, 0:128], c_gk[blk * 128:(blk + 1) * 128, :], (), ["STG0", "STG1"], "cx")
                for blk in range(2):
                    for kv in range(2):
                        b_, bk = sbank()
                        A("pe", lambda h, b_=b_, blk=blk, kv=kv: h.transpose(b_[0:64, 0:128], STG[:, blk, kv * 64:(kv + 1) * 64], ident[:]),
                          ["STG0", "STG1", "ident"], [bk])
                        CP("dve", KC[0:64, kv, 1024 + blk * 128:1024 + (blk + 1) * 128], b_[0:64, 0:128], [bk], ["KA*"])
                        CP("act", KC[64:128, kv, 1024 + blk * 128:1024 + (blk + 1) * 128], b_[0:64, 0:128], [bk], ["KA*"])
                for blk in range(2):
                    DMA("sp", STG[:, blk, 0:128], c_gv[blk * 128:(blk + 1) * 128, :], ["KA*"], ["STG0", "STG1"], "cx")
                for blk in range(2):
                    CP("dve", VC[:, 8 + blk, :].rearrange("p (h c) -> p h c", c=128)[:, :, 0:64],
                       STG[:, blk, 0:128].rearrange("p (h c) -> p h c", c=64), ["STG0", "STG1"], [f"VA.{8 + blk}"])
            if rope:
                swap64(WQ, wqk)
            CMB = {}
            for hg in range(8):
                if hg < 4:
                    CMB[hg] = (CM2[:, hg], "Y*")
                elif hg < 6:
                    CMB[hg] = (SQ[:].rearrange("p a b -> p (a b)")[:, 0:3328].rearrange("p (h x c) -> p h x c", x=26, c=64)[:, hg - 4], "SQ*")
                else:
                    CMB[hg] = (WSW[:, 0:3328].rearrange("p (h x c) -> p h x c", x=26, c=64)[:, hg - 6], "WSW")

            def cm_zero(which):
                def f():
                    if which == 0:
                        MS("dve", Y[:], 0.0, ["Y*"])
                        MS("dve", SQ[:].rearrange("p a b -> p (a b)")[:, 0:3328], 0.0, ["SQ*"])
                    else:
                        MS("dve", WSW[:, 0:3328], 0.0, ["WSW"])
                return f

            def cm_gen(hg):
                def f():
                    dstv, dk = CMB[hg]
                    TAh = STG[:].rearrange("p a b -> p (a b)")[:, 0:960]
                    for i in range(2):
                        src = bass.AP(tpad.tensor, hg * 15 * 160 + 16, [[1, 64], [160, 15], [1, 64]])
                        DMA("sp", TAh[64 * i:64 * i + 64, :].rearrange("p (a b) -> p a b", b=64), src, (), ["STG0", "STG1"], "tp")
                    for i in range(2):
                        srcs = sb_ap(STG, 64 * i, 64, 14 * 64 + 63, [[-64, 15], [-1, 64]])
                        dst = dstv[64 * i:64 * i + 64, i + 4:i + 19, :]
                        ACT(dst, srcs, AF.Exp, ["STG0", "STG1"], [dk])
                        ck = sb_ap(colok, 64 * i, 64, 0, [[0, 15], [1, 64]])
                        TT(dst, dst, ck, ALU.mult, ["colok", dk], [dk])
                return f
            side_t = {0: [cm_zero(0)] + [cm_gen(hg) for hg in range(0, 5)], 1: [cm_zero(1)] + [cm_gen(hg) for hg in range(5, 8)]} if g == "S" else {0: None, 1: None}
            for t in range(2):
                for hh in range(4):
                    b_, bk = sbank()
                    for k in range(8):
                        MM(b_[:, :], WQ(k, hh * 128, hh * 128 + 128), U[:, k, t * 512:(t + 1) * 512], k == 0, k == 7, [wqk, f"U{t}"], [bk])
                    bs_, bsk = None, None
                    if rope:
                        bs_, bsk = sbank()
                        for k in range(8):
                            MM(bs_[:, :], WSW[:, k * 512 + hh * 128:k * 512 + hh * 128 + 128], U[:, k, t * 512:(t + 1) * 512], k == 0, k == 7,
                               ["WSW", f"U{t}"], [bsk])
                    qk_norm_rope(QC[:, hh, :], "QA*", b_, bk, bs_, bsk, 0, t)
                if g == "S":
                    qsegs = [(0, 512, list(range(10)))]
                else:
                    qsegs = [(0, 256, [4 * t, 4 * t + 1]), (256, 512, [4 * t + 2, 4 * t + 3])]
                attend(8, lambda h, q0, q1: (QC[64 * (h % 2):64 * (h % 2) + 64, h // 2, q0:q1], "QA*"),
                       lambda h, kb: (KC[64 * (h % 2):64 * (h % 2) + 64, h // 4, kb * 128:(kb + 1) * 128], "KA*"),
                       lambda h, kb: (VC[:, kb, (h // 4) * 128:(h // 4) * 128 + 128], "VA*"),
                       64, 64 ** -0.5, qsegs,
                       lambda h, q0, q1, num, numk, den, denk, t=t: ep_std(h, t * 512 + q0, t * 512 + q1, num, numk, den, denk, chunk0=0),
                       side=side_t[t], side_every=8 if t == 0 else 12)
                while side_t[t]:
                    side_t[t].pop(0)()

            KN = ATT[:, 0:4 * NK].rearrange("p (h n) -> p h n", n=NK)
            VN = ATT[:, 5120:5120 + 10 * 512].rearrange("p (b c) -> p b c", c=512)
            QNt = [ATT[:, 10240 + tt * 2048:10240 + (tt + 1) * 2048].rearrange("p (h n) -> p h n", n=512) for tt in range(2)]
            WQn, wqnk = None, None
            for hf in range(2):
                WKn, wknk = wload(wi[:, :, 1280 + hf * 256:1280 + (hf + 1) * 256], 8, 256)
                WVn, wvnk = wload(wi[:, :, 1792 + hf * 256:1792 + (hf + 1) * 256], 8, 256)
                WQn, wqnk = wload(wi[:, :, 768 + hf * 256:768 + (hf + 1) * 256], 8, 256)
                for t in range(2):
                    for hh in range(4):
                        b_, bk = sbank()
                        for k in range(8):
                            MM(b_[0:64, :], WKn(k, hh * 64, hh * 64 + 64), U[:, k, t * 512:(t + 1) * 512], k == 0, k == 7, [wknk, f"U{t}"], [bk])
                        CP("act", KN[0:64, hh, t * 512:(t + 1) * 512], b_[0:64, :], [bk], ["KA*"])
                MS("dve", VN[:, :, :].rearrange("p b (h c) -> p b h c", c=128)[:, :, :, 64:128], 1.0, ["VA*", "KA*", "QA*"])
                for tb in range(8):
                    b_, bk = sbank()
                    for k in range(8):
                        MM(b_[:, 0:256], U[:, k, tb * 128:(tb + 1) * 128], WVn(k, 0, 256), k == 0, k == 7, [wvnk, f"U{tb // 4}"], [bk])
                    CP("act" if tb % 2 else "dve", VN[:, tb, :].rearrange("p (h c) -> p h c", c=128)[:, :, 0:64],
                       b_[:, 0:256].rearrange("p (h c) -> p h c", c=64), [bk], [f"VA.{tb}"])
                if g == "S":
                    for blk in range(2):
                        DMA("sp", STG[:, blk, 0:256], c_nk[blk * 128:(blk + 1) * 128, hf * 256:(hf + 1) * 256], (), ["STG0", "STG1"], "cx")
                    for blk in range(2):
                        for hh in range(4):
                            b_, bk = sbank()
                            A("pe", lambda h, b_=b_, blk=blk, hh=hh: h.transpose(b_[0:64, 0:128], STG[:, blk, hh * 64:(hh + 1) * 64], ident[:]),
                              ["STG0", "STG1", "ident"], [bk])
                            CP("dve", KN[0:64, hh, 1024 + blk * 128:1024 + (blk + 1) * 128], b_[0:64, 0:128], [bk], ["KA*"])
                    for blk in range(2):
                        DMA("pool", VN[:, 8 + blk, :].rearrange("p (h c) -> p h c", c=128)[:, :, 0:64],
                            c_nv[blk * 128:(blk + 1) * 128, hf * 256:(hf + 1) * 256].rearrange("p (h c) -> p h c", c=64), (), [f"VA.{8 + blk}"], f"cn{blk}")
                    CP("dve", KN[64:80, :, 0:1024], sb_ap(KPE, 0, 16, 0, [[0, 4], [1, 1024]]), ["AUG"], ["KA*", "VA*", "QA*"])
                    MS("dve", KN[64:80, :, 1024:1280], 0.0, ["KA*", "VA*", "QA*"])
                if g == "S":
                    for tt in range(2):
                        CP("act", QNt[tt][64:80, :, :], sb_ap(KPE, 32, 16, tt * 512, [[0, 4], [1, 512]]), ["AUG"], [f"QA.a{tt}", "KA*", "VA*"])
                for t in range(2):
                    QN = QNt[t]
                    for hh in range(4):
                        b_, bk = sbank()
                        for k in range(8):
                            MM(b_[0:64, :], WQn(k, hh * 64, hh * 64 + 64), U[:, k, t * 512:(t + 1) * 512], k == 0, k == 7, [wqnk, f"U{t}"], [bk])
                        ACT(QN[0:64, hh, :], b_[0:64, :], AF.Copy, [bk], [f"QA.q{t}{hh}"], scale=0.125)
                    if g == "S":
                        own = list(range(0, 6)) if t == 0 else list(range(2, 8))
                        qsegs = [(0, 512, own + [8, 9])]
                        kdim = 80

                        def cmf(h, q0, kb, t=t, hf=hf):
                            if kb >= 8:
                                return None
                            x0 = 7 + 8 * t - 2 * kb + 4
                            return CMB[hf * 4 + h][0][:, x0:x0 + 8, :].rearrange("p a b -> p (a b)")
                    else:
                        qsegs = [(0, 256, [4 * t, 4 * t + 1]), (256, 512, [4 * t + 2, 4 * t + 3])]
                        kdim = 64
                        cmf = None
                    attend(4, lambda h, q0, q1, QN=QN: (QN[0:kdim, h, q0:q1], "QA*"),
                           lambda h, kb: (KN[0:kdim, h, kb * 128:(kb + 1) * 128], "KA*"),
                           lambda h, kb: (VN[:, kb, h * 128:(h + 1) * 128], "VA*"),
                           64, 1.0, qsegs,
                           lambda h, q0, q1, num, numk, den, denk, t=t, hf=hf: ep_std(h, t * 512 + q0, t * 512 + q1, num, numk, den, denk,
                                                                                       chunk0=4 + hf * 2),
                           cmf=cmf)

        import os
        dbg = os.environ.get("KDBG", "")
        stop = False
        for g, col in (("P", 0), ("S", 1)):
            if stop:
                break
            P.phase = f"{g}:load"
            load_x(g)
            if dbg == f"{g},0,load":
                store_y(g)
                break
            for l in range(2):
                P.phase = f"{g}{l}:norm1"
                if g == "P":
                    for i in (0, 1):
                        if (l, i) not in early_done:
                            mod_part(l, i)
                for t in range(2):
                    norm_mod(l, col, 0, 1, t)
                if g == "P":
                    pending_side.extend([(lambda l=l, i=i: mod_part(l, i)) for i in (2, 3, 4, 5)])
                if dbg == f"{g},{l},norm":
                    stop = True
                    break
                P.phase = f"{g}{l}:mixer"
                if l == 0:
                    mixer_even(g, col)
                else:
                    mixer_odd(g, col)
                P.phase = f"{g}{l}:modrest"
                while pending_side:
                    pending_side.pop(0)()
                P.phase = f"{g}{l}:wout"
                if dbg == f"{g},{l},mixonly":
                    stop = True
                    break
                for t in range(2):
                    wout_phase(l, col, t)
                if dbg == f"{g},{l},mix":
                    stop = True
                    break
                P.phase = f"{g}{l}:norm2"
                for t in range(2):
                    norm_mod(l, col, 3, 4, t)
                P.phase = f"{g}{l}:ffn"
                if g == "P" and l == 0:
                    for i in (0, 1):
                        early_side.append(lambda i=i: (mod_part(1, i), early_done.add((1, i))))
                if g == "P":
                    ffn_phase(l, g, col)
                else:
                    ffn_phase_dve(l, g, col)
                if dbg == f"{g},{l},ffn":
                    stop = True
                    break
            P.phase = f"{g}:store"
            store_y(g)

        P.final_waits("sp")
        _CACHE["labels"] = P.labels
        with nc.Block() as block:
            P.emit(block)
    return nc


_CACHE = {}


def _rope_tables():
    def table(n, rot):
        t = np.arange(n)
        nf = rot // 4
        inv = 1.0 / (10000.0 ** (np.arange(nf) / nf))
        ang = np.concatenate([(t // 64)[:, None] * inv[None, :], (t % 64)[:, None] * inv[None, :]], axis=-1)
        return np.cos(ang).astype(np.float32), np.sin(ang).astype(np.float32)
    cos_a, sin_a = table(1024, 32)
    cos_h, sin_h = table(1024, 64)
    rh = np.zeros((128, 2, 1024), np.float32)
    for p in range(128):
        d = p % 64
        j = d % 32
        rh[p, 0] = cos_h[:, j]
        rh[p, 1] = (-1.0 if d < 32 else 1.0) * sin_h[:, j]
    ra = np.zeros((128, 2, 1024), np.float32)
    for p in range(64, 96):
        d = p - 64
        j = d % 16
        ra[p, 0] = cos_a[:, j]
        ra[p, 1] = (-1.0 if d < 16 else 1.0) * sin_a[:, j]
    return rh, ra


def _na_consts():
    rows = 16
    r = np.arange(rows)
    rs = np.clip(r - 4, 0, rows - 8)
    rowok = (r[None, :] >= rs[:, None]) & (r[None, :] < rs[:, None] + 8)
    col = np.arange(64)
    cs = np.clip(col - 8, 0, 48)
    col_ok = (col[None, :] >= cs[:, None]) & (col[None, :] < cs[:, None] + 16)
    aug = np.zeros((32, 1024), np.float32)
    tq = np.arange(1024) // 64
    for m in range(16):
        aug[m] = np.where(rowok[tq, m], 0.0, -BIG)
        aug[16 + m] = (tq == m).astype(np.float32)
    ck = np.zeros((128, 64), np.float32)
    for i in range(2):
        ck[64 * i:64 * i + 64] = col_ok.T.astype(np.float32)
    return aug, ck


def make_in_maps(inp):
    f = lambda k: np.ascontiguousarray(np.asarray(inp[k], dtype=np.float32))
    x_prompt, x_sample, c = f("x_prompt"), f("x_sample"), f("c")
    fm = lambda v: np.ascontiguousarray(v.reshape(-1, 128).T)
    vec_list = []
    nwv = f("norm_w")
    vec_list.append(np.concatenate([fm(nwv[l, i]) for l in range(2) for i in range(4)], axis=1))
    bm = f("b_mod")
    vec_list.append(np.concatenate([fm(bm[l]) for l in range(2)], axis=1))
    cwv = f("conv_w")
    vec_list.append(np.concatenate([fm(cwv[l, j]) for l in range(2) for j in range(3)], axis=1))
    cbv = f("conv_b")
    vec_list.append(np.concatenate([fm(cbv[l]) for l in range(2)], axis=1))
    vec_list.append(fm(f("q_norm_w")[0]))
    vec_list.append(fm(f("kv_norm_w")[0]))
    vec_list.append(fm(f("diff_subln_w")[0]))
    qk = f("qk_norm_w")[0]
    vec_list.append(np.stack([np.tile(qk[0], 2), np.tile(qk[1], 2)], axis=1))
    sw = lambda v: np.concatenate([v[32:], v[:32]])
    vec_list.append(np.stack([np.tile(sw(qk[0]), 2), np.tile(sw(qk[1]), 2)], axis=1))
    vecs = np.concatenate(vec_list, axis=1).astype(np.float32)
    vecs = np.ascontiguousarray(np.pad(vecs, ((0, 0), (0, 528 - vecs.shape[1]))))
    rh, ra = _rope_tables()
    aug, colok = _na_consts()
    shared = {
        "ident": np.eye(128, dtype=np.float32), "vecs": vecs, "ropeh": rh, "ropea": ra, "aug": aug, "colok": colok,
        "lamb": np.ascontiguousarray(np.broadcast_to(f("diff_lam")[0].reshape(1, 256), (128, 256))),
        "kvwb": np.ascontiguousarray(np.broadcast_to(f("kv_norm_w")[0].reshape(1, 128), (128, 128))),
        "qk1b": np.ascontiguousarray(np.broadcast_to(qk[1].reshape(1, 64), (128, 64))),
        "w_mod": f("w_mod"), "w_in_even": f("w_in_even")[0], "w_in_odd": f("w_in_odd")[0],
        "w_out_even": f("w_out_even")[0], "w_out_odd": f("w_out_odd")[0], "w_uq": f("w_uq")[0], "w_uk": f("w_uk")[0],
        "w_uv": f("w_uv")[0], "w_up": f("w_up"), "w_down": f("w_down"),
        "rpbp": np.ascontiguousarray(np.pad(f("na_rpb")[0].reshape(120, 31), ((0, 0), (64, 65)))),
    }
    c_ctx = f("c_ctx")
    caches = {"c_ckv": ("cache_mla_ckv", 128), "c_kpe": ("cache_mla_kpe", 32), "c_dk": ("cache_diff_k", 512),
              "c_dv": ("cache_diff_v", 512), "c_gk": ("cache_gqa_k", 128), "c_gv": ("cache_gqa_v", 128),
              "c_nk": ("cache_na_k", 512), "c_nv": ("cache_na_v", 512)}
    in_maps = []
    for core in range(NCORES):
        b = core % 4
        m = dict(shared)
        m["xp"] = np.ascontiguousarray(x_prompt[core * 4:(core + 1) * 4].reshape(1024, 1024))
        m["xs"] = np.ascontiguousarray(x_sample[b])
        cv = np.stack([c_ctx, c[b]], axis=0)
        m["cvT"] = np.ascontiguousarray(cv.reshape(2, 8, 128).transpose(2, 1, 0).reshape(128, 16))
        for k, (src, n) in caches.items():
            m[k] = np.ascontiguousarray(f(src)[b, 0].reshape(256, n))
        in_maps.append(m)
    return in_maps


def kernel(**inp):
    if "nc" not in _CACHE:
        _CACHE["nc"] = build_program()
    nc = _CACHE["nc"]
    in_maps = make_in_maps(inp)
    res = run_bass_kernel_spmd(nc, in_maps, core_ids=list(range(NCORES)))
    R = res.results
    y_prompt = np.concatenate([R[i]["yp"].reshape(4, 256, 1024) for i in range(NCORES)], axis=0)
    y_sample = np.stack([R[i]["ys"] for i in range(4)], axis=0)
    st0 = np.concatenate([R[i]["st0"].reshape(4, 256, 1184) for i in range(NCORES)], axis=0)
    st1 = np.concatenate([R[i]["st1"].reshape(4, 256, 1280) for i in range(NCORES)], axis=0)
    outs = (
        y_prompt, y_sample,
        st0[:, :, 0:128].reshape(32, 1, 256, 128), st0[:, :, 128:160].reshape(32, 1, 256, 32),
        st0[:, :, 160:672].reshape(32, 1, 256, 4, 128), st0[:, :, 672:1184].reshape(32, 1, 256, 4, 128),
        st1[:, :, 0:128].reshape(32, 1, 256, 2, 64), st1[:, :, 128:256].reshape(32, 1, 256, 2, 64),
        st1[:, :, 256:768].reshape(32, 1, 256, 8, 64), st1[:, :, 768:1280].reshape(32, 1, 256, 8, 64),
    )
    return tuple(np.ascontiguousarray(o, dtype=np.float32) for o in outs)
```

```python
import math
import numpy as np
from contextlib import ExitStack
import concourse.bass as bass
import concourse.mybir as mybir
from concourse.bass_utils import run_bass_kernel_spmd

F32 = mybir.dt.float32
BF16 = mybir.dt.bfloat16
AF = mybir.ActivationFunctionType
ALU = mybir.AluOpType

EPOCH = 6000
ENGS = ("pe", "act", "dve", "pool", "sp")
EPS = 1e-6
BIG = 30000.0
NCORES = 8


class Prog:
    def __init__(self, nc):
        self.nc = nc
        self.ops = {e: [] for e in ENGS}
        self.cnt = {e: 0 for e in ENGS}
        self.esems = {e: [] for e in ENGS}
        self.waited = {}
        self.lastw = {}
        self.readers = {}
        self.dsems = {}
        self.allkeys = set()
        self.phase = ''
        self.labels = {e: [] for e in ENGS}

    def _esem(self, e, ep):
        while len(self.esems[e]) <= ep:
            self.esems[e].append(self.nc.alloc_semaphore(name=f"s_{e}_{len(self.esems[e])}"))
        return self.esems[e][ep]

    def _event_of(self, e, idx):
        return (self._esem(e, idx // EPOCH), idx % EPOCH + 1, e, idx)

    def _need(self, eng, ev, waits):
        if ev is None:
            return
        sem, val, src, idx = ev
        if src == "pe" and eng == "pe":
            return
        key = (eng, id(sem))
        if self.waited.get(key, 0) >= val:
            return
        self.waited[key] = val
        waits.append((sem, val))

    def _expand(self, k):
        if k.endswith("*"):
            pre = k[:-1] + "."
            return [k] + [x for x in self.allkeys if x.startswith(pre)]
        if "." in k:
            self.allkeys.add(k)
            return [k, k.split(".")[0] + "*"]
        return [k]

    def op(self, eng, fn, reads=(), writes=(), dsem=None):
        waits = []
        rk = [x for k in reads for x in self._expand(k)]
        wk = [x for k in writes for x in self._expand(k)]
        for k in rk:
            self._need(eng, self.lastw.get(k), waits)
            if k.startswith("ps"):
                for ev in self.readers.get(k, ()):
                    if ev[2] != eng:
                        self._need(eng, ev, waits)
        for k in wk:
            self._need(eng, self.lastw.get(k), waits)
            for ev in self.readers.get(k, ()):
                self._need(eng, ev, waits)
        if dsem is not None:
            if dsem not in self.dsems:
                self.dsems[dsem] = [self.nc.alloc_semaphore(name=f"d_{dsem}"), 0]
            d = self.dsems[dsem]
            d[1] += 16
            ev = (d[0], d[1], "dma", None)
            inc = (d[0], 16)
        else:
            idx = self.cnt[eng]
            self.cnt[eng] += 1
            ev = self._event_of(eng, idx)
            inc = (ev[0], 1)
        for k in writes:
            if k.endswith("*"):
                for x in self._expand(k):
                    self.lastw[x] = ev
                    self.readers[x] = []
            else:
                self.lastw[k] = ev
                self.readers[k] = []
        for k in reads:
            if k.endswith("*"):
                for x in self._expand(k):
                    self.readers.setdefault(x, []).append(ev)
            else:
                self.readers.setdefault(k, []).append(ev)
        self.ops[eng].append((fn, waits, inc))
        self.labels[eng].append(self.phase)
        return ev

    def final_waits(self, eng="sp"):
        waits = []
        for name, (sem, val) in self.dsems.items():
            if val > 0:
                waits.append((sem, val))
        for e in ENGS:
            if e != eng and self.cnt[e] > 0:
                ev = self._event_of(e, self.cnt[e] - 1)
                waits.append((ev[0], ev[1]))
        self.ops[eng].append((None, waits, None))

    def emit(self, block):
        hmap = {"pe": "tensor", "act": "scalar", "dve": "vector", "pool": "gpsimd", "sp": "sync"}

        def mk(e):
            def body(h):
                for fn, waits, inc in self.ops[e]:
                    for sem, val in waits:
                        h.wait_ge(sem, val)
                    if fn is not None:
                        fn(h).then_inc(inc[0], inc[1])
            return body

        for e in ENGS:
            if self.ops[e]:
                getattr(block, hmap[e])(mk(e))


def sb_ap(t, p0, npart, off, dims):
    fsz = 1
    for s in t.shape[1:]:
        fsz *= s
    return bass.AP(t, p0 * fsz + off, [[fsz, npart]] + [list(d) for d in dims])


def build_program():
    nc = bass.Bass("TRN2", target_bir_lowering=False)
    di = lambda n, s: nc.dram_tensor(n, list(s), F32, kind="ExternalInput").ap()
    do = lambda n, s: nc.dram_tensor(n, list(s), F32, kind="ExternalOutput").ap()
    xin = {"P": di("xp", [1024, 1024]), "S": di("xs", [1024, 1024])}
    yout = {"P": do("yp", [1024, 1024]), "S": do("ys", [1024, 1024])}
    st_out = [do("st0", [1024, 1184]), do("st1", [1024, 1280])]
    d_ident = di("ident", [128, 128])
    d_cvT = di("cvT", [128, 16])
    d_vecs = di("vecs", [128, 528])
    d_ropeh = di("ropeh", [128, 2, 1024])
    d_ropea = di("ropea", [128, 2, 1024])
    d_aug = di("aug", [32, 1024])
    d_colok = di("colok", [128, 64])
    d_lam = di("lamb", [128, 256])
    d_kvwb = di("kvwb", [128, 128])
    c_ckv = di("c_ckv", [256, 128]); c_kpe = di("c_kpe", [256, 32])
    c_dk = di("c_dk", [256, 512]); c_dv = di("c_dv", [256, 512])
    c_gk = di("c_gk", [256, 128]); c_gv = di("c_gv", [256, 128])
    c_nk = di("c_nk", [256, 512]); c_nv = di("c_nv", [256, 512])
    w_mod = di("w_mod", [2, 1024, 6144])
    w_in = [di("w_in_even", [1024, 1952]), di("w_in_odd", [1024, 2304])]
    w_outw = [di("w_out_even", [1024, 1024]), di("w_out_odd", [1024, 1024])]
    w_uq = di("w_uq", [256, 768]); w_uk = di("w_uk", [128, 512]); w_uv = di("w_uv", [128, 512])
    w_up = di("w_up", [2, 1024, 5632]); w_down = di("w_down", [2, 2816, 1024])
    tpad = di("rpbp", [120, 160])
    d_qk1b = di("qk1b", [128, 64])

    VO = {}
    o = 0
    for name, n in [("nw", 64), ("bmod", 96), ("cw", 264), ("cb", 88), ("qnw", 2), ("kvw", 1), ("sub", 1),
                    ("qkw", 2), ("qkws", 2)]:
        VO[name] = o
        o += n
    assert o <= 528

    with ExitStack() as es:
        sb = lambda n, s, d: es.enter_context(nc.sbuf_tensor("sb_" + n, list(s), d))
        P = Prog(nc)
        ps = [es.enter_context(nc.psum_tensor(f"ps{i}", [128, 512], F32)) for i in range(8)]
        rr = {"s": 0, "a": 0}

        rr["ns"] = 4

        def sbank():
            i = rr["s"] % rr["ns"]
            rr["s"] += 1
            return ps[i], f"ps{i}"

        def abank():
            na = 8 - rr["ns"]
            i = rr["ns"] + rr["a"] % na
            rr["a"] += 1
            return ps[i], f"ps{i}"

        X = sb("X", [128, 8, 1024], F32)
        U = sb("U", [128, 8, 1024], BF16)
        Y = sb("Y*", [128, 8, 512], F32)
        SQ = sb("SQ*", [128, 8, 512], BF16)
        OT = sb("OT*", [128, 8, 1024], BF16)
        NW = 4
        WB = [sb(f"WB{i}", [128, 4096], BF16) for i in range(NW)]
        WSW = sb("WSW", [128, 4096], BF16)
        ATT = sb("ATT", [128, 14336], BF16)
        ident = sb("ident", [128, 128], F32)
        ones = sb("ones", [128, 128], BF16)
        bd = sb("bd", [128, 128], BF16)
        vecs = sb("vecs", [128, 528], F32)
        cvT = sb("cvT", [128, 16], F32)
        csT = sb("csT", [128, 16], BF16)
        modT = sb("modT", [128, 2, 96], F32)
        MT = sb("MT", [128, 2, 6, 16], F32)
        RS = sb("RS", [128, 512], F32)
        RD = sb("RD", [128, 2, 512], F32)
        R1 = sb("R1", [128, 512], F32)
        R2 = sb("R2", [128, 512], F32)
        R3 = sb("R3", [128, 512], F32)
        R4 = sb("R4", [128, 512], F32)
        PTALL = sb("PTALL", [128, 7 * 512], BF16)
        PT = [PTALL[:, i * 512:(i + 1) * 512] for i in range(6)]
        ZB = {(nm, par): PTALL[:, (ni * 2 + par) * 516:(ni * 2 + par + 1) * 516] for ni, nm in enumerate("gv") for par in range(2)}
        DG = {(nm, tap, par): PTALL[:, 2064 + ((ni * 2 + ti) * 2 + par) * 128:2064 + ((ni * 2 + ti) * 2 + par + 1) * 128]
              for ni, nm in enumerate("gv") for ti, tap in enumerate((0, 2)) for par in range(2)}
        identb = sb("identb", [128, 128], BF16)
        PTS = sb("PTS", [128, 512], BF16)
        ropeh = sb("ropeh", [128, 2, 1024], BF16)
        ropea = sb("ropea", [128, 2, 1024], BF16)
        colok = sb("colok", [128, 64], F32)
        CM2 = Y[:].bitcast(BF16).rearrange("p a b -> p (a b)")[:, 0:6656].rearrange("p (h x c) -> p h x c", x=26, c=64)
        qk1b = sb("qk1b", [128, 64], F32)
        lamt = sb("lamt", [128, 256], F32)
        lam = sb("lam", [128, 4], F32)
        kvwb = sb("kvwb", [128, 128], F32)
        epsD = sb("epsD", [128, 1], F32)
        CKVN = sb("CKVN*", [128, 1280], BF16)
        KPE = sb("KPE*", [128, 1280], BF16)
        WK96 = sb("WK96", [128, 2, 8, 96], BF16)
        STG = sb("STG", [128, 2, 512], F32)
        SMALL = sb("SMALL", [128, 8], F32)

        def A(eng, fn, r=(), w=(), dsem=None):
            return P.op(eng, fn, reads=r, writes=w, dsem=dsem)

        def MM(out, lhsT, rhs, st, sp_, r, w):
            A("pe", lambda h: h.matmul(out, lhsT=lhsT, rhs=rhs, start=st, stop=sp_), r, w)

        def ACT(out, in_, func, r, w, scale=None, bias=None):
            kw = {}
            if scale is not None:
                kw["scale"] = scale
            if bias is not None:
                kw["bias"] = bias
            A("act", lambda h: h.activation(out=out, in_=in_, func=func, **kw), r, w)

        def TT(out, in0, in1, op, r, w, eng="dve"):
            A(eng, lambda h: h.tensor_tensor(out=out, in0=in0, in1=in1, op=op), r, w)

        def STT(out, in0, scalar, in1, op0, op1, r, w):
            A("dve", lambda h: h.scalar_tensor_tensor(out=out, in0=in0, scalar=scalar, in1=in1, op0=op0, op1=op1), r, w)

        def TS(out, in0, s1, s2, op0, op1, r, w):
            A("dve", lambda h: h.tensor_scalar(out=out, in0=in0, scalar1=s1, scalar2=s2, op0=op0, op1=op1), r, w)

        def CP(eng, out, in_, r, w):
            if eng == "act":
                ACT(out, in_, AF.Copy, r, w)
            else:
                A(eng, lambda h: h.tensor_copy(out=out, in_=in_), r, w)

        def MS(eng, ap, val, w):
            A(eng, lambda h: h.memset(ap, val), (), w)

        uniq = [0]

        def DMA(eng, out, in_, r, w, dsem):
            if dsem in ("c0", "c1"):
                uniq[0] += 1
                dsem = f"c{uniq[0] + 10}"
            A(eng, lambda h: h.dma_start(out=out, in_=in_), r, w, dsem=dsem)

        wctr = [0]

        def wload(src, kc, ncols):
            i = wctr[0] % NW
            wctr[0] += 1
            key = f"WB{i}"
            dst = sb_ap(WB[i], 0, 128, 0, [[ncols, kc], [1, ncols]])
            DMA("pool", dst, src, (), [key], dsem=key)
            t = WB[i]
            return (lambda k, c0, c1: t[:, k * ncols + c0: k * ncols + c1]), key

        DMA("sp", ident[:], d_ident, (), ["ident"], "c0")
        DMA("sp", cvT[:], d_cvT, (), ["cvT"], "c0")
        DMA("sp", vecs[:], d_vecs, (), ["vecs"], "c0")
        DMA("sp", colok[:], d_colok, (), ["colok"], "c0")
        DMA("sp", lamt[:], d_lam, (), ["lamt"], "c0")
        DMA("sp", kvwb[:], d_kvwb, (), ["kvwb"], "c0")
        DMA("pool", ropeh[:], d_ropeh, (), ["ropeh"], "c1")
        DMA("pool", KPE[0:16, 0:1024], d_aug[16:32, :], (), ["AUG"], "c1")
        DMA("pool", KPE[32:48, 0:1024], d_aug[0:16, :], (), ["AUG"], "c1")
        DMA("pool", ropea[:], d_ropea, (), ["ropea"], "c1")
        MS("dve", ones[:], 1.0, ["ones"])
        MS("dve", bd[:], 0.0, ["bd"])
        MS("dve", bd[0:64, 0:64], 1.0, ["bd"])
        MS("dve", bd[64:128, 64:128], 1.0, ["bd"])
        MS("dve", epsD[:], EPS, ["epsD"])
        CP("dve", identb[:], ident[:], ["ident"], ["identb"])
        MS("dve", WK96[:], 0.0, ["WK96"])
        DMA("sp", qk1b[:], d_qk1b, (), ["qk1b"], "c0")
        lam_init = 0.8 - 0.6 * math.exp(-0.3 * 0)
        TT(lamt[:, 0:64], lamt[:, 0:64], lamt[:, 64:128], ALU.mult, ["lamt"], ["lamt"])
        TT(lamt[:, 128:192], lamt[:, 128:192], lamt[:, 192:256], ALU.mult, ["lamt"], ["lamt"])
        A("dve", lambda h: h.reduce_sum(out=lam[:, 0:1], in_=lamt[:, 0:64], axis=mybir.AxisListType.X), ["lamt"], ["lam"])
        A("dve", lambda h: h.reduce_sum(out=lam[:, 1:2], in_=lamt[:, 128:192], axis=mybir.AxisListType.X), ["lamt"], ["lam"])
        ACT(lam[:, 0:2], lam[:, 0:2], AF.Exp, ["lam"], ["lam"])
        TT(lam[:, 2:3], lam[:, 1:2], lam[:, 0:1], ALU.subtract, ["lam"], ["lam"])
        TS(lam[:, 3:4], lam[:, 2:3], -lam_init, None, ALU.add, ALU.bypass, ["lam"], ["lam"])
        TS(SMALL[:, 0:1], vecs[:, VO["sub"]:VO["sub"] + 1], 1.0 - lam_init, None, ALU.mult, ALU.bypass, ["vecs"], ["SMALL"])

        import os
        ACT(csT[:], cvT[:], AF.Silu, ["cvT"], ["csT"])
        MT_SRC = {0: (8, 0, "g"), 1: (0, None, "c"), 2: (16, 1, "m"), 3: (32, 2, "g"), 4: (24, None, "c"), 5: (40, 3, "m")}

        def mod_part(l, i):
            c0, nwi, kind = MT_SRC[i]
            pm, pmk = sbank()
            for pc in range(2):
                src = w_mod[l].rearrange("(kc p) n -> p kc n", p=128)[:, :, c0 * 128 + pc * 512:c0 * 128 + (pc + 1) * 512]
                W, wk = wload(src, 8, 512)
                for jj in range(4):
                    j = pc * 4 + jj
                    for k in range(8):
                        MM(pm[:, 2 * j:2 * j + 2], W(k, jj * 128, jj * 128 + 128), csT[:, 2 * k:2 * k + 2],
                           k == 0, k == 7, [wk, "csT"], [pmk])
            bmb = sb_ap(vecs, 0, 128, VO["bmod"] + 48 * l + c0, [[1, 8], [0, 2]])
            mk_ = f"modT{l}_{i}"
            mod_v = modT[:, l, c0 * 2:(c0 + 8) * 2].rearrange("p (a b) -> p a b", b=2)
            TT(mod_v, pm[:, 0:16].rearrange("p (a b) -> p a b", b=2), bmb, ALU.add, [pmk, "vecs"], [mk_])
            mt_v = MT[:, l, i, :].rearrange("p (a b) -> p a b", b=2)
            tk_ = f"MT{l}_{i}"
            if kind == "c":
                CP("dve", mt_v, mod_v, [mk_], [tk_])
            else:
                nwb = sb_ap(vecs, 0, 128, VO["nw"] + (l * 4 + nwi) * 8, [[1, 8], [0, 2]])
                if kind == "g":
                    STT(mt_v, mod_v, 1.0, nwb, ALU.add, ALU.mult, [mk_, "vecs"], [tk_])
                else:
                    TT(mt_v, mod_v, nwb, ALU.mult, [mk_, "vecs"], [tk_])

        def mtc(l, i, k, col):
            return MT[:, l, i, 2 * k + col:2 * k + col + 1]

        def rstd_from(ssb, ssk, inv_n, out, outk, nrow=128, n=512):
            ACT(out[0:nrow, 0:n], ssb[0:nrow, 0:n], AF.Ln, [ssk, "epsD"], [outk], scale=inv_n, bias=epsD[0:nrow, :])
            ACT(out[0:nrow, 0:n], out[0:nrow, 0:n], AF.Exp, [outk], [outk], scale=-0.5)

        def norm_mod(l, col, gi, si, t):
            xs = X[:, :, t * 512:(t + 1) * 512]
            ACT(SQ[:], xs, AF.Square, [f"X{t}"], ["SQ*"])
            sb_, sk = sbank()
            for k in range(8):
                MM(sb_[:, :], ones[:], SQ[:, k, :], k == 0, k == 7, ["ones", "SQ*"], [sk])
            rstd_from(sb_, sk, 1.0 / 1024, RS, "RS")
            rb = [(R1, "R1"), (R2, "R2"), (R3, "R3"), (R4, "R4")]
            for k in range(8):
                tb_, tk_ = rb[k % 4]
                STT(tb_[:, :], X[:, k, t * 512:(t + 1) * 512], mtc(l, gi, k, col), RS[:, :], ALU.mult, ALU.mult,
                    [f"X{t}", "RS", f"MT{l}_{gi}"], [tk_])
                ACT(U[:, k, t * 512:(t + 1) * 512], tb_[:, :], AF.Identity, [tk_, f"MT{l}_{si}"], [f"U{t}"], bias=mtc(l, si, k, col))

        def post_res(l, col, gwi, t, yk="Y*"):
            sb_, sk = sbank()
            for k in range(8):
                MM(sb_[:, :], ones[:], SQ[:, k, :], k == 0, k == 7, ["ones", "SQ*"], [sk])
            rstd_from(sb_, sk, 1.0 / 1024, RS, "RS")
            TT(Y[:], Y[:], sb_ap(RS, 0, 128, 0, [[0, 8], [1, 512]]), ALU.mult, ["Y*", "RS"], ["Y*"])
            for k in range(8):
                xs = X[:, k, t * 512:(t + 1) * 512]
                STT(xs, Y[:, k, :], mtc(l, gwi, k, col), xs, ALU.mult, ALU.add, ["Y*", f"MT{l}_{gwi}", f"X{t}"], [f"X{t}"])

        def proj_fm(W, wk, cols, t, evac, ukey=None, src=None, nk=8):
            for i, (c0, c1) in enumerate(cols):
                b_, bk = sbank()
                for k in range(nk):
                    rhs = U[:, k, t * 512:(t + 1) * 512] if src is None else src(k)
                    MM(b_[0:c1 - c0, :], W(k, c0, c1), rhs, k == 0, k == nk - 1, [wk, ukey or f"U{t}"], [bk])
                evac(i, b_, bk)

        def head_rms(b_, bk, lhs, inv_n, n=512):
            ACT(PTS[:, 0:n], b_[:, 0:n], AF.Square, [bk], ["PTS"])
            s2, s2k = sbank()
            MM(s2[:, 0:n], lhs, PTS[:, 0:n], True, True, ["ones", "bd", "PTS"], [s2k])
            rstd_from(s2, s2k, inv_n, RS, "RS", n=n)

        def rope_comb(out, okey, b_, bk, bs_, bsk, tab, tcols, p0, p1, w=None, ws=None, n=512):
            c = tab[p0:p1, 0, tcols[0]:tcols[1]]
            s = tab[p0:p1, 1, tcols[0]:tcols[1]]
            if w is None:
                TT(R1[p0:p1, 0:n], b_[p0:p1, 0:n], c, ALU.mult, [bk, "ropeh", "ropea"], ["R1"])
                TT(R2[p0:p1, 0:n], bs_[p0:p1, 0:n], s, ALU.mult, [bsk, "ropeh", "ropea"], ["R2"])
            else:
                STT(R1[p0:p1, 0:n], b_[p0:p1, 0:n], w, c, ALU.mult, ALU.mult, [bk, "ropeh", "vecs"], ["R1"])
                STT(R2[p0:p1, 0:n], bs_[p0:p1, 0:n], ws, s, ALU.mult, ALU.mult, [bsk, "ropeh", "vecs"], ["R2"])
            return R1, R2

        deferred = []
        gcount = [0]
        pending_side = []

        def attend(nh, qf, kf, vf, dv, scale, qsegs, ep, cmf=None, tag="", fused=True, side=None, side_every=8):
            base_phase = P.phase.split("/")[0]
            P.phase = base_phase + "/att" + tag
            items = []
            for h in range(nh):
                for (q0, q1, kbs) in qsegs:
                    for i, kb in enumerate(kbs):
                        items.append((h, q0, q1, kb, i == 0, i == len(kbs) - 1))
            LA = 4
            acc = {}
            pts = {}

            def emit_s(j):
                h, q0, q1, kb, first, last = items[j]
                n = q1 - q0
                s_, sk = sbank()
                qa, qk_ = qf(h, q0, q1)
                ka, kk_ = kf(h, kb)
                MM(s_[:, 0:n], ka, qa, True, True, [qk_, kk_], [sk])
                pt = PT[j % 6]
                ptk = f"PT{j % 6}"
                ACT(pt[:, 0:n], s_[:, 0:n], AF.Exp, [sk], [ptk], scale=scale)
                if cmf is not None:
                    cm = cmf(h, q0, kb)
                    if cm is not None:
                        TT(pt[:, 0:n], pt[:, 0:n], cm, ALU.mult, [ptk, "Y*", "SQ*", "WSW"], [ptk])
                pts[j] = (pt, ptk)

            def emit_pv(j):
                h, q0, q1, kb, first, last = items[j]
                n = q1 - q0
                if first:
                    acc[(h, q0)] = (abank(), (None, None) if fused else abank())
                (num, numk), (den, denk) = acc[(h, q0)]
                pt, ptk = pts.pop(j)
                va, vk_ = vf(h, kb)
                MM(num[:, 0:n], va, pt[:, 0:n], first, last, [vk_, ptk], [numk])
                if not fused:
                    MM(den[:, 0:n], ones[:], pt[:, 0:n], first, last, ["ones", ptk], [denk])
                if last:
                    ep(h, q0, q1, num, numk, den, denk)
                    del acc[(h, q0)]

            GRP = 2
            LA = 4
            rr["ns"] = 4
            gi_ = 0
            deferred.clear()
            for j0 in range(0, len(items) + LA, GRP):
                gi_ += 1
                gcount[0] = gi_
                while deferred and deferred[0][0] <= gi_:
                    deferred.pop(0)[1]()
                if side and gi_ % side_every == 0:
                    ph_ = P.phase
                    P.phase = base_phase + "/side"
                    side.pop(0)()
                    P.phase = ph_
                for j in range(j0, j0 + GRP):
                    if 0 <= j - LA < len(items):
                        emit_pv(j - LA)
                for j in range(j0, j0 + GRP):
                    if j < len(items):
                        emit_s(j)
            while deferred:
                deferred.pop(0)[1]()
            rr["ns"] = 4
            P.phase = base_phase

        def ep_std(h, q0, q1, num, numk, den, denk, chunk0=0):
            n = q1 - q0
            pb = 64 * (h % 2)
            sl = h % 2
            ACT(RD[0:64, sl, 0:n], num[64:128, 0:n], AF.Ln, [numk], [f"RD{sl}"])
            ACT(RD[0:64, sl, 0:n], RD[0:64, sl, 0:n], AF.Exp, [f"RD{sl}"], [f"RD{sl}"], scale=-1.0)
            TT(OT[pb:pb + 64, chunk0 + h // 2, q0:q1], num[0:64, 0:n], RD[0:64, sl, 0:n], ALU.mult,
               [numk, f"RD{sl}"], [f"OT.{chunk0 + h // 2}.{pb}.{q0}"])

        def load_x(g):
            xd = xin[g]
            for t in range(2):
                stg = Y[:].rearrange("p a b -> p (a b)")
                DMA("sp", Y[:].rearrange("p a b -> p (a b)").rearrange("p (k f) -> p k f", f=1024),
                    xd[t * 512:(t + 1) * 512, :].rearrange("(k p) f -> p k f", p=128), (), ["Y*"], "xl")
                for k in range(8):
                    b_, bk = sbank()
                    for blk in range(4):
                        A("pe", lambda h, b_=b_, blk=blk, k=k: h.transpose(
                            b_[:, blk * 128:(blk + 1) * 128], stg[:, blk * 1024 + k * 128: blk * 1024 + (k + 1) * 128],
                            ident[:]), ["Y*", "ident"], [bk])
                    CP("act" if k % 2 else "dve", X[:, k, t * 512:(t + 1) * 512], b_[:, :], [bk], [f"X{t}"])

        def store_y(g):
            yd = yout[g]
            stg = Y[:].rearrange("p a b -> p (a b)")
            for t in range(2):
                for blk in range(4):
                    for half in range(2):
                        b_, bk = sbank()
                        for kk in range(4):
                            k = half * 4 + kk
                            A("pe", lambda h, b_=b_, kk=kk, k=k, blk=blk, t=t: h.transpose(
                                b_[:, kk * 128:(kk + 1) * 128], X[:, k, t * 512 + blk * 128: t * 512 + (blk + 1) * 128],
                                ident[:]), [f"X{t}", "ident"], [bk])
                        CP("act" if half else "dve", stg[:, blk * 1024 + half * 512: blk * 1024 + (half + 1) * 512],
                           b_[:, :], [bk], ["Y*"])
                DMA("sp", yd[t * 512:(t + 1) * 512, :].rearrange("(k p) f -> p k f", p=128),
                    Y[:].rearrange("p a b -> p (a b)").rearrange("p (k f) -> p k f", f=1024), ["Y*"], (), "yo")

        def wout_phase(l, col, t):
            wo = w_outw[l].rearrange("(kc p) n -> p kc n", p=128)
            for half in range(2):
                W, wk = wload(wo[:, :, half * 512:(half + 1) * 512], 8, 512)
                for mm in range(4):
                    m = half * 4 + mm
                    b_, bk = sbank()
                    for k in range(8):
                        MM(b_[:, :], W(k, mm * 128, mm * 128 + 128), OT[:, k, t * 512:(t + 1) * 512], k == 0, k == 7,
                           [wk, "OT*"], [bk])
                    ACT(SQ[:, m, :], b_[:, :], AF.Square, [bk], [f"SQ.{m}"])
                    CP("dve", Y[:, m, :], b_[:, :], [bk], [f"Y.{m}"])
            post_res(l, col, 2, t)

        def ffn_phase_dve(l, g, col):
            wu = w_up[l].rearrange("(kc p) n -> p kc n", p=128)
            wd = w_down[l].rearrange("(j p) n -> p j n", p=128)
            cw = lambda tap, j: vecs[:, VO["cw"] + (l * 3 + tap) * 44 + j: VO["cw"] + (l * 3 + tap) * 44 + j + 1]
            cb = lambda j: vecs[:, VO["cb"] + l * 44 + j: VO["cb"] + l * 44 + j + 1]
            segs = [(0, 256), (256, 512)] if g == "P" else [(0, 512)]
            H = ATT[:, 0:11264].rearrange("p (j n) -> p j n", n=512)
            for t in range(2):
                for pc in range(11):
                    Wg, wgk = wload(wu[:, :, pc * 256:(pc + 1) * 256], 8, 256)
                    Wv, wvk = wload(wu[:, :, 2816 + pc * 256:2816 + (pc + 1) * 256], 8, 256)
                    for jj in range(2):
                        j = pc * 2 + jj
                        zb = {}
                        for nm, Wx, wxk in (("g", Wg, wgk), ("v", Wv, wvk)):
                            b_, bk = abank()
                            for k in range(8):
                                MM(b_[:, :], Wx(k, jj * 128, jj * 128 + 128), U[:, k, t * 512:(t + 1) * 512], k == 0, k == 7,
                                   [wxk, f"U{t}"], [bk])
                            zb[nm] = (b_, bk)
                        if g == "S":
                            ot = 1 - t
                            hcol = ot * 512 + (0 if ot == 1 else 511)
                            for nm, Wx, wxk in (("g", Wg, wgk), ("v", Wv, wvk)):
                                b_, bk = zb[nm]
                                hb, hbk = sbank()
                                for k in range(8):
                                    MM(hb[:, 0:1], Wx(k, jj * 128, jj * 128 + 128), U[:, k, hcol:hcol + 1], k == 0, k == 7,
                                       [wxk, f"U{ot}"], [hbk])
                                zb[nm + "h"] = (hb, hbk)
                        for ci, nm in enumerate(("g", "v")):
                            b_, bk = zb[nm]
                            jc = j if nm == "g" else 22 + j
                            a_ = (R1 if nm == "g" else R2) if j % 2 == 0 else (R3 if nm == "g" else R4)
                            ak = ("R1" if nm == "g" else "R2") if j % 2 == 0 else ("R3" if nm == "g" else "R4")
                            ACT(a_[:, :], b_[:, :], AF.Identity, [bk, "vecs"], [ak], scale=cw(1, jc), bias=cb(jc))
                            for (a, b) in segs:
                                STT(a_[:, a + 1:b], b_[:, a:b - 1], cw(0, jc), a_[:, a + 1:b], ALU.mult, ALU.add,
                                    [bk, ak, "vecs"], [ak])
                                STT(a_[:, a:b - 1], b_[:, a + 1:b], cw(2, jc), a_[:, a:b - 1], ALU.mult, ALU.add,
                                    [bk, ak, "vecs"], [ak])
                            if g == "S":
                                hb, hbk = zb[nm + "h"]
                                if t == 0:
                                    STT(a_[:, 511:512], hb[:, 0:1], cw(2, jc), a_[:, 511:512], ALU.mult, ALU.add,
                                        [hbk, ak, "vecs"], [ak])
                                else:
                                    STT(a_[:, 0:1], hb[:, 0:1], cw(0, jc), a_[:, 0:1], ALU.mult, ALU.add,
                                        [hbk, ak, "vecs"], [ak])
                        ag, agk, av, avk = (R1, "R1", R2, "R2") if j % 2 == 0 else (R3, "R3", R4, "R4")
                        ACT(ag[:, :], ag[:, :], AF.Silu, [agk], [agk])
                        TT(H[:, j, :], ag[:, :], av[:, :], ALU.mult, [agk, avk], [f"ATT.{j}"])
                for mp in range(4):
                    accs = [abank() for _ in range(2)]
                    for jh in range(2):
                        W, wk = wload(wd[:, jh * 11:(jh + 1) * 11, mp * 256:(mp + 1) * 256], 11, 256)
                        for mm in range(2):
                            b_, bk = accs[mm]
                            for jj in range(11):
                                j = jh * 11 + jj
                                MM(b_[:, :], W(jj, mm * 128, mm * 128 + 128), H[:, j, :], j == 0, j == 21, [wk, "ATT*"], [bk])
                    for mm in range(2):
                        m = mp * 2 + mm
                        b_, bk = accs[mm]
                        ACT(Y[:, m, :], b_[:, :], AF.Copy, [bk], [f"Y.{m}"])
                        TT(SQ[:, m, :], Y[:, m, :], Y[:, m, :], ALU.mult, [f"Y.{m}"], [f"SQ.{m}"])
                post_res(l, col, 5, t)

        def ffn_phase(l, g, col):
            wu = w_up[l].rearrange("(kc p) n -> p kc n", p=128)
            wd = w_down[l].rearrange("(j p) n -> p j n", p=128)
            cw = lambda tap, j: vecs[:, VO["cw"] + (l * 3 + tap) * 44 + j: VO["cw"] + (l * 3 + tap) * 44 + j + 1]
            cb = lambda j: vecs[:, VO["cb"] + l * 44 + j: VO["cb"] + l * 44 + j + 1]
            nseg = 2 if g == "P" else 1
            sw_ = 512 // nseg
            pw_ = sw_ + 2
            H = ATT[:, 0:11264].rearrange("p (j n) -> p j n", n=512)
            zkeys = lambda nm, par: f"ZB{nm}{par}"

            def stage_a(t, j, Wg, wgk, Wv, wvk, jj):
                par = j % 2
                zb = {}
                for nm, Wx, wxk in (("g", Wg, wgk), ("v", Wv, wvk)):
                    b_, bk = abank()
                    for k in range(8):
                        MM(b_[:, :], Wx(k, jj * 128, jj * 128 + 128), U[:, k, t * 512:(t + 1) * 512], k == 0, k == 7,
                           [wxk, f"U{t}"], [bk])
                    zb[nm] = (b_, bk)
                hb, hbk = None, None
                if g == "S":
                    ot = 1 - t
                    hcol = ot * 512 + (0 if ot == 1 else 511)
                    hb, hbk = sbank()
                    for ci, (nm, Wx, wxk) in enumerate((("g", Wg, wgk), ("v", Wv, wvk))):
                        for k in range(8):
                            MM(hb[:, ci:ci + 1], Wx(k, jj * 128, jj * 128 + 128), U[:, k, hcol:hcol + 1], k == 0, k == 7,
                               [wxk, f"U{ot}"], [hbk])
                for ci, nm in enumerate(("g", "v")):
                    b_, bk = zb[nm]
                    jc = j if nm == "g" else 22 + j
                    a_, ak = ((R1, "R1") if nm == "g" else (R2, "R2")) if par == 0 else ((R3, "R3") if nm == "g" else (R4, "R4"))
                    ACT(a_[:, :], b_[:, :], AF.Identity, [bk, "vecs"], [ak], scale=cw(1, jc), bias=cb(jc))
                    z_ = ZB[(nm, par)]
                    zk = zkeys(nm, par)
                    zin = z_[:, 0:nseg * pw_].rearrange("p (s w) -> p s w", w=pw_)[:, :, 1:1 + sw_]
                    ACT(zin, b_[:, :].rearrange("p (s w) -> p s w", w=sw_), AF.Copy, [bk], [zk])
                    if g == "S":
                        pad = 513 if t == 0 else 0
                        CP("dve", z_[:, pad:pad + 1], hb[:, ci:ci + 1], [hbk], [zk])
                    for tap in (0, 2):
                        TS(DG[(nm, tap, par)], identb[:], cw(tap, jc), None, ALU.mult, ALU.bypass, ["identb", "vecs"], [f"DG{nm}{tap}{par}"])
                return (t, j, par)

            def stage_b(info):
                t, j, par = info
                for nm in ("g", "v"):
                    a_, ak = ((R1, "R1") if nm == "g" else (R2, "R2")) if par == 0 else ((R3, "R3") if nm == "g" else (R4, "R4"))
                    z_ = ZB[(nm, par)]
                    zk = zkeys(nm, par)
                    B_, Bk = sbank()
                    for sg in range(nseg):
                        o0 = sg * sw_
                        zo = sg * pw_
                        MM(B_[:, o0:o0 + sw_], DG[(nm, 0, par)], z_[:, zo:zo + sw_], True, False, [f"DG{nm}0{par}", zk], [Bk])
                        MM(B_[:, o0:o0 + sw_], DG[(nm, 2, par)], z_[:, zo + 2:zo + 2 + sw_], False, True, [f"DG{nm}2{par}", zk], [Bk])
                    TT(a_[:, :], B_[:, :], a_[:, :], ALU.add, [Bk, ak], [ak])
                ag, agk, av, avk = (R1, "R1", R2, "R2") if par == 0 else (R3, "R3", R4, "R4")
                ACT(ag[:, :], ag[:, :], AF.Silu, [agk], [agk])
                TT(H[:, j, :], ag[:, :], av[:, :], ALU.mult, [agk, avk], [f"ATT.{j}"])

            for t in range(2):
                for nm in "gv":
                    for par in range(2):
                        z_ = ZB[(nm, par)]
                        if g == "P":
                            MS("dve", z_[:, 0:516].rearrange("p (s w) -> p s w", w=258)[:, :, 0:258:257], 0.0, [zkeys(nm, par)])
                        else:
                            zp = 0 if t == 0 else 513
                            MS("dve", z_[:, zp:zp + 1], 0.0, [zkeys(nm, par)])
                pending = None
                for pc in range(11):
                    Wg, wgk = wload(wu[:, :, pc * 256:(pc + 1) * 256], 8, 256)
                    Wv, wvk = wload(wu[:, :, 2816 + pc * 256:2816 + (pc + 1) * 256], 8, 256)
                    for jj in range(2):
                        j = pc * 2 + jj
                        cur = stage_a(t, j, Wg, wgk, Wv, wvk, jj)
                        if pending is not None:
                            stage_b(pending)
                        pending = cur
                stage_b(pending)
                for mp in range(4):
                    accs = [abank() for _ in range(2)]
                    for jh in range(2):
                        W, wk = wload(wd[:, jh * 11:(jh + 1) * 11, mp * 256:(mp + 1) * 256], 11, 256)
                        for mm in range(2):
                            b_, bk = accs[mm]
                            for jj in range(11):
                                j = jh * 11 + jj
                                MM(b_[:, :], W(jj, mm * 128, mm * 128 + 128), H[:, j, :], j == 0, j == 21, [wk, "ATT*"], [bk])
                    for mm in range(2):
                        m = mp * 2 + mm
                        b_, bk = accs[mm]
                        ACT(Y[:, m, :], b_[:, :], AF.Copy, [bk], [f"Y.{m}"])
                        TT(SQ[:, m, :], Y[:, m, :], Y[:, m, :], ALU.mult, [f"Y.{m}"], [f"SQ.{m}"])
                post_res(l, col, 5, t)

        def load_ctx_T(dst_fn, src, ncols, dkey):
            stg = STG[:, 0, :]
            for blk in range(2):
                DMA("sp", STG[:, blk, 0:ncols], src[blk * 128:(blk + 1) * 128, :], (), ["STG0", "STG1"], "cx")
            for blk in range(2):
                for c0 in range(0, ncols, 128):
                    cn = min(128, ncols - c0)
                    b_, bk = sbank()
                    A("pe", lambda h, b_=b_, blk=blk, c0=c0, cn=cn: h.transpose(b_[0:cn, 0:128], STG[:, blk, c0:c0 + cn], ident[:]),
                      ["STG0", "STG1", "ident"], [bk])
                    CP("dve", dst_fn(c0 // 128, blk, cn), b_[0:cn, 0:128], [bk], [dkey])

        def swap64(Wx, wxk):
            for k in range(8):
                src = Wx(k, 0, 512).rearrange("p (g s d) -> p g s d", s=2, d=32)
                dst = WSW[:, k * 512:(k + 1) * 512].rearrange("p (g s d) -> p g s d", s=2, d=32)
                CP("dve", dst[:, :, 0, :], src[:, :, 1, :], [wxk], ["WSW"])
                CP("dve", dst[:, :, 1, :], src[:, :, 0, :], [wxk], ["WSW"])

        def mixer_even(g, col):
            l = 0
            rope = g == "S"
            nkb = 10 if g == "S" else 8
            wi = w_in[0].rearrange("(kc p) n -> p kc n", p=128)
            WA, wak = wload(wi[:, :, 0:416], 8, 416)
            for v in range(2 if rope else 1):
                for k in range(8):
                    if v == 0:
                        CP("dve", WK96[:, 0, k, 64:96], WA(k, 384, 416), [wak], ["WK96"])
                    else:
                        CP("dve", WK96[:, 1, k, 64:80], WA(k, 400, 416), [wak], ["WK96"])
                        CP("dve", WK96[:, 1, k, 80:96], WA(k, 384, 400), [wak], ["WK96"])
            CQ = ATT[:, 0:2048].rearrange("p (a n) -> p a n", n=1024)
            for t in range(2):
                def ev_ckv(i, b_, bk, t=t):
                    head_rms(b_, bk, ones[:], 1.0 / 128)
                    STT(CKVN[:, t * 512:(t + 1) * 512], b_[:, :], vecs[:, VO["kvw"]:VO["kvw"] + 1], RS[:, :], ALU.mult, ALU.mult,
                        [bk, "RS", "vecs"], [f"CKVN.{t}"])
                proj_fm(WA, wak, [(256, 384)], t, ev_ckv)
                b_, bk = sbank()
                for k in range(8):
                    MM(b_[0:96, :], WK96[:, 0, k, :], U[:, k, t * 512:(t + 1) * 512], k == 0, k == 7, ["WK96", f"U{t}"], [bk])
                if rope:
                    bs_, bsk = sbank()
                    for k in range(8):
                        MM(bs_[0:96, :], WK96[:, 1, k, :], U[:, k, t * 512:(t + 1) * 512], k == 0, k == 7, ["WK96", f"U{t}"], [bsk])
                    rope_comb(None, None, b_, bk, bs_, bsk, ropea, (t * 512, (t + 1) * 512), 64, 96)
                    TT(KPE[64:96, t * 512:(t + 1) * 512], R1[64:96, :], R2[64:96, :], ALU.add, ["R1", "R2"], ["KPE*"])
                else:
                    CP("dve", KPE[64:96, t * 512:(t + 1) * 512], b_[64:96, :], [bk], ["KPE*"])
                cqb = []
                for i in range(2):
                    b2, b2k = abank()
                    for k in range(8):
                        MM(b2[:, :], WA(k, i * 128, i * 128 + 128), U[:, k, t * 512:(t + 1) * 512], k == 0, k == 7, [wak, f"U{t}"], [b2k])
                    ACT(SQ[:, i, :], b2[:, :], AF.Square, [b2k], ["SQ*"])
                    cqb.append((b2, b2k))
                s2, s2k = sbank()
                for i in range(2):
                    MM(s2[:, :], ones[:], SQ[:, i, :], i == 0, i == 1, ["ones", "SQ*"], [s2k])
                rstd_from(s2, s2k, 1.0 / 256, RS, "RS")
                for i in range(2):
                    b2, b2k = cqb[i]
                    STT(CQ[:, i, t * 512:(t + 1) * 512], b2[:, :], vecs[:, VO["qnw"] + i:VO["qnw"] + i + 1], RS[:, :], ALU.mult, ALU.mult,
                        [b2k, "RS", "vecs"], [f"CQ.{i}{t}"])
            if g == "S":
                load_ctx_T(lambda c, blk, cn: CKVN[0:cn, 1024 + blk * 128:1024 + (blk + 1) * 128], c_ckv, 128, "CKVN*")
                for blk in range(2):
                    DMA("sp", STG[:, blk, 0:32], c_kpe[blk * 128:(blk + 1) * 128, :], (), ["STG0", "STG1"], "cx")
                for blk in range(2):
                    b_, bk = sbank()
                    A("pe", lambda h, b_=b_, blk=blk: h.transpose(b_[0:32, 0:128], STG[:, blk, 0:32], ident[:]), ["STG0", "STG1", "ident"], [bk])
                    CP("dve", KPE[64:96, 1024 + blk * 128:1024 + (blk + 1) * 128], b_[0:32, 0:128], [bk], ["KPE*"])
            kmix = int(os.environ.get("KMIX", "99"))
            if kmix <= 1:
                return
            if g == "P":
                state_out(0, [(256, 416), (928, 1440), (1440, 1952)], wi)
            if kmix <= 2:
                return
            Wq, wqk = wload(w_uq.rearrange("(kc p) n -> p kc n", p=128), 2, 768)
            if rope:
                for k in range(2):
                    for hh in range(8):
                        c = hh * 96
                        CP("dve", WSW[:, k * 768 + c + 64:k * 768 + c + 80], Wq(k, c + 80, c + 96), [wqk], ["WSW"])
                        CP("dve", WSW[:, k * 768 + c + 80:k * 768 + c + 96], Wq(k, c + 64, c + 80), [wqk], ["WSW"])
                        CP("dve", WSW[:, k * 768 + c:k * 768 + c + 64], Wq(k, c, c + 64), [wqk], ["WSW"])
            Wk_, wkk = wload(w_uk.rearrange("(kc p) n -> p kc n", p=128), 1, 512)
            Wv_, wvk = wload(w_uv.rearrange("(kc p) n -> p kc n", p=128), 1, 512)
            NK = 1280
            KA = ATT[:, 2048:2048 + 4 * NK].rearrange("p (h n) -> p h n", n=NK)
            VA = ATT[:, 7168:7168 + 10 * 512].rearrange("p (b c) -> p b c", c=512)
            QA = ATT[:, 12288:12288 + 2048].rearrange("p (h n) -> p h n", n=512)
            for hf in range(2):
                for kt in range(0, NK if g == "S" else 1024, 512):
                    n = min(512, (NK if g == "S" else 1024) - kt)
                    for hh in range(4):
                        hg = hf * 4 + hh
                        b_, bk = sbank()
                        MM(b_[0:64, 0:n], Wk_(0, hg * 64, hg * 64 + 64), CKVN[:, kt:kt + n], True, True, [wkk, "CKVN*"], [bk])
                        CP("act" if hh % 2 else "dve", KA[0:64, hh, kt:kt + n], b_[0:64, 0:n], [bk], [f"KA.{hh}n{kt}"])
                        CP("dve" if hh % 2 else "act", KA[64:96, hh, kt:kt + n], KPE[64:96, kt:kt + n], ["KPE*"], [f"KA.{hh}r{kt}"])
                MS("dve", VA[:, :, :].rearrange("p b (h c) -> p b h c", c=128)[:, :, :, 64:128], 1.0, ["VA*"])
                for kb in range(nkb):
                    b_, bk = sbank()
                    MM(b_[:, 0:256], CKVN[:, kb * 128:(kb + 1) * 128], Wv_(0, hf * 256, hf * 256 + 256), True, True, [wvk, "CKVN*"], [bk])
                    CP("act" if kb % 2 else "dve", VA[:, kb, :].rearrange("p (h c) -> p h c", c=128)[:, :, 0:64],
                       b_[:, 0:256].rearrange("p (h c) -> p h c", c=64), [bk], [f"VA.{kb}"])
                for t in range(2):
                    for hh in range(4):
                        hg = hf * 4 + hh
                        b_, bk = sbank()
                        for k in range(2):
                            MM(b_[0:96, :], Wq(k, hg * 96, hg * 96 + 96), CQ[:, k, t * 512:(t + 1) * 512], k == 0, k == 1, [wqk, "CQ*"], [bk])
                        CP("act", QA[0:64, hh, :], b_[0:64, :], [bk], [f"QA.{hh}n"])
                        if rope:
                            bs_, bsk = sbank()
                            for k in range(2):
                                MM(bs_[0:96, :], WSW[:, k * 768 + hg * 96:k * 768 + hg * 96 + 96], CQ[:, k, t * 512:(t + 1) * 512],
                                   k == 0, k == 1, ["WSW", "CQ*"], [bsk])
                            rope_comb(None, None, b_, bk, bs_, bsk, ropea, (t * 512, (t + 1) * 512), 64, 96)
                            TT(QA[64:96, hh, :], R1[64:96, :], R2[64:96, :], ALU.add, ["R1", "R2"], [f"QA.{hh}r"])
                        else:
                            CP("dve", QA[64:96, hh, :], b_[64:96, :], [bk], [f"QA.{hh}r"])
                    if g == "S":
                        qsegs = [(0, 512, list(range(10)))]
                    else:
                        qsegs = [(0, 256, [4 * t, 4 * t + 1]), (256, 512, [4 * t + 2, 4 * t + 3])]
                    attend(4, lambda h, q0, q1: (QA[0:96, h, q0:q1], "QA*"),
                           lambda h, kb: (KA[0:96, h, kb * 128:(kb + 1) * 128], "KA*"),
                           lambda h, kb: (VA[:, kb, h * 128:(h + 1) * 128], "VA*"),
                           64, 96 ** -0.5, qsegs,
                           lambda h, q0, q1, num, numk, den, denk, t=t, hf=hf: ep_std(h + 0, t * 512 + q0, t * 512 + q1, num, numk, den, denk,
                                                                                       chunk0=hf * 2))
            if kmix <= 3:
                return
            NKd = NK if g == "S" else 1024
            QD = ATT[:, 0:2048].rearrange("p (h n) -> p h n", n=512)
            KD = ATT[:, 2048:2048 + 4 * NK].rearrange("p (h n) -> p h n", n=NK)
            VD = ATT[:, 7168:7168 + 10 * 512].rearrange("p (b c) -> p b c", c=512)
            WQd, wqdk = wload(wi[:, :, 416:928], 8, 512)
            WKd, wkdk = wload(wi[:, :, 928:1440], 8, 512)
            WVd, wvdk = wload(wi[:, :, 1440:1952], 8, 512)

            def proj_rope(Wx, wxk, dst_fn, dkey):
                if rope:
                    swap64(Wx, wxk)
                for t in range(2):
                    for hh in range(4):
                        b_, bk = sbank()
                        for k in range(8):
                            MM(b_[:, :], Wx(k, hh * 128, hh * 128 + 128), U[:, k, t * 512:(t + 1) * 512], k == 0, k == 7, [wxk, f"U{t}"], [bk])
                        if rope:
                            bs_, bsk = sbank()
                            for k in range(8):
                                MM(bs_[:, :], WSW[:, k * 512 + hh * 128:k * 512 + hh * 128 + 128], U[:, k, t * 512:(t + 1) * 512],
                                   k == 0, k == 7, ["WSW", f"U{t}"], [bsk])
                            rope_comb(None, None, b_, bk, bs_, bsk, ropeh, (t * 512, (t + 1) * 512), 0, 128)
                            TT(dst_fn(hh, t), R1[:, :], R2[:, :], ALU.add, ["R1", "R2"], [dkey])
                        else:
                            CP("act", dst_fn(hh, t), b_[:, :], [bk], [dkey])

            proj_rope(WKd, wkdk, lambda hh, t: KD[:, hh, t * 512:(t + 1) * 512], "KA*")
            for tb in range(8):
                b_, bk = sbank()
                for k in range(8):
                    MM(b_[:, :], U[:, k, tb * 128:(tb + 1) * 128], WVd(k, 0, 512), k == 0, k == 7, [wvdk, f"U{tb // 4}"], [bk])
                CP("act", VD[:, tb, :], b_[:, :], [bk], ["VA*"])
            if g == "S":
                load_ctx_T(lambda c, blk, cn: KD[:, c, 1024 + blk * 128:1024 + (blk + 1) * 128], c_dk, 512, "KA*")
                for blk in range(2):
                    DMA("pool", VD[:, 8 + blk, :], c_dv[blk * 128:(blk + 1) * 128, :], (), [f"VA.{8 + blk}"], f"cv{blk}")
            for t in range(2):
                def qdst(hh, tt):
                    return QD[:, hh, :]
                if rope and t == 0:
                    swap64(WQd, wqdk)
                for hh in range(4):
                    b_, bk = sbank()
                    for k in range(8):
                        MM(b_[:, :], WQd(k, hh * 128, hh * 128 + 128), U[:, k, t * 512:(t + 1) * 512], k == 0, k == 7, [wqdk, f"U{t}"], [bk])
                    if rope:
                        bs_, bsk = sbank()
                        for k in range(8):
                            MM(bs_[:, :], WSW[:, k * 512 + hh * 128:k * 512 + hh * 128 + 128], U[:, k, t * 512:(t + 1) * 512],
                               k == 0, k == 7, ["WSW", f"U{t}"], [bsk])
                        rope_comb(None, None, b_, bk, bs_, bsk, ropeh, (t * 512, (t + 1) * 512), 0, 128)
                        TT(QD[:, hh, :], R1[:, :], R2[:, :], ALU.add, ["R1", "R2"], ["QA*"])
                    else:
                        CP("act", QD[:, hh, :], b_[:, :], [bk], ["QA*"])
                if g == "S":
                    qsegs = [(0, 512, list(range(10)))]
                else:
                    qsegs = [(0, 256, [4 * t, 4 * t + 1]), (256, 512, [4 * t + 2, 4 * t + 3])]
                hold = {}

                def ep_diff(h8, q0, q1, num, numk, den, denk, t=t):
                    h, c = h8 // 2, h8 % 2
                    n = q1 - q0
                    ACT(RD[:, c, 0:n], den[:, 0:n], AF.Ln, [denk], [f"RD{c}"])
                    ACT(RD[:, c, 0:n], RD[:, c, 0:n], AF.Exp, [f"RD{c}"], [f"RD{c}"], scale=-1.0)
                    if c == 0:
                        TT(R1[:, q0:q1], num[:, 0:n], RD[:, 0, 0:n], ALU.mult, [numk, "RD0"], ["R1"])
                    else:
                        TT(R2[:, q0:q1], num[:, 0:n], RD[:, 1, 0:n], ALU.mult, [numk, "RD1"], ["R2"])
                        STT(R1[:, q0:q1], R2[:, q0:q1], lam[:, 3:4], R1[:, q0:q1], ALU.mult, ALU.add, ["R1", "R2", "lam"], ["R1"])
                        ob, obk = (R3, "R3") if (h + q0 // 256) % 2 == 0 else (R4, "R4")
                        CP("dve", ob[:, 0:n], R1[:, q0:q1], ["R1"], [obk])
                        ACT(SQ[:, (h + q0 // 256) % 2, 0:n], R1[:, q0:q1], AF.Square, ["R1"], [f"SQ.d{(h + q0 // 256) % 2}"])

                        def tail(h=h, q0=q0, q1=q1, n=n, ob=ob, obk=obk, t=t):
                            sl_ = (h + q0 // 256) % 2
                            s2, s2k = sbank()
                            MM(s2[:, 0:n], ones[:], SQ[:, sl_, 0:n], True, True, ["ones", f"SQ.d{sl_}"], [s2k])
                            rstd_from(s2, s2k, 1.0 / 128, RS, "RS", n=n)
                            STT(OT[:, 4 + h, t * 512 + q0:t * 512 + q1], ob[:, 0:n], SMALL[:, 0:1], RS[:, 0:n], ALU.mult, ALU.mult,
                                [obk, "RS", "SMALL"], [f"OT.{4 + h}.{t}.{q0}"])
                        deferred.append((gcount[0] + 2, tail))

                attend(8, lambda h8, q0, q1: (QD[64 * (h8 % 2):64 * (h8 % 2) + 64, h8 // 2, q0:q1], "QA*"),
                       lambda h8, kb: (KD[64 * (h8 % 2):64 * (h8 % 2) + 64, h8 // 2, kb * 128:(kb + 1) * 128], "KA*"),
                       lambda h8, kb: (VD[:, kb, (h8 // 2) * 128:(h8 // 2) * 128 + 128], "VA*"),
                       128, 64 ** -0.5, qsegs, ep_diff, fused=False)

        def state_out(l, colsets, wi):
            for ci, (c0, c1) in enumerate(colsets):
                pieces = [(a, min(a + 512, c1)) for a in range(c0, c1, 512)]
                for (a, b) in pieces:
                    w_ = b - a
                    W, wk = wload(wi[:, :, a:b], 8, w_)
                    off = sum(x1 - x0 for x0, x1 in colsets[:ci]) + (a - c0)
                    for tb in range(8):
                        b_, bk = sbank()
                        for k in range(8):
                            MM(b_[:, 0:w_], U[:, k, tb * 128:(tb + 1) * 128], W(k, 0, w_), k == 0, k == 7, [wk, f"U{tb // 4}"], [bk])
                        sl = tb % 2
                        sk_ = f"STG{sl}"
                        if (l == 0 and a == 256) or (l == 1 and a == 512):
                            nh_, hw_, wt = (1, 128, kvwb) if l == 0 else (2, 64, qk1b)
                            for hh in range(nh_):
                                A("act", lambda h, b_=b_, hh=hh, hw_=hw_: h.activation(
                                    out=PTS[:, 0:hw_], in_=b_[:, hh * hw_:(hh + 1) * hw_], func=AF.Square, accum_out=SMALL[:, 1:2]),
                                  [bk], ["PTS", "SMALL"])
                                ACT(SMALL[:, 2:3], SMALL[:, 1:2], AF.Ln, ["SMALL", "epsD"], ["SMALL"], scale=1.0 / hw_, bias=epsD[:, :])
                                ACT(SMALL[:, 2:3], SMALL[:, 2:3], AF.Exp, ["SMALL"], ["SMALL"], scale=-0.5)
                                STT(STG[:, sl, hh * hw_:(hh + 1) * hw_], b_[:, hh * hw_:(hh + 1) * hw_], SMALL[:, 2:3], wt[:, 0:hw_],
                                    ALU.mult, ALU.mult, [bk, "SMALL", "kvwb", "qk1b"], [sk_])
                            CP("dve", STG[:, sl, 128:w_], b_[:, 128:w_], [bk], [sk_])
                        else:
                            CP("act" if tb % 2 else "dve", STG[:, sl, 0:w_], b_[:, 0:w_], [bk], [sk_])
                        DMA("sp", st_out[l][tb * 128:(tb + 1) * 128, off:off + w_], STG[:, sl, 0:w_], [sk_], (), f"so{sl}")
                    if pending_side:
                        ph_ = P.phase
                        P.phase = ph_.split("/")[0] + "/mod"
                        pending_side.pop(0)()
                        P.phase = ph_

        def mixer_odd(g, col):
            l = 1
            rope = g == "S"
            NK = 1280
            nkeys = NK if g == "S" else 1024
            nkb = 10 if g == "S" else 8
            wi = w_in[1].rearrange("(kc p) n -> p kc n", p=128)
            qw = lambda i: vecs[:, VO["qkw"] + i:VO["qkw"] + i + 1]
            qws = lambda i: vecs[:, VO["qkws"] + i:VO["qkws"] + i + 1]
            if g == "P":
                state_out(1, [(512, 768), (1280, 1792), (1792, 2304)], wi)
            QC = ATT[:, 0:2048].rearrange("p (h n) -> p h n", n=512)
            KC = ATT[:, 2048:2048 + 2 * NK].rearrange("p (h n) -> p h n", n=NK)
            VC = ATT[:, 4608:4608 + 10 * 256].rearrange("p (b c) -> p b c", c=256)
            WQ, wqk = wload(wi[:, :, 0:512], 8, 512)
            WKV, wkvk = wload(wi[:, :, 512:768], 8, 256)
            MS("dve", VC[:, :, :].rearrange("p b (h c) -> p b h c", c=128)[:, :, :, 64:128], 1.0, ["VA*"])
            for tb in range(8):
                b_, bk = sbank()
                for k in range(8):
                    MM(b_[:, 0:128], U[:, k, tb * 128:(tb + 1) * 128], WKV(k, 128, 256), k == 0, k == 7, [wkvk, f"U{tb // 4}"], [bk])
                CP("act" if tb % 2 else "dve", VC[:, tb, :].rearrange("p (h c) -> p h c", c=128)[:, :, 0:64],
                   b_[:, 0:128].rearrange("p (h c) -> p h c", c=64), [bk], [f"VA.{tb}"])
            WD = WSW[:, 0:2048].rearrange("p (k c) -> p k c", c=256)
            WDS = WSW[:, 2048:4096].rearrange("p (k c) -> p k c", c=256)
            for k in range(8):
                for kv in range(2):
                    for dup in range(2):
                        c = kv * 128 + dup * 64
                        CP("dve", WD[:, k, c:c + 64], WKV(k, kv * 64, kv * 64 + 64), [wkvk], ["WSW"])
                        if rope:
                            CP("dve", WDS[:, k, c:c + 32], WKV(k, kv * 64 + 32, kv * 64 + 64), [wkvk], ["WSW"])
                            CP("dve", WDS[:, k, c + 32:c + 64], WKV(k, kv * 64, kv * 64 + 32), [wkvk], ["WSW"])

            def qk_norm_rope(dst, dkey, b_, bk, bs_, bsk, wi_, t):
                head_rms(b_, bk, bd[:], 1.0 / 64)
                if rope:
                    rope_comb(None, None, b_, bk, bs_, bsk, ropeh, (t * 512, (t + 1) * 512), 0, 128, w=qw(wi_), ws=qws(wi_))
                    TT(R1[:, :], R1[:, :], R2[:, :], ALU.add, ["R1", "R2"], ["R1"])
                    TT(dst, R1[:, :], RS[:, :], ALU.mult, ["R1", "RS"], [dkey])
                else:
                    STT(dst, b_[:, :], qw(wi_), RS[:, :], ALU.mult, ALU.mult, [bk, "RS", "vecs"], [dkey])

            for t in range(2):
                for kv in range(2):
                    b_, bk = sbank()
                    for k in range(8):
                        MM(b_[:, :], WD[:, k, kv * 128:(kv + 1) * 128], U[:, k, t * 512:(t + 1) * 512], k == 0, k == 7, ["WSW", f"U{t}"], [bk])
                    bs_, bsk = None, None
                    if rope:
                        bs_, bsk = sbank()
                        for k in range(8):
                            MM(bs_[:, :], WDS[:, k, kv * 128:(kv + 1) * 128], U[:, k, t * 512:(t + 1) * 512], k == 0, k == 7, ["WSW", f"U{t}"], [bsk])
                    qk_norm_rope(KC[:, kv, t * 512:(t + 1) * 512], "KA*", b_, bk, bs_, bsk, 1, t)
            if g == "S":
                for blk in range(2):
                    DMA("sp", STG[:, blk, 0:128], c_gk[blk * 128:(blk + 1) * 128, :], (), ["STG0", "STG1"], "cx")
                for blk in range(2):
                    for kv in range(2):
                        b_, bk = sbank()
                        A("pe", lambda h, b_=b_, blk=blk, kv=kv: h.transpose(b_[0:64, 0:128], STG[:, blk, kv * 64:(kv + 1) * 64], ident[:]),
                          ["STG0", "STG1", "ident"], [bk])
                        CP("dve", KC[0:64, kv, 1024 + blk * 128:1024 + (blk + 1) * 128], b_[0:64, 0:128], [bk], ["KA*"])
                        CP("act", KC[64:128, kv, 1024 + blk * 128:1024 + (blk + 1) * 128], b_[0:64, 0:128], [bk], ["KA*"])
                for blk in range(2):
                    DMA("sp", STG[:, blk, 0:128], c_gv[blk * 128:(blk + 1) * 128, :], ["KA*"], ["STG0", "STG1"], "cx")
                for blk in range(2):
                    CP("dve", VC[:, 8 + blk, :].rearrange("p (h c) -> p h c", c=128)[:, :, 0:64],
                       STG[:, blk, 0:128].rearrange("p (h c) -> p h c", c=64), ["STG0", "STG1"], [f"VA.{8 + blk}"])
            if rope:
                swap64(WQ, wqk)
            CMB = {}
            for hg in range(8):
                if hg < 4:
                    CMB[hg] = (CM2[:, hg], "Y*")
                elif hg < 6:
                    CMB[hg] = (SQ[:].rearrange("p a b -> p (a b)")[:, 0:3328].rearrange("p (h x c) -> p h x c", x=26, c=64)[:, hg - 4], "SQ*")
                else:
                    CMB[hg] = (WSW[:, 0:3328].rearrange("p (h x c) -> p h x c", x=26, c=64)[:, hg - 6], "WSW")

            def cm_zero(which):
                def f():
                    if which == 0:
                        MS("dve", Y[:], 0.0, ["Y*"])
                        MS("dve", SQ[:].rearrange("p a b -> p (a b)")[:, 0:3328], 0.0, ["SQ*"])
                    else:
                        MS("dve", WSW[:, 0:3328], 0.0, ["WSW"])
                return f

            def cm_gen(hg):
                def f():
                    dstv, dk = CMB[hg]
                    TAh = STG[:].rearrange("p a b -> p (a b)")[:, 0:960]
                    for i in range(2):
                        src = bass.AP(tpad.tensor, hg * 15 * 160 + 16, [[1, 64], [160, 15], [1, 64]])
                        DMA("sp", TAh[64 * i:64 * i + 64, :].rearrange("p (a b) -> p a b", b=64), src, (), ["STG0", "STG1"], "tp")
                    for i in range(2):
                        srcs = sb_ap(STG, 64 * i, 64, 14 * 64 + 63, [[-64, 15], [-1, 64]])
                        dst = dstv[64 * i:64 * i + 64, i + 4:i + 19, :]
                        ACT(dst, srcs, AF.Exp, ["STG0", "STG1"], [dk])
                        ck = sb_ap(colok, 64 * i, 64, 0, [[0, 15], [1, 64]])
                        TT(dst, dst, ck, ALU.mult, ["colok", dk], [dk])
                return f
            side_t = {0: [cm_zero(0)] + [cm_gen(hg) for hg in range(0, 5)], 1: [cm_zero(1)] + [cm_gen(hg) for hg in range(5, 8)]} if g == "S" else {0: None, 1: None}
            for t in range(2):
                for hh in range(4):
                    b_, bk = sbank()
                    for k in range(8):
                        MM(b_[:, :], WQ(k, hh * 128, hh * 128 + 128), U[:, k, t * 512:(t + 1) * 512], k == 0, k == 7, [wqk, f"U{t}"], [bk])
                    bs_, bsk = None, None
                    if rope:
                        bs_, bsk = sbank()
                        for k in range(8):
                            MM(bs_[:, :], WSW[:, k * 512 + hh * 128:k * 512 + hh * 128 + 128], U[:, k, t * 512:(t + 1) * 512], k == 0, k == 7,
                               ["WSW", f"U{t}"], [bsk])
                    qk_norm_rope(QC[:, hh, :], "QA*", b_, bk, bs_, bsk, 0, t)
                if g == "S":
                    qsegs = [(0, 512, list(range(10)))]
                else:
                    qsegs = [(0, 256, [4 * t, 4 * t + 1]), (256, 512, [4 * t + 2, 4 * t + 3])]
                attend(8, lambda h, q0, q1: (QC[64 * (h % 2):64 * (h % 2) + 64, h // 2, q0:q1], "QA*"),
                       lambda h, kb: (KC[64 * (h % 2):64 * (h % 2) + 64, h // 4, kb * 128:(kb + 1) * 128], "KA*"),
                       lambda h, kb: (VC[:, kb, (h // 4) * 128:(h // 4) * 128 + 128], "VA*"),
                       64, 64 ** -0.5, qsegs,
                       lambda h, q0, q1, num, numk, den, denk, t=t: ep_std(h, t * 512 + q0, t * 512 + q1, num, numk, den, denk, chunk0=0),
                       side=side_t[t], side_every=8 if t == 0 else 12)
                while side_t[t]:
                    side_t[t].pop(0)()

            KN = ATT[:, 0:4 * NK].rearrange("p (h n) -> p h n", n=NK)
            VN = ATT[:, 5120:5120 + 10 * 512].rearrange("p (b c) -> p b c", c=512)
            QNt = [ATT[:, 10240 + tt * 2048:10240 + (tt + 1) * 2048].rearrange("p (h n) -> p h n", n=512) for tt in range(2)]
            WQn, wqnk = None, None
            for hf in range(2):
                WKn, wknk = wload(wi[:, :, 1280 + hf * 256:1280 + (hf + 1) * 256], 8, 256)
                WVn, wvnk = wload(wi[:, :, 1792 + hf * 256:1792 + (hf + 1) * 256], 8, 256)
                WQn, wqnk = wload(wi[:, :, 768 + hf * 256:768 + (hf + 1) * 256], 8, 256)
                for t in range(2):
                    for hh in range(4):
                        b_, bk = sbank()
                        for k in range(8):
                            MM(b_[0:64, :], WKn(k, hh * 64, hh * 64 + 64), U[:, k, t * 512:(t + 1) * 512], k == 0, k == 7, [wknk, f"U{t}"], [bk])
                        CP("act", KN[0:64, hh, t * 512:(t + 1) * 512], b_[0:64, :], [bk], ["KA*"])
                MS("dve", VN[:, :, :].rearrange("p b (h c) -> p b h c", c=128)[:, :, :, 64:128], 1.0, ["VA*", "KA*", "QA*"])
                for tb in range(8):
                    b_, bk = sbank()
                    for k in range(8):
                        MM(b_[:, 0:256], U[:, k, tb * 128:(tb + 1) * 128], WVn(k, 0, 256), k == 0, k == 7, [wvnk, f"U{tb // 4}"], [bk])
                    CP("act" if tb % 2 else "dve", VN[:, tb, :].rearrange("p (h c) -> p h c", c=128)[:, :, 0:64],
                       b_[:, 0:256].rearrange("p (h c) -> p h c", c=64), [bk], [f"VA.{tb}"])
                if g == "S":
                    for blk in range(2):
                        DMA("sp", STG[:, blk, 0:256], c_nk[blk * 128:(blk + 1) * 128, hf * 256:(hf + 1) * 256], (), ["STG0", "STG1"], "cx")
                    for blk in range(2):
                        for hh in range(4):
                            b_, bk = sbank()
                            A("pe", lambda h, b_=b_, blk=blk, hh=hh: h.transpose(b_[0:64, 0:128], STG[:, blk, hh * 64:(hh + 1) * 64], ident[:]),
                              ["STG0", "STG1", "ident"], [bk])
                            CP("dve", KN[0:64, hh, 1024 + blk * 128:1024 + (blk + 1) * 128], b_[0:64, 0:128], [bk], ["KA*"])
                    for blk in range(2):
                        DMA("pool", VN[:, 8 + blk, :].rearrange("p (h c) -> p h c", c=128)[:, :, 0:64],
                            c_nv[blk * 128:(blk + 1) * 128, hf * 256:(hf + 1) * 256].rearrange("p (h c) -> p h c", c=64), (), [f"VA.{8 + blk}"], f"cn{blk}")
                    CP("dve", KN[64:80, :, 0:1024], sb_ap(KPE, 0, 16, 0, [[0, 4], [1, 1024]]), ["AUG"], ["KA*", "VA*", "QA*"])
                    MS("dve", KN[64:80, :, 1024:1280], 0.0, ["KA*", "VA*", "QA*"])
                if g == "S":
                    for tt in range(2):
                        CP("act", QNt[tt][64:80, :, :], sb_ap(KPE, 32, 16, tt * 512, [[0, 4], [1, 512]]), ["AUG"], [f"QA.a{tt}", "KA*", "VA*"])
                for t in range(2):
                    QN = QNt[t]
                    for hh in range(4):
                        b_, bk = sbank()
                        for k in range(8):
                            MM(b_[0:64, :], WQn(k, hh * 64, hh * 64 + 64), U[:, k, t * 512:(t + 1) * 512], k == 0, k == 7, [wqnk, f"U{t}"], [bk])
                        ACT(QN[0:64, hh, :], b_[0:64, :], AF.Copy, [bk], [f"QA.q{t}{hh}"], scale=0.125)
                    if g == "S":
                        own = list(range(0, 6)) if t == 0 else list(range(2, 8))
                        qsegs = [(0, 512, own + [8, 9])]
                        kdim = 80

                        def cmf(h, q0, kb, t=t, hf=hf):
                            if kb >= 8:
                                return None
                            x0 = 7 + 8 * t - 2 * kb + 4
                            return CMB[hf * 4 + h][0][:, x0:x0 + 8, :].rearrange("p a b -> p (a b)")
                    else:
                        qsegs = [(0, 256, [4 * t, 4 * t + 1]), (256, 512, [4 * t + 2, 4 * t + 3])]
                        kdim = 64
                        cmf = None
                    attend(4, lambda h, q0, q1, QN=QN: (QN[0:kdim, h, q0:q1], "QA*"),
                           lambda h, kb: (KN[0:kdim, h, kb * 128:(kb + 1) * 128], "KA*"),
                           lambda h, kb: (VN[:, kb, h * 128:(h + 1) * 128], "VA*"),
                           64, 1.0, qsegs,
                           lambda h, q0, q1, num, numk, den, denk, t=t, hf=hf: ep_std(h, t * 512 + q0, t * 512 + q1, num, numk, den, denk,
                                                                                       chunk0=4 + hf * 2),
                           cmf=cmf)

        import os
        dbg = os.environ.get("KDBG", "")
        stop = False
        for g, col in (("P", 0), ("S", 1)):
            if stop:
                break
            P.phase = f"{g}:load"
            load_x(g)
            if dbg == f"{g},0,load":
                store_y(g)
                break
            for l in range(2):
                P.phase = f"{g}{l}:norm1"
                if g == "P":
                    mod_part(l, 0)
                    mod_part(l, 1)
                for t in range(2):
                    norm_mod(l, col, 0, 1, t)
                if g == "P":
                    pending_side.extend([(lambda l=l, i=i: mod_part(l, i)) for i in (2, 3, 4, 5)])
                if dbg == f"{g},{l},norm":
                    stop = True
                    break
                P.phase = f"{g}{l}:mixer"
                if l == 0:
                    mixer_even(g, col)
                else:
                    mixer_odd(g, col)
                P.phase = f"{g}{l}:modrest"
                while pending_side:
                    pending_side.pop(0)()
                P.phase = f"{g}{l}:wout"
                if dbg == f"{g},{l},mixonly":
                    stop = True
                    break
                for t in range(2):
                    wout_phase(l, col, t)
                if dbg == f"{g},{l},mix":
                    stop = True
                    break
                P.phase = f"{g}{l}:norm2"
                for t in range(2):
                    norm_mod(l, col, 3, 4, t)
                P.phase = f"{g}{l}:ffn"
                if g == "P":
                    ffn_phase(l, g, col)
                else:
                    ffn_phase_dve(l, g, col)
                if dbg == f"{g},{l},ffn":
                    stop = True
                    break
            P.phase = f"{g}:store"
            store_y(g)

        P.final_waits("sp")
        _CACHE["labels"] = P.labels
        with nc.Block() as block:
            P.emit(block)
    return nc


_CACHE = {}


def _rope_tables():
    def table(n, rot):
        t = np.arange(n)
        nf = rot // 4
        inv = 1.0 / (10000.0 ** (np.arange(nf) / nf))
        ang = np.concatenate([(t // 64)[:, None] * inv[None, :], (t % 64)[:, None] * inv[None, :]], axis=-1)
        return np.cos(ang).astype(np.float32), np.sin(ang).astype(np.float32)
    cos_a, sin_a = table(1024, 32)
    cos_h, sin_h = table(1024, 64)
    rh = np.zeros((128, 2, 1024), np.float32)
    for p in range(128):
        d = p % 64
        j = d % 32
        rh[p, 0] = cos_h[:, j]
        rh[p, 1] = (-1.0 if d < 32 else 1.0) * sin_h[:, j]
    ra = np.zeros((128, 2, 1024), np.float32)
    for p in range(64, 96):
        d = p - 64
        j = d % 16
        ra[p, 0] = cos_a[:, j]
        ra[p, 1] = (-1.0 if d < 16 else 1.0) * sin_a[:, j]
    return rh, ra


def _na_consts():
    rows = 16
    r = np.arange(rows)
    rs = np.clip(r - 4, 0, rows - 8)
    rowok = (r[None, :] >= rs[:, None]) & (r[None, :] < rs[:, None] + 8)
    col = np.arange(64)
    cs = np.clip(col - 8, 0, 48)
    col_ok = (col[None, :] >= cs[:, None]) & (col[None, :] < cs[:, None] + 16)
    aug = np.zeros((32, 1024), np.float32)
    tq = np.arange(1024) // 64
    for m in range(16):
        aug[m] = np.where(rowok[tq, m], 0.0, -BIG)
        aug[16 + m] = (tq == m).astype(np.float32)
    ck = np.zeros((128, 64), np.float32)
    for i in range(2):
        ck[64 * i:64 * i + 64] = col_ok.T.astype(np.float32)
    return aug, ck


def make_in_maps(inp):
    f = lambda k: np.ascontiguousarray(np.asarray(inp[k], dtype=np.float32))
    x_prompt, x_sample, c = f("x_prompt"), f("x_sample"), f("c")
    fm = lambda v: np.ascontiguousarray(v.reshape(-1, 128).T)
    vec_list = []
    nwv = f("norm_w")
    vec_list.append(np.concatenate([fm(nwv[l, i]) for l in range(2) for i in range(4)], axis=1))
    bm = f("b_mod")
    vec_list.append(np.concatenate([fm(bm[l]) for l in range(2)], axis=1))
    cwv = f("conv_w")
    vec_list.append(np.concatenate([fm(cwv[l, j]) for l in range(2) for j in range(3)], axis=1))
    cbv = f("conv_b")
    vec_list.append(np.concatenate([fm(cbv[l]) for l in range(2)], axis=1))
    vec_list.append(fm(f("q_norm_w")[0]))
    vec_list.append(fm(f("kv_norm_w")[0]))
    vec_list.append(fm(f("diff_subln_w")[0]))
    qk = f("qk_norm_w")[0]
    vec_list.append(np.stack([np.tile(qk[0], 2), np.tile(qk[1], 2)], axis=1))
    sw = lambda v: np.concatenate([v[32:], v[:32]])
    vec_list.append(np.stack([np.tile(sw(qk[0]), 2), np.tile(sw(qk[1]), 2)], axis=1))
    vecs = np.concatenate(vec_list, axis=1).astype(np.float32)
    vecs = np.ascontiguousarray(np.pad(vecs, ((0, 0), (0, 528 - vecs.shape[1]))))
    rh, ra = _rope_tables()
    aug, colok = _na_consts()
    shared = {
        "ident": np.eye(128, dtype=np.float32), "vecs": vecs, "ropeh": rh, "ropea": ra, "aug": aug, "colok": colok,
        "lamb": np.ascontiguousarray(np.broadcast_to(f("diff_lam")[0].reshape(1, 256), (128, 256))),
        "kvwb": np.ascontiguousarray(np.broadcast_to(f("kv_norm_w")[0].reshape(1, 128), (128, 128))),
        "qk1b": np.ascontiguousarray(np.broadcast_to(qk[1].reshape(1, 64), (128, 64))),
        "w_mod": f("w_mod"), "w_in_even": f("w_in_even")[0], "w_in_odd": f("w_in_odd")[0],
        "w_out_even": f("w_out_even")[0], "w_out_odd": f("w_out_odd")[0], "w_uq": f("w_uq")[0], "w_uk": f("w_uk")[0],
        "w_uv": f("w_uv")[0], "w_up": f("w_up"), "w_down": f("w_down"),
        "rpbp": np.ascontiguousarray(np.pad(f("na_rpb")[0].reshape(120, 31), ((0, 0), (64, 65)))),
    }
    c_ctx = f("c_ctx")
    caches = {"c_ckv": ("cache_mla_ckv", 128), "c_kpe": ("cache_mla_kpe", 32), "c_dk": ("cache_diff_k", 512),
              "c_dv": ("cache_diff_v", 512), "c_gk": ("cache_gqa_k", 128), "c_gv": ("cache_gqa_v", 128),
              "c_nk": ("cache_na_k", 512), "c_nv": ("cache_na_v", 512)}
    in_maps = []
    for core in range(NCORES):
        b = core % 4
        m = dict(shared)
        m["xp"] = np.ascontiguousarray(x_prompt[core * 4:(core + 1) * 4].reshape(1024, 1024))
        m["xs"] = np.ascontiguousarray(x_sample[b])
        cv = np.stack([c_ctx, c[b]], axis=0)
        m["cvT"] = np.ascontiguousarray(cv.reshape(2, 8, 128).transpose(2, 1, 0).reshape(128, 16))
        for k, (src, n) in caches.items():
            m[k] = np.ascontiguousarray(f(src)[b, 0].reshape(256, n))
        in_maps.append(m)
    return in_maps


def kernel(**inp):
    if "nc" not in _CACHE:
        _CACHE["nc"] = build_program()
    nc = _CACHE["nc"]
    in_maps = make_in_maps(inp)
    res = run_bass_kernel_spmd(nc, in_maps, core_ids=list(range(NCORES)))
    R = res.results
    y_prompt = np.concatenate([R[i]["yp"].reshape(4, 256, 1024) for i in range(NCORES)], axis=0)
    y_sample = np.stack([R[i]["ys"] for i in range(4)], axis=0)
    st0 = np.concatenate([R[i]["st0"].reshape(4, 256, 1184) for i in range(NCORES)], axis=0)
    st1 = np.concatenate([R[i]["st1"].reshape(4, 256, 1280) for i in range(NCORES)], axis=0)
    outs = (
        y_prompt, y_sample,
        st0[:, :, 0:128].reshape(32, 1, 256, 128), st0[:, :, 128:160].reshape(32, 1, 256, 32),
        st0[:, :, 160:672].reshape(32, 1, 256, 4, 128), st0[:, :, 672:1184].reshape(32, 1, 256, 4, 128),
        st1[:, :, 0:128].reshape(32, 1, 256, 2, 64), st1[:, :, 128:256].reshape(32, 1, 256, 2, 64),
        st1[:, :, 256:768].reshape(32, 1, 256, 8, 64), st1[:, :, 768:1280].reshape(32, 1, 256, 8, 64),
    )
    return tuple(np.ascontiguousarray(o, dtype=np.float32) for o in outs)
```
